# Optimizing a Trainium2 kernel written in Bass

```python
import jax, jax.numpy as jnp
from jax import lax
import numpy as np

D_MODEL = 1024
BATCH = 4
SEQ = 4096
DEPTH = 4

N_MIXERS = 3
EXPAND = 2
D_INNER = EXPAND * D_MODEL
POOL_WINDOWS = (2, 4, 8, 16)
N_POOL_GROUPS = len(POOL_WINDOWS)
POOL_GROUP = D_INNER // N_POOL_GROUPS
CONV_WIDTH = 3
N_HEADS = 16
QK_NOPE_DIM = 128
QK_ROPE_DIM = 64
V_HEAD_DIM = D_INNER // N_HEADS
Q_LORA_RANK = 384
KV_LORA_RANK = 256
MLA_IN_DIM = Q_LORA_RANK + KV_LORA_RANK + QK_ROPE_DIM + D_INNER
ATTN_SCALE = (QK_NOPE_DIM + QK_ROPE_DIM) ** -0.5
ROPE_BASE = 10000.0
Q_BLOCK = 128
NORM_EPS = 1e-6
MAX_POS_OFFSET = 1024
N_POOL = (DEPTH + 2) // 3
N_CONV = (DEPTH + 1) // 3
N_MLA = DEPTH // 3

kernel_name = "hybrid_pool_conv_mla_gated_trunk"


def rms_norm(x, g):
    xf = x.astype(jnp.float32)
    y = xf * lax.rsqrt(jnp.mean(xf * xf, axis=-1, keepdims=True) + NORM_EPS)
    return (y * g.astype(jnp.float32)).astype(x.dtype)


def pool_mixer(xn, w_in, w_grp, scale, w_out):
    B, S, _ = xn.shape
    u, z = jnp.split(xn @ w_in, 2, axis=-1)
    uf = u.astype(jnp.float32).reshape(B, S, N_POOL_GROUPS, POOL_GROUP)
    cs = jnp.cumsum(uf, axis=1)
    count_base = jnp.arange(1, S + 1, dtype=jnp.float32)
    pooled = []
    for g, w in enumerate(POOL_WINDOWS):
        c = cs[:, :, g]
        prev = jnp.pad(c, ((0, 0), (w, 0), (0, 0)))[:, :S]
        mean = (c - prev) / jnp.minimum(count_base, float(w))[None, :, None]
        pooled.append(mean - uf[:, :, g])
    pooled = jnp.stack(pooled, axis=2).astype(u.dtype)
    mixed = jnp.einsum('bsgc,gcd->bsgd', pooled, w_grp).reshape(B, S, D_INNER) * scale
    return (mixed * jax.nn.silu(z)) @ w_out


def causal_depthwise_conv(x, w):
    return lax.conv_general_dilated(
        x, w[:, None, :].astype(x.dtype), window_strides=(1,),
        padding=[(CONV_WIDTH - 1, 0)], dimension_numbers=('NWC', 'WIO', 'NWC'),
        feature_group_count=x.shape[-1])


def conv_mixer(xn, w_in, conv_w, w_out):
    b, c, h, z = jnp.split(xn @ w_in, 4, axis=-1)
    y = b * causal_depthwise_conv(c * h, conv_w)
    return (y * jax.nn.silu(z)) @ w_out


def apply_rope(x, cos, sin):
    half = x.shape[-1] // 2
    x1, x2 = x[..., :half], x[..., half:]
    return jnp.concatenate([x1 * cos - x2 * sin, x2 * cos + x1 * sin], axis=-1).astype(x.dtype)


def causal_block_attention(q_nope, q_rope, k_nope, k_rope, v):
    B, S, H, _ = q_nope.shape
    nb = S // Q_BLOCK
    qn = q_nope.reshape(B, nb, Q_BLOCK, H, QK_NOPE_DIM).transpose(1, 0, 2, 3, 4)
    qr = q_rope.reshape(B, nb, Q_BLOCK, H, QK_ROPE_DIM).transpose(1, 0, 2, 3, 4)
    starts = jnp.arange(nb, dtype=jnp.int32) * Q_BLOCK
    key_idx = jnp.arange(S, dtype=jnp.int32)

    def one_block(args):
        qn_b, qr_b, start = args
        s = (jnp.einsum('bqhd,bkhd->bhqk', qn_b, k_nope).astype(jnp.float32)
             + jnp.einsum('bqhr,bkr->bhqk', qr_b, k_rope).astype(jnp.float32)) * ATTN_SCALE
        q_idx = start + jnp.arange(Q_BLOCK, dtype=jnp.int32)
        mask = key_idx[None, :] <= q_idx[:, None]
        s = jnp.where(mask[None, None], s, jnp.float32(-1e30))
        p = jax.nn.softmax(s, axis=-1).astype(v.dtype)
        return jnp.einsum('bhqk,bkhd->bqhd', p, v)

    o = lax.map(one_block, (qn, qr, starts))
    return o.transpose(1, 0, 2, 3, 4).reshape(B, S, H, V_HEAD_DIM)


def mla_mixer(xn, cos, sin, w_in, q_norm, w_q_up, kv_norm, w_kv_up, w_out):
    B, S, _ = xn.shape
    h = xn @ w_in
    q_lat, kv_lat, k_rope, z = jnp.split(
        h, [Q_LORA_RANK, Q_LORA_RANK + KV_LORA_RANK,
            Q_LORA_RANK + KV_LORA_RANK + QK_ROPE_DIM], axis=-1)
    q = (rms_norm(q_lat, q_norm) @ w_q_up).reshape(B, S, N_HEADS, QK_NOPE_DIM + QK_ROPE_DIM)
    q_nope = q[..., :QK_NOPE_DIM]
    q_rope = apply_rope(q[..., QK_NOPE_DIM:], cos[:, :, None, :], sin[:, :, None, :])
    kv = (rms_norm(kv_lat, kv_norm) @ w_kv_up).reshape(B, S, N_HEADS, QK_NOPE_DIM + V_HEAD_DIM)
    k_nope, v = kv[..., :QK_NOPE_DIM], kv[..., QK_NOPE_DIM:]
    k_rope = apply_rope(k_rope, cos, sin)
    o = causal_block_attention(q_nope, q_rope, k_nope, k_rope, v)
    return (o.reshape(B, S, D_INNER) * jax.nn.silu(z)) @ w_out


def setup_inputs(seed: int = 0) -> dict:
    key = jax.random.key(seed)
    ks = jax.random.split(key, 20)

    def normal(k, shape, scale):
        return jax.random.normal(k, shape, jnp.float32) * scale

    def gain(k, shape):
        return 1.0 + 0.02 * jax.random.normal(k, shape, jnp.float32)

    x = normal(ks[0], (BATCH, SEQ, D_MODEL), 1.0)
    offset = jax.random.randint(ks[1], (BATCH, 1), 0, MAX_POS_OFFSET, dtype=jnp.int32)
    positions = (offset + jnp.arange(SEQ, dtype=jnp.int32)[None, :]).astype(jnp.int32)
    return {
        "x": x,
        "positions": positions,
        "pool_norm": gain(ks[2], (N_POOL, D_MODEL)),
        "pool_w_in": normal(ks[3], (N_POOL, D_MODEL, 2 * D_INNER), D_MODEL ** -0.5),
        "pool_w_grp": normal(ks[4], (N_POOL, N_POOL_GROUPS, POOL_GROUP, POOL_GROUP), POOL_GROUP ** -0.5),
        "pool_scale": gain(ks[5], (N_POOL, D_INNER)),
        "pool_w_out": normal(ks[6], (N_POOL, D_INNER, D_MODEL), D_INNER ** -0.5),
        "conv_norm": gain(ks[7], (N_CONV, D_MODEL)),
        "conv_w_in": normal(ks[8], (N_CONV, D_MODEL, 4 * D_INNER), D_MODEL ** -0.5),
        "conv_w": normal(ks[9], (N_CONV, CONV_WIDTH, D_INNER), CONV_WIDTH ** -0.5),
        "conv_w_out": normal(ks[10], (N_CONV, D_INNER, D_MODEL), D_INNER ** -0.5),
        "mla_norm": gain(ks[11], (N_MLA, D_MODEL)),
        "mla_w_in": normal(ks[12], (N_MLA, D_MODEL, MLA_IN_DIM), D_MODEL ** -0.5),
        "mla_q_norm": gain(ks[13], (N_MLA, Q_LORA_RANK)),
        "mla_w_q_up": normal(ks[14], (N_MLA, Q_LORA_RANK, N_HEADS * (QK_NOPE_DIM + QK_ROPE_DIM)), Q_LORA_RANK ** -0.5),
        "mla_kv_norm": gain(ks[15], (N_MLA, KV_LORA_RANK)),
        "mla_w_kv_up": normal(ks[16], (N_MLA, KV_LORA_RANK, N_HEADS * (QK_NOPE_DIM + V_HEAD_DIM)), KV_LORA_RANK ** -0.5),
        "mla_w_out": normal(ks[17], (N_MLA, D_INNER, D_MODEL), D_INNER ** -0.5),
        "final_norm": gain(ks[18], (D_MODEL,)),
    }


def reference(x, positions, pool_norm, pool_w_in, pool_w_grp, pool_scale, pool_w_out,
              conv_norm, conv_w_in, conv_w, conv_w_out,
              mla_norm, mla_w_in, mla_q_norm, mla_w_q_up, mla_kv_norm, mla_w_kv_up, mla_w_out,
              final_norm):
    inv_freq = ROPE_BASE ** (-jnp.arange(0, QK_ROPE_DIM, 2, dtype=jnp.float32) / QK_ROPE_DIM)
    angles = positions.astype(jnp.float32)[..., None] * inv_freq
    cos, sin = jnp.cos(angles).astype(x.dtype), jnp.sin(angles).astype(x.dtype)

    for i in range(DEPTH):
        kind, j = i % N_MIXERS, i // N_MIXERS
        if kind == 0:
            xn = rms_norm(x, pool_norm[j])
            x = x + pool_mixer(xn, pool_w_in[j], pool_w_grp[j], pool_scale[j], pool_w_out[j])
        elif kind == 1:
            xn = rms_norm(x, conv_norm[j])
            x = x + conv_mixer(xn, conv_w_in[j], conv_w[j], conv_w_out[j])
        else:
            xn = rms_norm(x, mla_norm[j])
            x = x + mla_mixer(xn, cos, sin, mla_w_in[j], mla_q_norm[j], mla_w_q_up[j],
                              mla_kv_norm[j], mla_w_kv_up[j], mla_w_out[j])
    return rms_norm(x, final_norm)
```

```python
import numpy as np
from contextlib import ExitStack
import concourse.bass as bass
import concourse.mybir as mybir
from concourse.bass_utils import run_bass_kernel_spmd

F32 = mybir.dt.float32
BF16 = mybir.dt.bfloat16
I32 = mybir.dt.int32
ALU = mybir.AluOpType
AF = mybir.ActivationFunctionType

NTOK = 1056
HALO = 32
OWN = 1024
TW = 352
TILES = [(0, TW), (TW, TW), (2 * TW, TW)]
EPS = 1e-6
ATT_SCALE = float(192 ** -0.5)
NEG = -30000.0
NV = 136
C_PNORM = (0, 8)
C_PSCALE = (16, 32)
C_CNORM = 48
C_CONVW = 56
C_MNORM = 104
C_QNORM = 112
C_KVNORM = 115
C_FNORM = 117
C_INVF = 125
C_FLAGA = 126
C_FLAGB = 127
C_SGN = 128
C_EPS = 129
C_NEGPI = 130
C_ZERO = 131
C_TINY = 132
CW1 = 6.28125
CW2 = float(2 * np.pi - 6.28125)
INV2PI = float(1.0 / (2 * np.pi))


def _freeze(fn, depth=0):
    import types
    if not isinstance(fn, types.FunctionType) or fn.__closure__ is None or depth > 4:
        return fn
    cells = []
    for c in fn.__closure__:
        try:
            v = c.cell_contents
        except ValueError:
            cells.append(c)
            continue
        if isinstance(v, types.FunctionType) and v is not fn:
            v = _freeze(v, depth + 1)
        cells.append(types.CellType(v))
    g = types.FunctionType(fn.__code__, fn.__globals__, fn.__name__, fn.__defaults__, tuple(cells))
    g.__kwdefaults__ = fn.__kwdefaults__
    return g


class Sched:
    ENG = ("sync", "gpsimd", "scalar", "vector", "tensor")

    def __init__(self, nc):
        self.nc = nc
        self.ops = {e: [] for e in self.ENG}
        self.last_w = {}
        self.readers = {}
        self.dma_cnt = {}
        self.dma_inc = {}

    def add(self, eng, fn, reads=(), writes=(), dma_slot=None, ndma=1, inc=16):
        idx = len(self.ops[eng])
        fn = _freeze(fn)
        if dma_slot is not None:
            writes = list(writes) + [("slot", dma_slot)]
        deps = set()
        for k in reads:
            d = self.last_w.get(k)
            if d is not None:
                deps.add(d)
        for k in writes:
            d = self.last_w.get(k)
            if d is not None:
                deps.add(d)
            for d in self.readers.get(k, ()):
                deps.add(d)
        if dma_slot is not None:
            n = self.dma_cnt.get(dma_slot, 0) + ndma
            self.dma_cnt[dma_slot] = n
            self.dma_inc[dma_slot] = inc
            me = ("d", dma_slot, n)
        else:
            me = ("e", eng, idx)
        for k in reads:
            lst = self.readers.setdefault(k, [])
            lst[:] = [d for d in lst if not (d[0] == me[0] and d[1] == me[1])]
            lst.append(me)
        for k in writes:
            self.last_w[k] = me
            self.readers[k] = []
        deps.discard(me)
        self.ops[eng].append(dict(fn=fn, deps=deps, me=me, dma=dma_slot, signal=False))
        return me

    def alias(self, old_keys, new_keys):
        deps = set()
        for k in old_keys:
            d = self.last_w.get(k)
            if d is not None:
                deps.add(d)
            for d in self.readers.get(k, ()):
                deps.add(d)
        for k in new_keys:
            lst = self.readers.setdefault(k, [])
            for d in deps:
                if d not in lst:
                    lst.append(d)

    def emit(self, new_sem):
        nc = self.nc
        for eng in self.ENG:
            for idx, op in enumerate(self.ops[eng]):
                best = {}
                for d in op["deps"]:
                    key = (d[0], d[1])
                    if key not in best or best[key][2] < d[2]:
                        best[key] = d
                need = []
                for d in best.values():
                    if d[0] == "e" and d[1] == eng:
                        if eng == "tensor":
                            continue
                        if eng in ("vector", "scalar") and idx - d[2] >= 4:
                            continue
                    need.append(d)
                    if d[0] == "e":
                        self.ops[d[1]][d[2]]["signal"] = True
                op["need"] = need
        cnt = {}
        for eng in self.ENG:
            c = 0
            arr = []
            for op in self.ops[eng]:
                if op["signal"]:
                    c += 1
                arr.append(c)
            cnt[eng] = arr
        esem = {eng: new_sem("e_" + eng) for eng in self.ENG if self.ops[eng]}
        dsem = {slot: new_sem("d_" + str(slot)) for slot in self.dma_cnt}
        self.nsem = len(esem) + len(dsem)
        ops = self.ops

        def make_body(eng):
            def body(e):
                seen = {}
                for op in ops[eng]:
                    for d in op["need"]:
                        if d[0] == "e":
                            sem, val, k = esem[d[1]], cnt[d[1]][d[2]], ("e", d[1])
                        else:
                            sem, val, k = dsem[d[1]], self.dma_inc[d[1]] * d[2], ("d", d[1])
                        if seen.get(k, 0) >= val:
                            continue
                        seen[k] = val
                        e.wait_ge(sem, val)
                    if op["fn"] is None:
                        continue
                    ins = op["fn"](e)
                    if op["dma"] is not None:
                        iv = self.dma_inc[op["dma"]]
                        if isinstance(ins, (list, tuple)):
                            for i_ in ins:
                                i_.then_inc(dsem[op["dma"]], iv)
                        else:
                            ins.then_inc(dsem[op["dma"]], iv)
                    elif op["signal"]:
                        ins.then_inc(esem[eng], 1)
            return body

        with nc.Block() as block:
            for eng in self.ENG:
                if ops[eng]:
                    getattr(block, eng)(make_body(eng))


def build_program(parts=("p1", "cc", "p2"), debug_out=False):
    nc = bass.Bass("TRN2", target_bir_lowering=False)

    def din(name, shape, dt=F32):
        return nc.dram_tensor(name, list(shape), dt, kind="ExternalInput").ap()

    x_in = din("x_in", [2, 128, 8, NTOK])
    pos_in = din("pos_in", [2, NTOK], I32)
    vecs_d = din("vecs", [128, NV])
    masks_d = din("masks", [128, 320])
    rcnt_d = din("rcnt", [2, 128, 64])
    pool_w_in = din("pool_w_in", [2, 1024, 4096])
    pool_w_grp = din("pool_w_grp", [2, 4, 512, 512])
    pool_w_out = din("pool_w_out", [2, 2048, 1024])
    conv_w_in = din("conv_w_in", [1, 1024, 8192])
    conv_w_out = din("conv_w_out", [1, 2048, 1024])
    mla_w_in = din("mla_w_in", [1, 1024, 2752])
    mla_w_q_up = din("mla_w_q_up", [1, 384, 3072])
    mla_w_kv_up = din("mla_w_kv_up", [1, 256, 4096])
    mla_w_out = din("mla_w_out", [1, 2048, 1024])
    out_d = nc.dram_tensor("out", [2, 128, 8, OWN], F32, kind="ExternalOutput").ap()
    xpark = nc.dram_tensor("xpark", [2, 128, 8, NTOK], F32).ap()
    csd = nc.dram_tensor("csd", [2, 64, 2 * NTOK], BF16).ap()
    gin2 = [nc.dram_tensor("gin%d" % i, [320, 1024], BF16).ap() for i in range(2)]
    gout2 = [nc.dram_tensor("gout%d" % i, [640, 1024], BF16).ap() for i in range(2)]
    dbg = None
    if debug_out:
        dbg = nc.dram_tensor("dbg", [2, 128, 8, NTOK], F32, kind="ExternalOutput").ap()
        dbg_lat = nc.dram_tensor("dbg_lat", [320, 2048], BF16, kind="ExternalOutput").ap()

    S = Sched(nc)
    es = ExitStack()
    with es:
        def sb(name, shape, dt):
            return es.enter_context(nc.sbuf_tensor(name, list(shape), dt))

        resid = sb("resid", [128, 8, NTOK], F32)
        xn = sb("xn", [128, 8, NTOK], BF16)
        vecs = sb("vecs_sb", [128, NV], F32)
        masks = sb("masks_sb", [128, 320], BF16)
        ones = sb("ones_sb", [128, 128], BF16)
        rcnt = sb("rcnt_sb", [128, 64], F32)
        wA = [sb("wA%d" % i, [128, 8, 512], BF16) for i in range(2)]
        wGr = [sb("wG%d" % i, [128, 2048], BF16) for i in range(2)]
        wO = [sb("wO%d" % i, [128, 4, 1024], BF16) for i in range(1)]
        sq = [sb("sq%d" % i, [128, TW], BF16) for i in range(3)]
        rstd = [sb("rstd%d" % i, [128, TW], F32) for i in range(1)] * 2
        tmpf = [sb("tmpf%d" % i, [128, TW], F32) for i in range(4)]
        X = sb("X", [128, 8512], F32)
        Y = sb("Y", [128, 2, 4, NTOK], BF16)
        qn = sb("qn", [128, 3, NTOK], BF16)
        kvn = sb("kvn", [128, 2, 4096], BF16)
        kr = sb("kr", [64, 4096], BF16)
        qnope = [sb("qnope%d" % i, [128, NTOK], BF16) for i in range(2)]
        qrope = [sb("qrope%d" % i, [64, NTOK], BF16) for i in range(2)]
        szb = [sb("sz%d" % i, [128, NTOK], BF16) for i in range(2)]
        cs_tab = sb("cs_tab", [64, 2, NTOK], BF16)
        PSr = sb("PSr", [128, 6 * TW], BF16)
        accs = [sb("acc%d" % i, [128, TW], F32) for i in range(1)] * 2
        accq = [sb("accq%d" % i, [128, TW], F32) for i in range(1)] * 2
        posi = sb("posi", [64, TW], I32)
        ps = [es.enter_context(nc.psum_tensor("ps%d" % i, [128, 512], F32)) for i in range(8)]

        Xa = X[:]
        U = [Xa[:, b * 1072:(b + 1) * 1072] for b in range(2)]
        TA = Xa[:, 2144:3216]
        TB = Xa[:, 3216:4288]
        pooled = Xa[:, 4288:8512].bitcast(BF16).rearrange("p (b c t) -> p b c t", b=2, c=4)
        Kb = [Xa[:, b * 2048:(b + 1) * 2048].bitcast(BF16) for b in range(2)]
        Vb = [Xa[:, 4096 + b * 2048:4096 + (b + 1) * 2048].bitcast(BF16).rearrange("p (j d) -> p j d", d=128)
              for b in range(2)]
        wAx = [wA[0], wA[1]] + [Xa[:, 4288 + i * 2048:4288 + (i + 1) * 2048].bitcast(BF16).rearrange("p (k n) -> p k n", k=8)
                                for i in range(2)]
        WX_K = [("wA", 2), ("wA", 3)]
        wG = [w[:].rearrange("p (c n) -> p c n", c=4) for w in wGr]
        wQ = [w[:, 0:768].rearrange("p (c n) -> p c n", c=3) for w in wGr]
        wKV = [w[:, 1024:1536].rearrange("p (c n) -> p c n", c=2) for w in wGr]
        Pt = [PSr[:, i * TW:(i + 1) * TW] for i in range(5)]
        accb = PSr[:, 5 * TW:6 * TW]
        stage = [PSr[:, i * 3 * TW:(i + 1) * 3 * TW].rearrange("p (c n) -> p c n", c=3) for i in range(2)]
        tri = masks[:, 0:128]
        tri_h = masks[:, 128:160]
        mBH1 = masks[:, 160:192]
        ident = masks[:, 192:320]

        XK_A = [("U", 0), ("U", 1), ("TA",), ("TB",)] + [("pooled", p, c) for p in range(2) for c in range(4)]
        XK_B = [(kv_, b_, t_) for kv_ in ("K", "V") for b_ in range(2) for t_ in range(8)]
        PS_A = [("P", i) for i in range(5)] + [("accb",)]
        PS_B = [("stage", i) for i in range(2)]
        WG_A = [("wG", i) for i in range(2)]
        WG_B = [("wQ", i) for i in range(2)] + [("wKV", i) for i in range(2)]

        def vcol(c, rows=128):
            return vecs[0:rows, c:c + 1]

        rr = {}

        def rot(name, n):
            v = rr.get(name, 0)
            rr[name] = (v + 1) % n
            return v

        def nb():
            return rot("bank", 8)

        def VE(fn, r, w):
            S.add("vector", fn, reads=r, writes=w)

        def AC(fn, r, w):
            S.add("scalar", fn, reads=r, writes=w)

        def PO(fn, r, w):
            S.add("gpsimd", fn, reads=r, writes=w)

        def PE(fn, r, w):
            S.add("tensor", fn, reads=r, writes=w)

        def DMA(eng, fn, r, w, slot, n=1):
            S.add(eng, fn, reads=r, writes=w, dma_slot=slot, ndma=n)

        def rkeys(ti):
            return [("resid", m, ti) for m in range(8)]

        def xkeys(ti):
            return [("xn", k, ti) for k in range(8)]

        DMA("sync", lambda e: e.dma_start(out=vecs[:], in_=vecs_d), [], ["vecs"], "c0")
        DMA("gpsimd", lambda e: e.dma_start(out=masks[:], in_=masks_d), [], ["masks"], "c1")
        PO(lambda e: e.memset(ones[:], 1.0), [], ["ones"])
        for (buf_, key_) in ((U[0], ("U", 0)), (U[1], ("U", 1)), (TA, ("TA",)), (TB, ("TB",))):
            PO(lambda e, buf_=buf_: e.memset(buf_[:, 0:16], 0.0), [], [key_])

        def emit_norm(gcol, tis):
            for ti in tis:
                off, n = TILES[ti]
                b = nb()
                for k in range(8):
                    si = rot("sq", 3)
                    if k % 2 == 0:
                        AC(lambda e, k=k, si=si: e.activation(out=sq[si][:, :n], in_=resid[:, k, off:off + n], func=AF.Square),
                           [("resid", k, ti)], [("sq", si)])
                    else:
                        PO(lambda e, k=k, si=si: e.tensor_tensor(out=sq[si][:, :n], in0=resid[:, k, off:off + n],
                                                                 in1=resid[:, k, off:off + n], op=ALU.mult),
                           [("resid", k, ti)], [("sq", si)])
                    PE(lambda e, k=k, si=si: e.matmul(ps[b][:, :n], lhsT=ones[:], rhs=sq[si][:, :n], start=(k == 0), stop=(k == 7)),
                       [("sq", si), "ones"], [("ps", b)])
                ri = 0
                AC(lambda e, ri=ri: e.activation(out=rstd[ri][:, :n], in_=ps[b][:, :n], func=AF.Ln, bias=vcol(C_EPS), scale=1.0 / 1024),
                   [("ps", b), "vecs"], [("rstd", ri)])
                AC(lambda e, ri=ri: e.activation(out=rstd[ri][:, :n], in_=rstd[ri][:, :n], func=AF.Exp, scale=-0.5), [("rstd", ri)], [("rstd", ri)])
                for k in range(8):
                    VE(lambda e, k=k, ri=ri: e.scalar_tensor_tensor(out=xn[:, k, off:off + n], in0=resid[:, k, off:off + n],
                                                                      scalar=vcol(gcol + k), in1=rstd[ri][:, :n],
                                                                      op0=ALU.mult, op1=ALU.mult),
                       [("resid", k, ti), ("rstd", ri), "vecs"], [("xn", k, ti)])

        def load_wA(buf, src3, ncols, col0=0):
            DMA("gpsimd", lambda e: e.dma_start(out=wA[buf][:, :, col0:col0 + ncols], in_=src3), [], [("wA", buf)], "wA%d" % buf)

        def win3(w2d, c0, c1):
            return w2d.rearrange("(k p) n -> p k n", p=128)[:, :, c0:c1]

        pend = []

        def flush_pending():
            while pend:
                for st_ in pend.pop(0):
                    st_()

        def emit_wout(wobuf, ypar, tis, bankfn=None, defer=False):
            bankfn = bankfn or nb
            steps = []
            for m in range(8):
                for ti in tis:
                    def step(m=m, ti=ti):
                        off, n = TILES[ti]
                        b = bankfn()

                        def mm(e):
                            for d in range(4):
                                r = e.matmul(ps[b][:, :n], lhsT=wO[wobuf][:, d, m * 128:(m + 1) * 128],
                                             rhs=Y[:, ypar, d, off:off + n], start=(d == 0), stop=(d == 3))
                            return r
                        PE(mm, [("wO", wobuf)] + [("y", ypar, d, ti) for d in range(4)], [("ps", b)])

                        def wfin():
                            VE(lambda e: e.tensor_tensor(out=resid[:, m, off:off + n], in0=resid[:, m, off:off + n],
                                                         in1=ps[b][:, :n], op=ALU.add),
                               [("ps", b), ("resid", m, ti)], [("resid", m, ti)])
                        if defer:
                            return wfin
                        wfin()
                    steps.append(step)
            return steps

        def load_wO(buf, w2d, r0):
            DMA("gpsimd", lambda e: e.dma_start(out=wO[buf][:], in_=w2d[r0:r0 + 512, :].rearrange("(d p) n -> p d n", p=128)),
                [], [("wO", buf)], "wO%d" % buf)

        def emit_pool(s, j, tis, last, after_norm=None):
            gc = C_PNORM[j]
            sc = C_PSCALE[j]
            w_in = pool_w_in[j]
            w_out = pool_w_out[j]
            wu, wz = 0, 1

            def ld_u(g):
                load_wA(wu, win3(w_in, g * 512, (g + 1) * 512), 512)

            def ld_z(g):
                load_wA(wz, win3(w_in, 2048 + g * 512, 2048 + (g + 1) * 512), 512)

            def ld_g(g):
                wg_ = g % 2
                DMA("gpsimd", lambda e: e.dma_start(out=wG[wg_], in_=pool_w_grp[j, g].rearrange("(c p) n -> p c n", p=128)),
                    [], [("wG", wg_)], "wG%d" % wg_)
            ld_u(0)
            ld_z(0)
            ld_g(0)
            emit_norm(gc, [0, 1, 2])
            if after_norm is not None:
                after_norm()
            DMA("sync", lambda e: e.dma_start(out=rcnt[:], in_=rcnt_d[s]), [], ["rcnt"], "c0")
            for g in range(4):
                wg = g % 2
                ppar = g % 2
                win = 2 << g
                pend_steps = []
                while pend:
                    pend_steps += pend.pop(0)
                for c in range(4):
                    q0, q1 = (c * len(pend_steps)) // 4, ((c + 1) * len(pend_steps)) // 4
                    if c > 0:
                        for st_ in pend_steps[(c - 1) * len(pend_steps) // 4:c * len(pend_steps) // 4]:
                            st_()
                    ub = rot("U", 2)
                    for ti in range(3):
                        off, n = TILES[ti]
                        b = nb()

                        def mm(e, c=c, off=off, n=n, b=b, wu=wu):
                            for k in range(8):
                                r = e.matmul(ps[b][:, :n], lhsT=wA[wu][:, k, c * 128:(c + 1) * 128], rhs=xn[:, k, off:off + n],
                                             start=(k == 0), stop=(k == 7))
                            return r
                        PE(mm, [("wA", wu)] + xkeys(ti), [("ps", b)])
                        AC(lambda e, off=off, n=n, b=b, ub=ub: e.activation(out=U[ub][:, 16 + off:16 + off + n], in_=ps[b][:, :n], func=AF.Copy),
                           [("ps", b)], [("U", ub)])
                    src = U[ub]
                    srck = ("U", ub)
                    sh = 1
                    bufs = [(TA, ("TA",)), (TB, ("TB",))]
                    bi = 0
                    while sh < win:
                        dst, dstk = bufs[bi]
                        (PO if (c % 2 == 1 or (g >= 2 and c > 0)) else VE)(lambda e, src=src, dst=dst, sh=sh: e.tensor_tensor(out=dst[:, 16:16 + NTOK], in0=src[:, 16:16 + NTOK],
                                                                              in1=src[:, 16 - sh:16 - sh + NTOK], op=ALU.add),
                           [srck], [dstk])
                        src, srck = dst, dstk
                        sh *= 2
                        bi ^= 1
                    VE(lambda e, src=src, c=c, ub=ub: e.scalar_tensor_tensor(out=pooled[:, ppar, c, :], in0=src[:, 16:16 + NTOK], scalar=1.0 / win,
                                                                            in1=U[ub][:, 16:16 + NTOK], op0=ALU.mult, op1=ALU.subtract),
                       [srck, ("U", ub)], [("pooled", ppar, c)])
                    t16 = rot("tmpf", 4)
                    VE(lambda e, src=src, t16=t16: e.tensor_tensor(out=tmpf[t16][:, 0:16], in0=src[:, 16 + HALO:16 + HALO + 16],
                                                                   in1=rcnt[:, g * 16:(g + 1) * 16], op=ALU.mult),
                       [srck, "rcnt"], [("tmpf", t16)])
                    VE(lambda e, c=c, ub=ub, t16=t16: e.tensor_tensor(out=pooled[:, ppar, c, HALO:HALO + 16], in0=tmpf[t16][:, 0:16],
                                                                      in1=U[ub][:, 16 + HALO:16 + HALO + 16], op=ALU.subtract),
                       [("tmpf", t16), ("U", ub)], [("pooled", ppar, c)])
                if g + 1 < 4:
                    ld_u(g + 1)
                    ld_g(g + 1)
                for st_ in pend_steps[3 * len(pend_steps) // 4:]:
                    st_()
                flush_pending()
                ypar = rot("ypar", 2)
                load_wO(0, w_out, g * 512)
                for d in range(4):
                    for ti in tis:
                        off, n = TILES[ti]
                        bz = nb()

                        def mmz(e, d=d, off=off, n=n, bz=bz, wz=wz):
                            for k in range(8):
                                r = e.matmul(ps[bz][:, :n], lhsT=wA[wz][:, k, d * 128:(d + 1) * 128], rhs=xn[:, k, off:off + n],
                                             start=(k == 0), stop=(k == 7))
                            return r
                        PE(mmz, [("wA", wz)] + xkeys(ti), [("ps", bz)])
                        AC(lambda e, d=d, off=off, n=n, bz=bz, ypar=ypar: e.activation(out=Y[:, ypar, d, off:off + n], in_=ps[bz][:, :n], func=AF.Silu),
                           [("ps", bz)], [("y", ypar, d, ti)])
                for d in range(4):
                    for ti in tis:
                        off, n = TILES[ti]
                        bm = nb()

                        def mmg(e, d=d, off=off, n=n, bm=bm, wg=wg):
                            for c in range(4):
                                r = e.matmul(ps[bm][:, :n], lhsT=wG[wg][:, c, d * 128:(d + 1) * 128], rhs=pooled[:, ppar, c, off:off + n],
                                             start=(c == 0), stop=(c == 3))
                            return r
                        PE(mmg, [("wG", wg)] + [("pooled", ppar, c) for c in range(4)], [("ps", bm)])
                        VE(lambda e, d=d, off=off, n=n, bm=bm, ypar=ypar: e.scalar_tensor_tensor(
                            out=Y[:, ypar, d, off:off + n], in0=ps[bm][:, :n], scalar=vcol(sc + g * 4 + d), in1=Y[:, ypar, d, off:off + n],
                            op0=ALU.mult, op1=ALU.mult),
                           [("ps", bm), ("y", ypar, d, ti), "vecs"], [("y", ypar, d, ti)])
                if g + 1 < 4:
                    ld_z(g + 1)
                pend.append(emit_wout(0, ypar, tis))
            if last:
                flush_pending()

        def emit_conv(s, hooks=None):
            w_in = conv_w_in[0]
            w_out = conv_w_out[0]
            S.alias([("pooled", p_, c_) for p_ in range(2) for c_ in range(4)], WX_K)
            src = w_in.rearrange("(k p) n -> p k n", p=128)

            def ld_chunk(j):
                wb_ = j % 4

                def ld(e):
                    r = []
                    for q in range(4):
                        r.append(e.dma_start(out=wAx[wb_][:, :, q * 128:(q + 1) * 128],
                                             in_=src[:, :, q * 2048 + j * 128:q * 2048 + (j + 1) * 128]))
                    return r
                DMA("gpsimd", ld, [], [("wA", wb_)], "wA%d" % wb_, n=4)
            for j0 in range(3):
                ld_chunk(j0)
            emit_norm(C_CNORM, [0, 1, 2])
            for jh in range(4):
                ypar = rot("ypar", 2)
                for jj in range(4):
                    j = jh * 4 + jj
                    wb = j % 4
                    if j + 3 < 16:
                        ld_chunk(j + 3)
                    cb = rot("U", 2)
                    CH = U[cb]
                    for ti in range(3):
                        off, n = TILES[ti]
                        bq = []
                        for q in range(4):
                            b = nb()
                            bq.append(b)

                            def mm(e, q=q, off=off, n=n, b=b, wb=wb):
                                for k in range(8):
                                    r = e.matmul(ps[b][:, :n], lhsT=wAx[wb][:, k, q * 128:(q + 1) * 128], rhs=xn[:, k, off:off + n],
                                                 start=(k == 0), stop=(k == 7))
                                return r
                            PE(mm, [("wA", wb)] + xkeys(ti), [("ps", b)])
                        bb, bc, bh, bz = bq
                        tc_ = rot("tmpf", 4)
                        AC(lambda e, n=n, bc=bc, tc_=tc_: e.activation(out=tmpf[tc_][:, :n], in_=ps[bc][:, :n], func=AF.Copy),
                           [("ps", bc)], [("tmpf", tc_)])
                        VE(lambda e, off=off, n=n, bh=bh, tc_=tc_, CH=CH: e.tensor_tensor(out=CH[:, 16 + off:16 + off + n], in0=tmpf[tc_][:, :n],
                                                                                      in1=ps[bh][:, :n], op=ALU.mult),
                           [("ps", bh), ("tmpf", tc_)], [("U", cb)])
                        tz = rot("tmpf", 4)
                        AC(lambda e, n=n, bz=bz, tz=tz: e.activation(out=tmpf[tz][:, :n], in_=ps[bz][:, :n], func=AF.Silu),
                           [("ps", bz)], [("tmpf", tz)])
                        VE(lambda e, n=n, bb=bb, tz=tz: e.tensor_tensor(out=tmpf[tz][:, :n], in0=tmpf[tz][:, :n], in1=ps[bb][:, :n], op=ALU.mult),
                           [("ps", bb), ("tmpf", tz)], [("tmpf", tz)])
                        ta = rot("tmpf", 4)
                        cw = C_CONVW + j * 3
                        AC(lambda e, off=off, n=n, ta=ta, CH=CH, cw=cw: e.activation(out=tmpf[ta][:, :n], in_=CH[:, 16 + off:16 + off + n],
                                                                                 func=AF.Copy, scale=vcol(cw + 2)),
                           [("U", cb), "vecs"], [("tmpf", ta)])
                        VE(lambda e, off=off, n=n, ta=ta, CH=CH, cw=cw: e.scalar_tensor_tensor(
                            out=tmpf[ta][:, :n], in0=CH[:, 15 + off:15 + off + n], scalar=vcol(cw + 1), in1=tmpf[ta][:, :n],
                            op0=ALU.mult, op1=ALU.add), [("U", cb), ("tmpf", ta), "vecs"], [("tmpf", ta)])
                        VE(lambda e, off=off, n=n, ta=ta, CH=CH, cw=cw: e.scalar_tensor_tensor(
                            out=tmpf[ta][:, :n], in0=CH[:, 14 + off:14 + off + n], scalar=vcol(cw + 0), in1=tmpf[ta][:, :n],
                            op0=ALU.mult, op1=ALU.add), [("U", cb), ("tmpf", ta), "vecs"], [("tmpf", ta)])
                        VE(lambda e, off=off, n=n, ta=ta, tz=tz, jj=jj, ypar=ypar: e.tensor_tensor(
                            out=Y[:, ypar, jj, off:off + n], in0=tmpf[ta][:, :n], in1=tmpf[tz][:, :n], op=ALU.mult),
                           [("tmpf", ta), ("tmpf", tz)], [("y", ypar, jj, ti)])
                    if jj == 0:
                        flush_pending()
                        load_wO(0, w_out, jh * 512)
                    if hooks:
                        hooks(j)
                pend.append(emit_wout(0, ypar, [0, 1, 2]))
            flush_pending()
            S.alias(WX_K, XK_A + XK_B)

        def emit_rope_tab(s, c0, ncol):
            ops = []
            T1, T2, T3 = TA[0:64, 0:ncol], TA[0:64, 528:528 + ncol], TB[0:64, 0:ncol]
            k1 = k2 = ("TA",)
            k3 = ("TB",)
            pieces = [(0, min(TW, ncol))] + ([(TW, ncol - TW)] if ncol > TW else [])
            for (o_, n_) in pieces:
                ops.append(lambda o_=o_, n_=n_: DMA("sync", lambda e: e.dma_start(out=posi[:, :n_], in_=pos_in[s:s + 1, c0 + o_:c0 + o_ + n_].partition_broadcast(64)),
                    [], ["posi"], "c0"))
                ops.append(lambda o_=o_, n_=n_: VE(lambda e: e.tensor_copy(out=T1[:, o_:o_ + n_], in_=posi[:, :n_]), ["posi", k1], [k1]))
            ops.append(lambda: VE(lambda e: e.tensor_scalar(out=T1, in0=T1, scalar1=vcol(C_INVF, 64), scalar2=None, op0=ALU.mult), [k1, "vecs"], [k1]))
            for which in range(2):
                ops.append(lambda which=which: VE(lambda e: e.tensor_scalar(out=T2, in0=T1, scalar1=INV2PI, scalar2=(0.25 if which == 0 else 0.0),
                                                                            op0=ALU.mult, op1=ALU.add), [k1], [k2]))
                for (o_, n_) in pieces:
                    ops.append(lambda o_=o_, n_=n_: VE(lambda e: e.tensor_copy(out=posi[:, :n_], in_=T2[:, o_:o_ + n_]), [k2, "posi"], ["posi"]))
                    ops.append(lambda o_=o_, n_=n_: VE(lambda e: e.tensor_copy(out=T2[:, o_:o_ + n_], in_=posi[:, :n_]), ["posi", k2], [k2]))
                ops.append(lambda: VE(lambda e: e.scalar_tensor_tensor(out=T3, in0=T2, scalar=-CW1, in1=T1, op0=ALU.mult, op1=ALU.add), [k1, k2], [k3]))
                ops.append(lambda: VE(lambda e: e.scalar_tensor_tensor(out=T3, in0=T2, scalar=-CW2, in1=T3, op0=ALU.mult, op1=ALU.add), [k2, k3], [k3]))
                if which == 0:
                    ops.append(lambda: VE(lambda e: e.tensor_scalar(out=T3, in0=T3, scalar1=float(np.pi / 2), scalar2=3.1415925, op0=ALU.add, op1=ALU.min), [k3], [k3]))
                    ops.append(lambda: VE(lambda e: e.tensor_scalar(out=T3, in0=T3, scalar1=-3.1415925, scalar2=None, op0=ALU.max), [k3], [k3]))
                else:
                    ops.append(lambda: VE(lambda e: e.tensor_scalar(out=T3, in0=T3, scalar1=3.1415925, scalar2=-3.1415925, op0=ALU.min, op1=ALU.max), [k3], [k3]))
                ops.append(lambda: AC(lambda e: e.activation(out=T3, in_=T3, func=AF.Sin), [k3], [k3]))
                cskeys = [("cs", which, ti) for ti in range(3)]
                if which == 0:
                    ops.append(lambda cskeys=cskeys: VE(lambda e: e.tensor_copy(out=cs_tab[:, 0, c0:c0 + ncol], in_=T3), [k3] + cskeys, cskeys))
                else:
                    ops.append(lambda cskeys=cskeys: VE(lambda e: e.tensor_scalar(out=cs_tab[:, 1, c0:c0 + ncol], in0=T3, scalar1=vcol(C_SGN, 64), scalar2=None, op0=ALU.mult),
                                                        [k3, "vecs"] + cskeys, cskeys))
            return ops

        def emit_rope(dst, ps_x, ps_sw, ti, r, w):
            off, n = TILES[ti]
            t1 = rot("tmpf", 4)
            t2 = rot("tmpf", 4)
            VE(lambda e: e.tensor_tensor(out=tmpf[t1][0:64, :n], in0=ps_x[0:64, :n], in1=cs_tab[:, 0, off:off + n], op=ALU.mult),
               r + [("cs", 0, ti)], [("tmpf", t1)])
            VE(lambda e: e.tensor_tensor(out=tmpf[t2][0:64, :n], in0=ps_sw[0:64, :n], in1=cs_tab[:, 1, off:off + n], op=ALU.mult),
               r + [("cs", 1, ti)], [("tmpf", t2)])
            PO(lambda e: e.tensor_tensor(out=dst, in0=tmpf[t1][0:64, :n], in1=tmpf[t2][0:64, :n], op=ALU.add),
               [("tmpf", t1), ("tmpf", t2)], w)

        def emit_lat(s):
            w_in = mla_w_in[0]
            wb = 0
            src = w_in.rearrange("(k p) n -> p k n", p=128)

            def ld(e):
                return [e.dma_start(out=wA[wb][:, :, 0:320], in_=src[:, :, 384:704]),
                        e.dma_start(out=wA[wb][:, :, 320:352], in_=src[:, :, 672:704]),
                        e.dma_start(out=wA[wb][:, :, 352:384], in_=src[:, :, 640:672])]
            DMA("gpsimd", ld, [], [("wA", wb)], "wA%d" % wb, n=3)
            emit_norm(C_MNORM, [0, 1, 2])
            for ti in (0, 1, 2):
                off, n = TILES[ti]
                b0, b1, b2, b3, b4 = [nb() for _ in range(5)]

                def mm(e, c0, c1, b):
                    for k in range(8):
                        r = e.matmul(ps[b][0:c1 - c0, :n], lhsT=wA[wb][:, k, c0:c1], rhs=xn[:, k, off:off + n], start=(k == 0), stop=(k == 7))
                    return r
                for (c0, c1, b) in ((0, 128, b0), (128, 256, b1), (256, 320, b2), (320, 384, b3)):
                    PE(lambda e, c0=c0, c1=c1, b=b: mm(e, c0, c1, b), [("wA", wb)] + xkeys(ti), [("ps", b)])
                sis = []
                for c, b in ((0, b0), (1, b1)):
                    si = rot("sq", 3)
                    sis.append(si)
                    AC(lambda e, b=b, si=si: e.activation(out=sq[si][:, :n], in_=ps[b][:, :n], func=AF.Square), [("ps", b)], [("sq", si)])
                    PE(lambda e, c=c, si=si: e.matmul(ps[b4][:, :n], lhsT=ones[:], rhs=sq[si][:, :n], start=(c == 0), stop=(c == 1)),
                       [("sq", si), "ones"], [("ps", b4)])
                ri = 0
                AC(lambda e, ri=ri: e.activation(out=rstd[ri][:, :n], in_=ps[b4][:, :n], func=AF.Ln, bias=vcol(C_EPS), scale=1.0 / 256),
                   [("ps", b4), "vecs"], [("rstd", ri)])
                AC(lambda e, ri=ri: e.activation(out=rstd[ri][:, :n], in_=rstd[ri][:, :n], func=AF.Exp, scale=-0.5), [("rstd", ri)], [("rstd", ri)])
                st = rot("stage", 2)
                for c, b in ((0, b0), (1, b1)):
                    VE(lambda e, c=c, b=b, ri=ri, st=st: e.scalar_tensor_tensor(out=stage[st][:, c, :n], in0=ps[b][:, :n], scalar=vcol(C_KVNORM + c),
                                                                                in1=rstd[ri][:, :n], op0=ALU.mult, op1=ALU.mult),
                       [("ps", b), ("rstd", ri), "vecs"], [("stage", st)])
                emit_rope(stage[st][0:64, 2, :n], ps[b2], ps[b3], ti, [("ps", b2), ("ps", b3)], [("stage", st)])
                lo = max(off, HALO) - off
                t0 = off + lo - HALO
                no = n - lo

                def st_dma(e, st=st, t0=t0, n=n, lo=lo, no=no):
                    return [e.dma_start(out=gin2[s][0:256, t0:t0 + no].rearrange("(c p) n -> p c n", p=128), in_=stage[st][:, 0:2, lo:n]),
                            e.dma_start(out=gin2[s][256:320, t0:t0 + no], in_=stage[st][0:64, 2, lo:n])]
                DMA("sync", st_dma, [("stage", st)], [("gin", s)], "gin", n=2)

        lat_done = set()
        cs_done = set()

        def lat_segs(s):
            if s == 0:
                return [("r0A", gout2[0][0:320, :]), ("own", gin2[0][0:320, :])]
            return [("r0A", gout2[0][0:320, :]), ("r1A", gout2[0][320:640, :]), ("r1B", gout2[1][320:640, :]),
                    ("own", gin2[1][0:320, :])]

        def load_latents(s):
            if s in lat_done:
                return
            lat_done.add(s)
            rd = [("gout", 0), ("gin", 0)] if s == 0 else [("gout", 0), ("gout", 1), ("gin", 0), ("gin", 1)]
            for i, (nm, ap_) in enumerate(lat_segs(s)):
                def ld(e, i=i, ap_=ap_):
                    return [e.dma_start(out=kvn[:, :, i * 1024:(i + 1) * 1024], in_=ap_[0:256, :].rearrange("(c p) n -> p c n", p=128)),
                            e.dma_start(out=kr[:, i * 1024:(i + 1) * 1024], in_=ap_[256:320, :])]
                DMA("sync", ld, rd, [("kvn", i), ("kr", i)], "lat", n=2)

        def load_cs(s):
            if s in cs_done:
                return
            cs_done.add(s)
            DMA("sync", lambda e: e.dma_start(out=cs_tab[:].rearrange("p a t -> p (a t)"), in_=csd[s]), ["csd"],
                [("cs", w_, t_) for w_ in range(2) for t_ in range(3)], "lat")

        def emit_attn(s):
            w_in = mla_w_in[0]
            w_out = mla_w_out[0]
            NK = 2048 if s == 0 else 4096
            load_latents(s)
            segs = lat_segs(s)
            nseg = len(segs)
            qnk = lambda ti: [("qn", c, ti) for c in range(3)]

            B_S = (0, 1, 2, 3)
            B_O = (4, 5)
            B_L = (6,)
            B_P = (6, 7)

            def pbank():
                return B_P[rot("pbank", 2)]

            def prep_steps(h, hb, wzb, hh):
                steps = []
                wq = hb

                def ldw():
                    srcq = mla_w_q_up[0].rearrange("(c p) n -> p c n", p=128)
                    srck = mla_w_kv_up[0].rearrange("(c p) n -> p c n", p=128)

                    def ld(e):
                        return [e.dma_start(out=wQ[wq][:, :, 0:192], in_=srcq[:, :, h * 192:(h + 1) * 192]),
                                e.dma_start(out=wQ[wq][:, :, 192:224], in_=srcq[:, :, h * 192 + 160:h * 192 + 192]),
                                e.dma_start(out=wQ[wq][:, :, 224:256], in_=srcq[:, :, h * 192 + 128:h * 192 + 160]),
                                e.dma_start(out=wKV[wq][:, :, :], in_=srck[:, :, h * 256:(h + 1) * 256])]
                    DMA("gpsimd", ld, [], [("wQ", wq), ("wKV", wq)], "wG%d" % wq, n=4)
                steps.append(ldw)
                for kt in range(NK // 512):
                    def kstep(kt=kt):
                        b = pbank()
                        seg = kt // 2

                        def mm(e):
                            for c in range(2):
                                r = e.matmul(ps[b][:, :], lhsT=wKV[wq][:, c, 0:128], rhs=kvn[:, c, kt * 512:(kt + 1) * 512], start=(c == 0), stop=(c == 1))
                            return r
                        PE(mm, [("wKV", wq), ("kvn", seg)], [("ps", b)])
                        return lambda: VE(lambda e: e.tensor_copy(out=Kb[hb][:, kt * 512:(kt + 1) * 512], in_=ps[b][:, :]), [("ps", b)], [("K", hb, kt)])
                    steps.append(kstep)

                    def vstep(kt=kt):
                        b = pbank()
                        seg = kt // 2

                        def mm(e):
                            for jb in range(4):
                                for c in range(2):
                                    r = e.matmul(ps[b][:, jb * 128:(jb + 1) * 128], lhsT=kvn[:, c, kt * 512 + jb * 128:kt * 512 + (jb + 1) * 128],
                                                 rhs=wKV[wq][:, c, 128:256], start=(c == 0), stop=(c == 1))
                            return r
                        PE(mm, [("wKV", wq), ("kvn", seg)], [("ps", b)])
                        return lambda: VE(lambda e: e.tensor_copy(out=Vb[hb][:, kt * 4:(kt + 1) * 4, :], in_=ps[b][:, :].rearrange("p (j d) -> p j d", d=128)),
                                          [("ps", b)], [("V", hb, kt)])
                    steps.append(vstep)
                for ti in range(3):
                    off, n = TILES[ti]

                    def qstep(ti=ti, off=off, n=n):
                        b = pbank()

                        def mm(e):
                            for c in range(3):
                                r = e.matmul(ps[b][:, :n], lhsT=wQ[wq][:, c, 0:128], rhs=qn[:, c, off:off + n], start=(c == 0), stop=(c == 2))
                            return r
                        PE(mm, [("wQ", wq)] + qnk(ti), [("ps", b)])
                        return lambda: VE(lambda e: e.tensor_copy(out=qnope[hb][:, off:off + n], in_=ps[b][:, :n]), [("ps", b)], [("qnope", hb, ti)])
                    steps.append(qstep)

                    def rstep(ti=ti, off=off, n=n):
                        b1 = pbank()
                        b2 = pbank()

                        def mm1(e):
                            for c in range(3):
                                r = e.matmul(ps[b1][0:64, :n], lhsT=wQ[wq][:, c, 128:192], rhs=qn[:, c, off:off + n], start=(c == 0), stop=(c == 2))
                            return r

                        def mm2(e):
                            for c in range(3):
                                r = e.matmul(ps[b2][0:64, :n], lhsT=wQ[wq][:, c, 192:256], rhs=qn[:, c, off:off + n], start=(c == 0), stop=(c == 2))
                            return r
                        PE(mm1, [("wQ", wq)] + qnk(ti), [("ps", b1)])
                        PE(mm2, [("wQ", wq)] + qnk(ti), [("ps", b2)])
                        return lambda: emit_rope(qrope[hb][:, off:off + n], ps[b1], ps[b2], ti, [("ps", b1), ("ps", b2)], [("qrope", hb, ti)])
                    steps.append(rstep)

                    def zstep(ti=ti, off=off, n=n):
                        b = pbank()

                        def mm(e):
                            for k in range(8):
                                r = e.matmul(ps[b][:, :n], lhsT=wA[wzb][:, k, hh * 128:(hh + 1) * 128], rhs=xn[:, k, off:off + n], start=(k == 0), stop=(k == 7))
                            return r
                        PE(mm, [("wA", wzb)] + xkeys(ti), [("ps", b)])
                        def zfin():
                            tq = rot("tmpf", 4)
                            AC(lambda e: e.activation(out=tmpf[tq][:, :n], in_=ps[b][:, :n], func=AF.Tanh, scale=0.5), [("ps", b)], [("tmpf", tq)])
                            VE(lambda e: e.scalar_tensor_tensor(out=szb[hb][:, off:off + n], in0=tmpf[tq][:, :n], scalar=1.0, in1=ps[b][:, :n],
                                                                op0=ALU.add, op1=ALU.mult), [("tmpf", tq), ("ps", b)], [("sz", hb, ti)])
                        return zfin
                    steps.append(zstep)
                return steps

            prep0 = prep_steps(0, 0, 1, 0)
            n_kv = 1
            for st_ in prep0[:n_kv]:
                r_ = st_()
                if callable(r_):
                    r_()
            wb = 0
            load_wA(wb, win3(w_in, 0, 384), 384)
            load_wA(1, win3(w_in, 704, 704 + 512), 512)
            emit_norm(C_MNORM, [0, 1, 2])
            for ti in range(3):
                off, n = TILES[ti]
                bq = [nb() for _ in range(3)]
                b4 = nb()
                for c in range(3):
                    def mm(e, c=c, b=bq[c]):
                        for k in range(8):
                            r = e.matmul(ps[b][:, :n], lhsT=wA[wb][:, k, c * 128:(c + 1) * 128], rhs=xn[:, k, off:off + n], start=(k == 0), stop=(k == 7))
                        return r
                    PE(mm, [("wA", wb)] + xkeys(ti), [("ps", bq[c])])
                for c in range(3):
                    si = rot("sq", 3)
                    AC(lambda e, c=c, si=si: e.activation(out=sq[si][:, :n], in_=ps[bq[c]][:, :n], func=AF.Square), [("ps", bq[c])], [("sq", si)])
                    PE(lambda e, c=c, si=si: e.matmul(ps[b4][:, :n], lhsT=ones[:], rhs=sq[si][:, :n], start=(c == 0), stop=(c == 2)),
                       [("sq", si), "ones"], [("ps", b4)])
                ri = 0
                AC(lambda e, ri=ri: e.activation(out=rstd[ri][:, :n], in_=ps[b4][:, :n], func=AF.Ln, bias=vcol(C_EPS), scale=1.0 / 384),
                   [("ps", b4), "vecs"], [("rstd", ri)])
                AC(lambda e, ri=ri: e.activation(out=rstd[ri][:, :n], in_=rstd[ri][:, :n], func=AF.Exp, scale=-0.5), [("rstd", ri)], [("rstd", ri)])
                for c in range(3):
                    VE(lambda e, c=c, ri=ri: e.scalar_tensor_tensor(out=qn[:, c, off:off + n], in0=ps[bq[c]][:, :n], scalar=vcol(C_QNORM + c),
                                                                    in1=rstd[ri][:, :n], op0=ALU.mult, op1=ALU.mult),
                       [("ps", bq[c]), ("rstd", ri), "vecs"], [("qn", c, ti)])
            own = nseg - 1

            def tile_sched(ti):
                off, n = TILES[ti]
                a0 = off - HALO
                sched = []
                for sg in range(nseg - 1):
                    for jb in range(8):
                        bias = None
                        if s == 0 and sg == 0:
                            bias = C_FLAGA
                        if s == 1 and sg == 2:
                            bias = C_FLAGB
                        mk = []
                        if ti == 0 and jb == 7:
                            if s == 0 and sg == 0:
                                mk.append((0, HALO, tri_h))
                            if s == 1 and sg == 1:
                                mk.append((0, HALO, mBH1))
                            if s == 1 and sg == 2:
                                mk.append((0, HALO, tri_h))
                        sched.append((sg * 8 + jb, 0, n, bias, mk))
                for kb in range(8):
                    delta = 128 * kb - a0
                    if delta >= n:
                        continue
                    if delta >= 0:
                        w = min(128, n - delta)
                        sched.append((own * 8 + kb, delta, n, None, [(delta, w, tri[:, 0:w])]))
                    else:
                        wv = 128 + delta
                        if wv > 0:
                            sched.append((own * 8 + kb, 0, n, None, [(0, wv, tri[:, -delta:128])]))
                        else:
                            sched.append((own * 8 + kb, 0, n, None, []))
                return sched

            items = []
            head_range = {}
            tcount = 0
            for h in range(16):
                st0 = len(items)
                for ti in range(3):
                    off, n = TILES[ti]
                    ctx = dict(h=h, hb=h % 2, hh=h % 4, ypar=(h // 4) % 2, ti=ti, off=off, n=n, sched=tile_sched(ti),
                               bo=B_O[tcount % 2], bl=B_L[0], bs={}, ap=0)
                    tcount += 1
                    for i in range(len(ctx["sched"])):
                        items.append((ctx, i))
                head_range[h] = (st0, len(items))

            def qk(ctx, i):
                kb, c0, c1, bias, mk = ctx["sched"][i]
                hb, off, ti = ctx["hb"], ctx["off"], ctx["ti"]
                bs = B_S[rot("bs", 4)]
                ctx["bs"][i] = bs
                seg = kb // 8

                def mm(e):
                    e.matmul(ps[bs][:, c0:c1], lhsT=Kb[hb][:, kb * 128:(kb + 1) * 128], rhs=qnope[hb][:, off + c0:off + c1], start=True, stop=False)
                    r = e.matmul(ps[bs][:, c0:c1], lhsT=kr[:, kb * 128:(kb + 1) * 128], rhs=qrope[hb][:, off + c0:off + c1], start=False, stop=(len(mk) == 0))
                    for mi, (m0, mw, map_) in enumerate(mk):
                        r = e.matmul(ps[bs][:, m0:m0 + mw], lhsT=ident, rhs=map_, start=False, stop=(mi == len(mk) - 1))
                    return r
                PE(mm, [("K", hb, kb // 4), ("kr", seg), ("qnope", hb, ti), ("qrope", hb, ti), "masks"], [("ps", bs)])

            def pv(ctx, i):
                kb, c0, c1, bias, mk = ctx["sched"][i]
                hb, bo, bl = ctx["hb"], ctx["bo"], ctx["bl"]
                nblk = len(ctx["sched"])
                bs = ctx["bs"][i]
                pi = rot("P", 5)
                bias_ap = vcol(bias) if bias is not None else vcol(C_ZERO)
                AC(lambda e: e.activation(out=Pt[pi][:, c0:c1], in_=ps[bs][:, c0:c1], func=AF.Exp, bias=bias_ap, scale=ATT_SCALE),
                   [("ps", bs), "vecs"], [("P", pi)])

                if i % 2 == 0:
                    EN, acc, akey = VE, accs[ctx["ap"]], ("acc", ctx["ap"])
                else:
                    EN, acc, akey = PO, accq[ctx["ap"]], ("accq", ctx["ap"])
                if i < 2:
                    EN(lambda e: e.tensor_copy(out=acc[:, c0:c1], in_=Pt[pi][:, c0:c1]), [("P", pi)], [akey])
                else:
                    EN(lambda e: e.tensor_tensor(out=acc[:, c0:c1], in0=acc[:, c0:c1], in1=Pt[pi][:, c0:c1], op=ALU.add), [("P", pi), akey], [akey])
                PE(lambda e: e.matmul(ps[bo][:, c0:c1], lhsT=Vb[hb][:, kb, :], rhs=Pt[pi][:, c0:c1], start=(i == 0), stop=(i == nblk - 1)),
                   [("V", hb, kb // 4), ("P", pi)], [("ps", bo)])

            def fin(ctx):
                hb, bo, bl, off, n, ti = ctx["hb"], ctx["bo"], ctx["bl"], ctx["off"], ctx["n"], ctx["ti"]
                ypar, hh = ctx["ypar"], ctx["hh"]
                t1 = rot("tmpf", 4)
                t2 = rot("tmpf", 4)
                acc = accs[ctx["ap"]]
                acq = accq[ctx["ap"]]
                bl = pbank()
                PO(lambda e: e.tensor_tensor(out=accb[:, :n], in0=acc[:, :n], in1=acq[:, :n], op=ALU.add),
                   [("acc", ctx["ap"]), ("accq", ctx["ap"])], [("accb",)])
                PE(lambda e: e.matmul(ps[bl][:, :n], lhsT=ones[:], rhs=accb[:, :n], start=True, stop=True), [("accb",), "ones"], [("ps", bl)])
                VE(lambda e: e.tensor_scalar(out=tmpf[t1][:, :n], in0=ps[bl][:, :n], scalar1=2.0, scalar2=1e-30, op0=ALU.mult, op1=ALU.add),
                   [("ps", bl)], [("tmpf", t1)])
                VE(lambda e: e.reciprocal(out=tmpf[t1][:, :n], in_=tmpf[t1][:, :n]), [("tmpf", t1)], [("tmpf", t1)])
                VE(lambda e: e.tensor_tensor(out=tmpf[t2][:, :n], in0=ps[bo][:, :n], in1=tmpf[t1][:, :n], op=ALU.mult),
                   [("ps", bo), ("tmpf", t1)], [("tmpf", t2)])
                PO(lambda e: e.tensor_tensor(out=Y[:, ypar, hh, off:off + n], in0=tmpf[t2][:, :n], in1=szb[hb][:, off:off + n], op=ALU.mult),
                   [("tmpf", t2), ("sz", hb, ti)], [("y", ypar, hh, ti)])

            wz_of = {0: 1}

            def wz_for(hg):
                if hg not in wz_of:
                    bfi = (1 + hg) % 2
                    load_wA(bfi, win3(w_in, 704 + hg * 512, 704 + (hg + 1) * 512), 512)
                    wz_of[hg] = bfi
                return wz_of[hg]

            load_cs(s)
            for st_ in prep0[n_kv:]:
                r_ = st_()
                if callable(r_):
                    r_()
            load_wO(0, w_out, 0)
            prep_steps(1, 1, wz_for(0), 1)[0]()
            side_at = {}
            for h in range(16):
                side = []
                if h % 4 == 0 and h > 0:
                    side += emit_wout(0, ((h - 1) // 4) % 2, [0, 1, 2], pbank, defer=True)
                    side.append(lambda h=h: load_wO(0, w_out, (h // 4) * 512))
                if h % 4 == 1 and h // 4 + 1 < 4:
                    side.append(lambda h=h: wz_for(h // 4 + 1))
                if h + 1 < 16:
                    side.append(("prep", h + 1))
                if h + 2 < 16:
                    side.append(("ldw", h + 2))
                a_, b_ = head_range[h]
                span = max(1, int((b_ - a_) * 0.85))
                side_at[h] = (a_, span, side)

            def run_side(h):
                a_, span, side = side_at[h]
                flat = []
                for x in side:
                    if isinstance(x, tuple):
                        hn = x[1]
                        flat += prep_steps(hn, hn % 2, wz_of[hn // 4], hn % 4)
                    else:
                        flat.append(x)
                return flat

            qk(*items[0])
            qk(*items[1])
            qk(*items[2])
            cur_h = -1
            flat = []
            fi = 0
            pend_fin = []

            def run_side_step(f):
                drain_fin()
                r_ = f()
                if callable(r_):
                    pend_fin.append(r_)

            def drain_fin():
                while pend_fin:
                    pend_fin.pop(0)()
            for idx, (ctx, i) in enumerate(items):
                h = ctx["h"]
                if h != cur_h:
                    while fi < len(flat):
                        drain_fin()
                        run_side_step(flat[fi])
                        fi += 1
                    cur_h = h
                    a_, span, side = side_at[h]
                    pre = [x for x in side if not isinstance(x, tuple)]
                    flat = []
                    for x in side:
                        if isinstance(x, tuple) and x[0] == "prep":
                            hn = x[1]
                            wz_for(hn // 4)
                            flat += prep_steps(hn, hn % 2, wz_of[hn // 4], hn % 4)[1:]
                        elif isinstance(x, tuple) and x[0] == "ldw":
                            hn = x[1]
                            flat.append(prep_steps(hn, hn % 2, 0, hn % 4)[0])
                        else:
                            flat.append(x)
                    fi = 0
                if idx + 3 < len(items):
                    qk(*items[idx + 3])
                pv(ctx, i)
                if i == len(ctx["sched"]) - 1:
                    drain_fin()
                    fin(ctx)
                drain_fin()
                a_, span, _sd = side_at[h]
                tgt = ((idx - a_ + 1) * len(flat)) // span
                while fi < min(tgt, len(flat)):
                    run_side_step(flat[fi])
                    fi += 1
            drain_fin()
            while fi < len(flat):
                run_side_step(flat[fi])
                drain_fin()
                fi += 1
            for f in emit_wout(0, 1, [0, 1, 2], pbank):
                f()

        def emit_final(s):
            for ti in (0, 1, 2):
                off, n = TILES[ti]
                b = nb()
                for k in range(8):
                    si = rot("sq", 3)
                    AC(lambda e, k=k, si=si: e.activation(out=sq[si][:, :n], in_=resid[:, k, off:off + n], func=AF.Square),
                       [("resid", k, ti)], [("sq", si)])
                    PE(lambda e, k=k, si=si: e.matmul(ps[b][:, :n], lhsT=ones[:], rhs=sq[si][:, :n], start=(k == 0), stop=(k == 7)),
                       [("sq", si), "ones"], [("ps", b)])
                ri = 0
                AC(lambda e, ri=ri: e.activation(out=rstd[ri][:, :n], in_=ps[b][:, :n], func=AF.Ln, bias=vcol(C_EPS), scale=1.0 / 1024),
                   [("ps", b), "vecs"], [("rstd", ri)])
                AC(lambda e, ri=ri: e.activation(out=rstd[ri][:, :n], in_=rstd[ri][:, :n], func=AF.Exp, scale=-0.5), [("rstd", ri)], [("rstd", ri)])
                for k in range(8):
                    VE(lambda e, k=k, ri=ri: e.scalar_tensor_tensor(out=resid[:, k, off:off + n], in0=resid[:, k, off:off + n],
                                                                    scalar=vcol(C_FNORM + k), in1=rstd[ri][:, :n], op0=ALU.mult, op1=ALU.mult),
                       [("resid", k, ti), ("rstd", ri), "vecs"], [("resid", k, ti)])
                lo = max(off, HALO)
                t0 = lo - HALO
                DMA("sync", lambda e, t0=t0, lo=lo, off=off, n=n: e.dma_start(out=out_d[s, :, :, t0:t0 + (off + n - lo)], in_=resid[:, :, lo:off + n]),
                    rkeys(ti), ["out"], "out")

        def load_resid(s, src, rkey=None):
            for ti in range(3):
                off, n = TILES[ti]
                DMA("gpsimd", lambda e, off=off, n=n: e.dma_start(out=resid[:, :, off:off + n], in_=src[s, :, :, off:off + n]),
                    [rkey] if rkey else [], [("resid", m, ti) for m in range(8)], "res")

        def store_resid(s, dst, slot):
            for ti in range(3):
                off, n = TILES[ti]
                DMA("sync", lambda e, off=off, n=n: e.dma_start(out=dst[s, :, :, off:off + n], in_=resid[:, :, off:off + n]),
                    [("resid", m, ti) for m in range(8)], [slot], "xst")

        if "p1" in parts:
            for s in range(2):
                load_resid(s, x_in)
                emit_pool(s, 0, [0, 1, 2], True)
                rq = emit_rope_tab(s, 0, 528) + emit_rope_tab(s, 528, 528)
                rq.append(lambda s=s: DMA("sync", lambda e: e.dma_start(out=csd[s], in_=cs_tab[:].rearrange("p a t -> p (a t)")),
                                          [("cs", w_, t_) for w_ in range(2) for t_ in range(3)], ["csd"], "xst"))

                def rope_hook(j, rq=rq, s=s):
                    if s == 1 and j == 8 and "cc" in parts and "p2" in parts:
                        load_latents(0)
                    if j < 1:
                        return
                    k_ = (len(rq) + (14 - j)) // max(1, 15 - j) if j < 15 else len(rq)
                    for _ in range(min(k_, len(rq))):
                        rq.pop(0)()
                emit_conv(s, hooks=rope_hook)
                while rq:
                    rq.pop(0)()
                S.alias(PS_A, PS_B)
                store_resid(s, xpark, "xpark")
                emit_lat(s)
                if debug_out:
                    store_resid(s, dbg, "dbg")
                    DMA("sync", lambda e, s=s: e.dma_start(out=dbg_lat[:, s * 1024:(s + 1) * 1024], in_=gin2[s]), [("gin", s)], ["dbg_lat"], "xst")
                if "cc" in parts:
                    def cc(e, s=s):
                        return e.collective_compute("AllGather", ALU.bypass, replica_groups=[[0, 1], [2, 3], [4, 5], [6, 7]],
                                                    ins=[gin2[s]], outs=[gout2[s]])
                    S.add("gpsimd", cc, reads=[("gin", s)], writes=[("gout", s)], dma_slot="cc%d" % s, inc=1)
        if "p2" in parts:
            for s in range(2):
                load_resid(s, xpark, "xpark")
                S.alias(XK_A, XK_B)
                S.alias(PS_B, PS_A)
                S.alias(WG_A, WG_B)
                emit_attn(s)
                if s == 0:
                    load_latents(1)
                    load_cs(1)
                S.alias(XK_B, XK_A)
                S.alias(WG_B, WG_A)
                emit_pool(s, 1, [0, 1, 2], True)
                emit_final(s)
        fin_reads = ["out"]
        if debug_out:
            fin_reads += ["dbg", "dbg_lat"]
        if "p2" not in parts:
            fin_reads = ["dbg", "dbg_lat"]
        S.add("sync", None, reads=fin_reads)

        def new_sem(name):
            return es.enter_context(nc.semaphore(name))
        S.emit(new_sem)
    return nc, S


def _subshard_ids(rank):
    return (0, 3) if rank == 0 else (1, 2)


def _prep_inputs(inputs):
    x = np.asarray(inputs["x"], dtype=np.float32)
    positions = np.asarray(inputs["positions"], dtype=np.int32)
    B, S_, D = x.shape
    f32 = np.float32

    def chunks(v):
        v = np.asarray(v, dtype=f32)
        return np.ascontiguousarray(v.reshape(-1, 128).T)

    shared = {}
    for k in ("pool_w_in", "pool_w_grp", "pool_w_out", "conv_w_in", "conv_w_out", "mla_w_in", "mla_w_q_up", "mla_w_kv_up", "mla_w_out"):
        shared[k] = np.ascontiguousarray(np.asarray(inputs[k], dtype=f32))
    invf = (10000.0 ** (-np.arange(0, 64, 2, dtype=np.float32) / np.float32(64))).astype(f32)
    kq = np.arange(128)[:, None]
    tri = (np.arange(128)[None, :] >= kq).astype(f32)
    tri_h = (kq <= 96 + np.arange(32)[None, :]).astype(f32)
    in_maps = []
    for core in range(8):
        b, rank = core // 2, core % 2
        gids = _subshard_ids(rank)
        vecs = np.zeros((128, NV), f32)
        vecs[:, 0:8] = chunks(inputs["pool_norm"][0])
        vecs[:, 8:16] = chunks(inputs["pool_norm"][1])
        vecs[:, 16:32] = chunks(inputs["pool_scale"][0])
        vecs[:, 32:48] = chunks(inputs["pool_scale"][1])
        vecs[:, 48:56] = chunks(inputs["conv_norm"][0])
        cw = np.asarray(inputs["conv_w"][0], dtype=f32)
        for j in range(16):
            for i in range(3):
                vecs[:, C_CONVW + j * 3 + i] = cw[i, j * 128:(j + 1) * 128]
        vecs[:, 104:112] = chunks(inputs["mla_norm"][0])
        vecs[:, 112:115] = chunks(inputs["mla_q_norm"][0])
        vecs[:, 115:117] = chunks(inputs["mla_kv_norm"][0])
        vecs[:, 117:125] = chunks(inputs["final_norm"])
        vecs[0:32, C_INVF] = invf
        vecs[32:64, C_INVF] = invf
        vecs[:, C_FLAGA] = NEG if rank == 0 else 0.0
        vecs[:, C_FLAGB] = 0.0 if rank == 0 else NEG
        vecs[0:32, C_SGN] = -1.0
        vecs[32:64, C_SGN] = 1.0
        vecs[:, C_EPS] = EPS
        vecs[:, C_NEGPI] = -np.pi
        vecs[:, C_ZERO] = 0.0
        vecs[:, C_TINY] = 1e-30
        masks = np.zeros((128, 320), f32)
        masks[:, 0:128] = (1.0 - tri) * NEG
        masks[:, 128:160] = (1.0 - tri_h) * NEG
        masks[:, 160:192] = 0.0 if rank == 0 else (1.0 - tri_h) * NEG
        masks[:, 192:320] = np.eye(128, dtype=f32)
        x_in = np.zeros((2, 128, 8, NTOK), f32)
        pos_in = np.zeros((2, NTOK), np.int32)
        rcnt = np.zeros((2, 128, 64), f32)
        for s, g in enumerate(gids):
            t0 = g * OWN - HALO
            lo = max(t0, 0)
            xs = x[b, lo:g * OWN + OWN, :]
            xt = xs.T.reshape(8, 128, -1).transpose(1, 0, 2)
            x_in[s, :, :, lo - t0:] = xt
            pos_in[s, lo - t0:] = positions[b, lo:g * OWN + OWN]
            if lo > t0:
                pos_in[s, :lo - t0] = positions[b, 0]
            for gi in range(4):
                w = 2 << gi
                if g == 0:
                    rcnt[s, :, gi * 16:(gi + 1) * 16] = 1.0 / np.minimum(np.arange(1, 17), w).astype(f32)
                else:
                    rcnt[s, :, gi * 16:(gi + 1) * 16] = 1.0 / w
        m = dict(shared)
        m.update(x_in=x_in, pos_in=pos_in, vecs=vecs, masks=masks, rcnt=rcnt)
        in_maps.append(m)
    return in_maps


_CACHE = {}


def kernel(**inputs):
    in_maps = _prep_inputs(inputs)
    if "nc" not in _CACHE:
        _CACHE["nc"] = build_program()[0]
    nc = _CACHE["nc"]
    res = run_bass_kernel_spmd(nc, in_maps, core_ids=list(range(8)))
    B = 4
    out = np.zeros((B, 4096, 1024), np.float32)
    for core in range(8):
        b, rank = core // 2, core % 2
        o = np.asarray(res.results[core]["out"])
        for s, g in enumerate(_subshard_ids(rank)):
            out[b, g * OWN:(g + 1) * OWN, :] = o[s].transpose(2, 1, 0).reshape(OWN, 1024)
    return out
```

```python
import numpy as np
from contextlib import ExitStack
import concourse.bass as bass
import concourse.mybir as mybir
from concourse.bass_utils import run_bass_kernel_spmd

F32 = mybir.dt.float32
BF16 = mybir.dt.bfloat16
I32 = mybir.dt.int32
ALU = mybir.AluOpType
AF = mybir.ActivationFunctionType

NTOK = 1056
HALO = 32
OWN = 1024
TW = 352
TILES = [(0, TW), (TW, TW), (2 * TW, TW)]
EPS = 1e-6
ATT_SCALE = float(192 ** -0.5)
NEG = -30000.0
NV = 136
C_PNORM = (0, 8)
C_PSCALE = (16, 32)
C_CNORM = 48
C_CONVW = 56
C_MNORM = 104
C_QNORM = 112
C_KVNORM = 115
C_FNORM = 117
C_INVF = 125
C_FLAGA = 126
C_FLAGB = 127
C_SGN = 128
C_EPS = 129
C_NEGPI = 130
C_ZERO = 131
C_TINY = 132
CW1 = 6.28125
CW2 = float(2 * np.pi - 6.28125)
INV2PI = float(1.0 / (2 * np.pi))


def _freeze(fn, depth=0):
    import types
    if not isinstance(fn, types.FunctionType) or fn.__closure__ is None or depth > 4:
        return fn
    cells = []
    for c in fn.__closure__:
        try:
            v = c.cell_contents
        except ValueError:
            cells.append(c)
            continue
        if isinstance(v, types.FunctionType) and v is not fn:
            v = _freeze(v, depth + 1)
        cells.append(types.CellType(v))
    g = types.FunctionType(fn.__code__, fn.__globals__, fn.__name__, fn.__defaults__, tuple(cells))
    g.__kwdefaults__ = fn.__kwdefaults__
    return g


class Sched:
    ENG = ("sync", "gpsimd", "scalar", "vector", "tensor")

    def __init__(self, nc):
        self.nc = nc
        self.ops = {e: [] for e in self.ENG}
        self.last_w = {}
        self.readers = {}
        self.dma_cnt = {}
        self.dma_inc = {}

    def add(self, eng, fn, reads=(), writes=(), dma_slot=None, ndma=1, inc=16):
        idx = len(self.ops[eng])
        fn = _freeze(fn)
        if dma_slot is not None:
            writes = list(writes) + [("slot", dma_slot)]
        deps = set()
        for k in reads:
            d = self.last_w.get(k)
            if d is not None:
                deps.add(d)
        for k in writes:
            d = self.last_w.get(k)
            if d is not None:
                deps.add(d)
            for d in self.readers.get(k, ()):
                deps.add(d)
        if dma_slot is not None:
            n = self.dma_cnt.get(dma_slot, 0) + ndma
            self.dma_cnt[dma_slot] = n
            self.dma_inc[dma_slot] = inc
            me = ("d", dma_slot, n)
        else:
            me = ("e", eng, idx)
        for k in reads:
            lst = self.readers.setdefault(k, [])
            lst[:] = [d for d in lst if not (d[0] == me[0] and d[1] == me[1])]
            lst.append(me)
        for k in writes:
            self.last_w[k] = me
            self.readers[k] = []
        deps.discard(me)
        self.ops[eng].append(dict(fn=fn, deps=deps, me=me, dma=dma_slot, signal=False))
        return me

    def alias(self, old_keys, new_keys):
        deps = set()
        for k in old_keys:
            d = self.last_w.get(k)
            if d is not None:
                deps.add(d)
            for d in self.readers.get(k, ()):
                deps.add(d)
        for k in new_keys:
            lst = self.readers.setdefault(k, [])
            for d in deps:
                if d not in lst:
                    lst.append(d)

    def emit(self, new_sem):
        nc = self.nc
        for eng in self.ENG:
            for idx, op in enumerate(self.ops[eng]):
                best = {}
                for d in op["deps"]:
                    key = (d[0], d[1])
                    if key not in best or best[key][2] < d[2]:
                        best[key] = d
                need = []
                for d in best.values():
                    if d[0] == "e" and d[1] == eng:
                        if eng == "tensor":
                            continue
                        if eng in ("vector", "scalar") and idx - d[2] >= 4:
                            continue
                    need.append(d)
                    if d[0] == "e":
                        self.ops[d[1]][d[2]]["signal"] = True
                op["need"] = need
        cnt = {}
        for eng in self.ENG:
            c = 0
            arr = []
            for op in self.ops[eng]:
                if op["signal"]:
                    c += 1
                arr.append(c)
            cnt[eng] = arr
        esem = {eng: new_sem("e_" + eng) for eng in self.ENG if self.ops[eng]}
        dsem = {slot: new_sem("d_" + str(slot)) for slot in self.dma_cnt}
        self.nsem = len(esem) + len(dsem)
        ops = self.ops

        def make_body(eng):
            def body(e):
                seen = {}
                for op in ops[eng]:
                    for d in op["need"]:
                        if d[0] == "e":
                            sem, val, k = esem[d[1]], cnt[d[1]][d[2]], ("e", d[1])
                        else:
                            sem, val, k = dsem[d[1]], self.dma_inc[d[1]] * d[2], ("d", d[1])
                        if seen.get(k, 0) >= val:
                            continue
                        seen[k] = val
                        e.wait_ge(sem, val)
                    if op["fn"] is None:
                        continue
                    ins = op["fn"](e)
                    if op["dma"] is not None:
                        iv = self.dma_inc[op["dma"]]
                        if isinstance(ins, (list, tuple)):
                            for i_ in ins:
                                i_.then_inc(dsem[op["dma"]], iv)
                        else:
                            ins.then_inc(dsem[op["dma"]], iv)
                    elif op["signal"]:
                        ins.then_inc(esem[eng], 1)
            return body

        with nc.Block() as block:
            for eng in self.ENG:
                if ops[eng]:
                    getattr(block, eng)(make_body(eng))


def build_program(parts=("p1", "cc", "p2"), debug_out=False):
    nc = bass.Bass("TRN2", target_bir_lowering=False)

    def din(name, shape, dt=F32):
        return nc.dram_tensor(name, list(shape), dt, kind="ExternalInput").ap()

    x_in = din("x_in", [2, 128, 8, NTOK])
    pos_in = din("pos_in", [2, NTOK], I32)
    vecs_d = din("vecs", [128, NV])
    masks_d = din("masks", [128, 320])
    rcnt_d = din("rcnt", [2, 128, 64])
    pool_w_in = din("pool_w_in", [2, 1024, 4096])
    pool_w_grp = din("pool_w_grp", [2, 4, 512, 512])
    pool_w_out = din("pool_w_out", [2, 2048, 1024])
    conv_w_in = din("conv_w_in", [1, 1024, 8192])
    conv_w_out = din("conv_w_out", [1, 2048, 1024])
    mla_w_in = din("mla_w_in", [1, 1024, 2752])
    mla_w_q_up = din("mla_w_q_up", [1, 384, 3072])
    mla_w_kv_up = din("mla_w_kv_up", [1, 256, 4096])
    mla_w_out = din("mla_w_out", [1, 2048, 1024])
    out_d = nc.dram_tensor("out", [2, 128, 8, OWN], F32, kind="ExternalOutput").ap()
    xpark = nc.dram_tensor("xpark", [2, 128, 8, NTOK], F32).ap()
    csd = nc.dram_tensor("csd", [2, 64, 2 * NTOK], BF16).ap()
    gin2 = [nc.dram_tensor("gin%d" % i, [320, 1024], BF16).ap() for i in range(2)]
    gout2 = [nc.dram_tensor("gout%d" % i, [640, 1024], BF16).ap() for i in range(2)]
    dbg = None
    if debug_out:
        dbg = nc.dram_tensor("dbg", [2, 128, 8, NTOK], F32, kind="ExternalOutput").ap()
        dbg_lat = nc.dram_tensor("dbg_lat", [320, 2048], BF16, kind="ExternalOutput").ap()

    S = Sched(nc)
    es = ExitStack()
    with es:
        def sb(name, shape, dt):
            return es.enter_context(nc.sbuf_tensor(name, list(shape), dt))

        resid = sb("resid", [128, 8, NTOK], F32)
        xn = sb("xn", [128, 8, NTOK], BF16)
        vecs = sb("vecs_sb", [128, NV], F32)
        masks = sb("masks_sb", [128, 320], BF16)
        ones = sb("ones_sb", [128, 128], BF16)
        rcnt = sb("rcnt_sb", [128, 64], F32)
        wA = [sb("wA%d" % i, [128, 8, 512], BF16) for i in range(2)]
        wGr = [sb("wG%d" % i, [128, 2048], BF16) for i in range(2)]
        wO = [sb("wO%d" % i, [128, 4, 1024], BF16) for i in range(1)]
        sq = [sb("sq%d" % i, [128, TW], BF16) for i in range(3)]
        rstd = [sb("rstd%d" % i, [128, TW], F32) for i in range(1)] * 2
        tmpf = [sb("tmpf%d" % i, [128, TW], F32) for i in range(4)]
        X = sb("X", [128, 8512], F32)
        Y = sb("Y", [128, 2, 4, NTOK], BF16)
        qn = sb("qn", [128, 3, NTOK], BF16)
        kvn = sb("kvn", [128, 2, 4096], BF16)
        kr = sb("kr", [64, 4096], BF16)
        qnope = [sb("qnope%d" % i, [128, NTOK], BF16) for i in range(2)]
        qrope = [sb("qrope%d" % i, [64, NTOK], BF16) for i in range(2)]
        szb = [sb("sz%d" % i, [128, NTOK], BF16) for i in range(2)]
        cs_tab = sb("cs_tab", [64, 2, NTOK], BF16)
        PSr = sb("PSr", [128, 6 * TW], BF16)
        accs = [sb("acc%d" % i, [128, TW], F32) for i in range(1)] * 2
        accq = [sb("accq%d" % i, [128, TW], F32) for i in range(1)] * 2
        posi = sb("posi", [64, TW], I32)
        ps = [es.enter_context(nc.psum_tensor("ps%d" % i, [128, 512], F32)) for i in range(8)]

        Xa = X[:]
        U = [Xa[:, b * 1072:(b + 1) * 1072] for b in range(2)]
        TA = Xa[:, 2144:3216]
        TB = Xa[:, 3216:4288]
        pooled = Xa[:, 4288:8512].bitcast(BF16).rearrange("p (b c t) -> p b c t", b=2, c=4)
        Kb = [Xa[:, b * 2048:(b + 1) * 2048].bitcast(BF16) for b in range(2)]
        Vb = [Xa[:, 4096 + b * 2048:4096 + (b + 1) * 2048].bitcast(BF16).rearrange("p (j d) -> p j d", d=128)
              for b in range(2)]
        wAx = [wA[0], wA[1]] + [Xa[:, 4288 + i * 2048:4288 + (i + 1) * 2048].bitcast(BF16).rearrange("p (k n) -> p k n", k=8)
                                for i in range(2)]
        WX_K = [("wA", 2), ("wA", 3)]
        wG = [w[:].rearrange("p (c n) -> p c n", c=4) for w in wGr]
        wQ = [w[:, 0:768].rearrange("p (c n) -> p c n", c=3) for w in wGr]
        wKV = [w[:, 1024:1536].rearrange("p (c n) -> p c n", c=2) for w in wGr]
        Pt = [PSr[:, i * TW:(i + 1) * TW] for i in range(5)]
        accb = PSr[:, 5 * TW:6 * TW]
        stage = [PSr[:, i * 3 * TW:(i + 1) * 3 * TW].rearrange("p (c n) -> p c n", c=3) for i in range(2)]
        tri = masks[:, 0:128]
        tri_h = masks[:, 128:160]
        mBH1 = masks[:, 160:192]
        ident = masks[:, 192:320]

        XK_A = [("U", 0), ("U", 1), ("TA",), ("TB",)] + [("pooled", p, c) for p in range(2) for c in range(4)]
        XK_B = [(kv_, b_, t_) for kv_ in ("K", "V") for b_ in range(2) for t_ in range(8)]
        PS_A = [("P", i) for i in range(5)] + [("accb",)]
        PS_B = [("stage", i) for i in range(2)]
        WG_A = [("wG", i) for i in range(2)]
        WG_B = [("wQ", i) for i in range(2)] + [("wKV", i) for i in range(2)]

        def vcol(c, rows=128):
            return vecs[0:rows, c:c + 1]

        rr = {}

        def rot(name, n):
            v = rr.get(name, 0)
            rr[name] = (v + 1) % n
            return v

        def nb():
            return rot("bank", 8)

        def VE(fn, r, w):
            S.add("vector", fn, reads=r, writes=w)

        def AC(fn, r, w):
            S.add("scalar", fn, reads=r, writes=w)

        def PO(fn, r, w):
            S.add("gpsimd", fn, reads=r, writes=w)

        def PE(fn, r, w):
            S.add("tensor", fn, reads=r, writes=w)

        def DMA(eng, fn, r, w, slot, n=1):
            S.add(eng, fn, reads=r, writes=w, dma_slot=slot, ndma=n)

        def rkeys(ti):
            return [("resid", m, ti) for m in range(8)]

        def xkeys(ti):
            return [("xn", k, ti) for k in range(8)]

        DMA("sync", lambda e: e.dma_start(out=vecs[:], in_=vecs_d), [], ["vecs"], "c0")
        DMA("gpsimd", lambda e: e.dma_start(out=masks[:], in_=masks_d), [], ["masks"], "c1")
        PO(lambda e: e.memset(ones[:], 1.0), [], ["ones"])
        for (buf_, key_) in ((U[0], ("U", 0)), (U[1], ("U", 1)), (TA, ("TA",)), (TB, ("TB",))):
            PO(lambda e, buf_=buf_: e.memset(buf_[:, 0:16], 0.0), [], [key_])

        def emit_norm(gcol, tis):
            for ti in tis:
                off, n = TILES[ti]
                b = nb()
                for k in range(8):
                    si = rot("sq", 3)
                    if k % 2 == 0:
                        AC(lambda e, k=k, si=si: e.activation(out=sq[si][:, :n], in_=resid[:, k, off:off + n], func=AF.Square),
                           [("resid", k, ti)], [("sq", si)])
                    else:
                        PO(lambda e, k=k, si=si: e.tensor_tensor(out=sq[si][:, :n], in0=resid[:, k, off:off + n],
                                                                 in1=resid[:, k, off:off + n], op=ALU.mult),
                           [("resid", k, ti)], [("sq", si)])
                    PE(lambda e, k=k, si=si: e.matmul(ps[b][:, :n], lhsT=ones[:], rhs=sq[si][:, :n], start=(k == 0), stop=(k == 7)),
                       [("sq", si), "ones"], [("ps", b)])
                ri = 0
                AC(lambda e, ri=ri: e.activation(out=rstd[ri][:, :n], in_=ps[b][:, :n], func=AF.Ln, bias=vcol(C_EPS), scale=1.0 / 1024),
                   [("ps", b), "vecs"], [("rstd", ri)])
                AC(lambda e, ri=ri: e.activation(out=rstd[ri][:, :n], in_=rstd[ri][:, :n], func=AF.Exp, scale=-0.5), [("rstd", ri)], [("rstd", ri)])
                for k in range(8):
                    VE(lambda e, k=k, ri=ri: e.scalar_tensor_tensor(out=xn[:, k, off:off + n], in0=resid[:, k, off:off + n],
                                                                      scalar=vcol(gcol + k), in1=rstd[ri][:, :n],
                                                                      op0=ALU.mult, op1=ALU.mult),
                       [("resid", k, ti), ("rstd", ri), "vecs"], [("xn", k, ti)])

        def load_wA(buf, src3, ncols, col0=0):
            DMA("gpsimd", lambda e: e.dma_start(out=wA[buf][:, :, col0:col0 + ncols], in_=src3), [], [("wA", buf)], "wA%d" % buf)

        def win3(w2d, c0, c1):
            return w2d.rearrange("(k p) n -> p k n", p=128)[:, :, c0:c1]

        pend = []

        def flush_pending():
            while pend:
                for st_ in pend.pop(0):
                    st_()

        def emit_wout(wobuf, ypar, tis, bankfn=None, defer=False):
            bankfn = bankfn or nb
            steps = []
            for m in range(8):
                for ti in tis:
                    def step(m=m, ti=ti):
                        off, n = TILES[ti]
                        b = bankfn()

                        def mm(e):
                            for d in range(4):
                                r = e.matmul(ps[b][:, :n], lhsT=wO[wobuf][:, d, m * 128:(m + 1) * 128],
                                             rhs=Y[:, ypar, d, off:off + n], start=(d == 0), stop=(d == 3))
                            return r
                        PE(mm, [("wO", wobuf)] + [("y", ypar, d, ti) for d in range(4)], [("ps", b)])

                        def wfin():
                            VE(lambda e: e.tensor_tensor(out=resid[:, m, off:off + n], in0=resid[:, m, off:off + n],
                                                         in1=ps[b][:, :n], op=ALU.add),
                               [("ps", b), ("resid", m, ti)], [("resid", m, ti)])
                        if defer:
                            return wfin
                        wfin()
                    steps.append(step)
            return steps

        def load_wO(buf, w2d, r0):
            DMA("gpsimd", lambda e: e.dma_start(out=wO[buf][:], in_=w2d[r0:r0 + 512, :].rearrange("(d p) n -> p d n", p=128)),
                [], [("wO", buf)], "wO%d" % buf)

        def emit_pool(s, j, tis, last, after_norm=None):
            gc = C_PNORM[j]
            sc = C_PSCALE[j]
            w_in = pool_w_in[j]
            w_out = pool_w_out[j]
            wu, wz = 0, 1

            def ld_u(g):
                load_wA(wu, win3(w_in, g * 512, (g + 1) * 512), 512)

            def ld_z(g):
                load_wA(wz, win3(w_in, 2048 + g * 512, 2048 + (g + 1) * 512), 512)

            def ld_g(g):
                wg_ = g % 2
                DMA("gpsimd", lambda e: e.dma_start(out=wG[wg_], in_=pool_w_grp[j, g].rearrange("(c p) n -> p c n", p=128)),
                    [], [("wG", wg_)], "wG%d" % wg_)
            ld_u(0)
            ld_z(0)
            ld_g(0)
            emit_norm(gc, [0, 1, 2])
            if after_norm is not None:
                after_norm()
            DMA("sync", lambda e: e.dma_start(out=rcnt[:], in_=rcnt_d[s]), [], ["rcnt"], "c0")
            for g in range(4):
                wg = g % 2
                ppar = g % 2
                win = 2 << g
                pend_steps = []
                while pend:
                    pend_steps += pend.pop(0)
                for c in range(4):
                    q0, q1 = (c * len(pend_steps)) // 4, ((c + 1) * len(pend_steps)) // 4
                    if c > 0:
                        for st_ in pend_steps[(c - 1) * len(pend_steps) // 4:c * len(pend_steps) // 4]:
                            st_()
                    ub = rot("U", 2)
                    for ti in range(3):
                        off, n = TILES[ti]
                        b = nb()

                        def mm(e, c=c, off=off, n=n, b=b, wu=wu):
                            for k in range(8):
                                r = e.matmul(ps[b][:, :n], lhsT=wA[wu][:, k, c * 128:(c + 1) * 128], rhs=xn[:, k, off:off + n],
                                             start=(k == 0), stop=(k == 7))
                            return r
                        PE(mm, [("wA", wu)] + xkeys(ti), [("ps", b)])
                        AC(lambda e, off=off, n=n, b=b, ub=ub: e.activation(out=U[ub][:, 16 + off:16 + off + n], in_=ps[b][:, :n], func=AF.Copy),
                           [("ps", b)], [("U", ub)])
                    src = U[ub]
                    srck = ("U", ub)
                    sh = 1
                    bufs = [(TA, ("TA",)), (TB, ("TB",))]
                    bi = 0
                    while sh < win:
                        dst, dstk = bufs[bi]
                        (PO if (c % 2 == 1 or (g >= 2 and c > 0)) else VE)(lambda e, src=src, dst=dst, sh=sh: e.tensor_tensor(out=dst[:, 16:16 + NTOK], in0=src[:, 16:16 + NTOK],
                                                                              in1=src[:, 16 - sh:16 - sh + NTOK], op=ALU.add),
                           [srck], [dstk])
                        src, srck = dst, dstk
                        sh *= 2
                        bi ^= 1
                    VE(lambda e, src=src, c=c, ub=ub: e.scalar_tensor_tensor(out=pooled[:, ppar, c, :], in0=src[:, 16:16 + NTOK], scalar=1.0 / win,
                                                                            in1=U[ub][:, 16:16 + NTOK], op0=ALU.mult, op1=ALU.subtract),
                       [srck, ("U", ub)], [("pooled", ppar, c)])
                    t16 = rot("tmpf", 4)
                    VE(lambda e, src=src, t16=t16: e.tensor_tensor(out=tmpf[t16][:, 0:16], in0=src[:, 16 + HALO:16 + HALO + 16],
                                                                   in1=rcnt[:, g * 16:(g + 1) * 16], op=ALU.mult),
                       [srck, "rcnt"], [("tmpf", t16)])
                    VE(lambda e, c=c, ub=ub, t16=t16: e.tensor_tensor(out=pooled[:, ppar, c, HALO:HALO + 16], in0=tmpf[t16][:, 0:16],
                                                                      in1=U[ub][:, 16 + HALO:16 + HALO + 16], op=ALU.subtract),
                       [("tmpf", t16), ("U", ub)], [("pooled", ppar, c)])
                if g + 1 < 4:
                    ld_u(g + 1)
                    ld_g(g + 1)
                for st_ in pend_steps[3 * len(pend_steps) // 4:]:
                    st_()
                flush_pending()
                ypar = rot("ypar", 2)
                load_wO(0, w_out, g * 512)
                for d in range(4):
                    for ti in tis:
                        off, n = TILES[ti]
                        bz = nb()

                        def mmz(e, d=d, off=off, n=n, bz=bz, wz=wz):
                            for k in range(8):
                                r = e.matmul(ps[bz][:, :n], lhsT=wA[wz][:, k, d * 128:(d + 1) * 128], rhs=xn[:, k, off:off + n],
                                             start=(k == 0), stop=(k == 7))
                            return r
                        PE(mmz, [("wA", wz)] + xkeys(ti), [("ps", bz)])
                        AC(lambda e, d=d, off=off, n=n, bz=bz, ypar=ypar: e.activation(out=Y[:, ypar, d, off:off + n], in_=ps[bz][:, :n], func=AF.Silu),
                           [("ps", bz)], [("y", ypar, d, ti)])
                for d in range(4):
                    for ti in tis:
                        off, n = TILES[ti]
                        bm = nb()

                        def mmg(e, d=d, off=off, n=n, bm=bm, wg=wg):
                            for c in range(4):
                                r = e.matmul(ps[bm][:, :n], lhsT=wG[wg][:, c, d * 128:(d + 1) * 128], rhs=pooled[:, ppar, c, off:off + n],
                                             start=(c == 0), stop=(c == 3))
                            return r
                        PE(mmg, [("wG", wg)] + [("pooled", ppar, c) for c in range(4)], [("ps", bm)])
                        VE(lambda e, d=d, off=off, n=n, bm=bm, ypar=ypar: e.scalar_tensor_tensor(
                            out=Y[:, ypar, d, off:off + n], in0=ps[bm][:, :n], scalar=vcol(sc + g * 4 + d), in1=Y[:, ypar, d, off:off + n],
                            op0=ALU.mult, op1=ALU.mult),
                           [("ps", bm), ("y", ypar, d, ti), "vecs"], [("y", ypar, d, ti)])
                if g + 1 < 4:
                    ld_z(g + 1)
                pend.append(emit_wout(0, ypar, tis))
            if last:
                flush_pending()

        def emit_conv(s, hooks=None):
            w_in = conv_w_in[0]
            w_out = conv_w_out[0]
            S.alias([("pooled", p_, c_) for p_ in range(2) for c_ in range(4)], WX_K)
            src = w_in.rearrange("(k p) n -> p k n", p=128)

            def ld_chunk(j):
                wb_ = j % 4

                def ld(e):
                    r = []
                    for q in range(4):
                        r.append(e.dma_start(out=wAx[wb_][:, :, q * 128:(q + 1) * 128],
                                             in_=src[:, :, q * 2048 + j * 128:q * 2048 + (j + 1) * 128]))
                    return r
                DMA("gpsimd", ld, [], [("wA", wb_)], "wA%d" % wb_, n=4)
            for j0 in range(3):
                ld_chunk(j0)
            emit_norm(C_CNORM, [0, 1, 2])
            for jh in range(4):
                ypar = rot("ypar", 2)
                for jj in range(4):
                    j = jh * 4 + jj
                    wb = j % 4
                    if j + 3 < 16:
                        ld_chunk(j + 3)
                    cb = rot("U", 2)
                    CH = U[cb]
                    for ti in range(3):
                        off, n = TILES[ti]
                        bq = []
                        for q in range(4):
                            b = nb()
                            bq.append(b)

                            def mm(e, q=q, off=off, n=n, b=b, wb=wb):
                                for k in range(8):
                                    r = e.matmul(ps[b][:, :n], lhsT=wAx[wb][:, k, q * 128:(q + 1) * 128], rhs=xn[:, k, off:off + n],
                                                 start=(k == 0), stop=(k == 7))
                                return r
                            PE(mm, [("wA", wb)] + xkeys(ti), [("ps", b)])
                        bb, bc, bh, bz = bq
                        tc_ = rot("tmpf", 4)
                        AC(lambda e, n=n, bc=bc, tc_=tc_: e.activation(out=tmpf[tc_][:, :n], in_=ps[bc][:, :n], func=AF.Copy),
                           [("ps", bc)], [("tmpf", tc_)])
                        VE(lambda e, off=off, n=n, bh=bh, tc_=tc_, CH=CH: e.tensor_tensor(out=CH[:, 16 + off:16 + off + n], in0=tmpf[tc_][:, :n],
                                                                                      in1=ps[bh][:, :n], op=ALU.mult),
                           [("ps", bh), ("tmpf", tc_)], [("U", cb)])
                        tz = rot("tmpf", 4)
                        AC(lambda e, n=n, bz=bz, tz=tz: e.activation(out=tmpf[tz][:, :n], in_=ps[bz][:, :n], func=AF.Silu),
                           [("ps", bz)], [("tmpf", tz)])
                        VE(lambda e, n=n, bb=bb, tz=tz: e.tensor_tensor(out=tmpf[tz][:, :n], in0=tmpf[tz][:, :n], in1=ps[bb][:, :n], op=ALU.mult),
                           [("ps", bb), ("tmpf", tz)], [("tmpf", tz)])
                        ta = rot("tmpf", 4)
                        cw = C_CONVW + j * 3
                        AC(lambda e, off=off, n=n, ta=ta, CH=CH, cw=cw: e.activation(out=tmpf[ta][:, :n], in_=CH[:, 16 + off:16 + off + n],
                                                                                 func=AF.Copy, scale=vcol(cw + 2)),
                           [("U", cb), "vecs"], [("tmpf", ta)])
                        VE(lambda e, off=off, n=n, ta=ta, CH=CH, cw=cw: e.scalar_tensor_tensor(
                            out=tmpf[ta][:, :n], in0=CH[:, 15 + off:15 + off + n], scalar=vcol(cw + 1), in1=tmpf[ta][:, :n],
                            op0=ALU.mult, op1=ALU.add), [("U", cb), ("tmpf", ta), "vecs"], [("tmpf", ta)])
                        VE(lambda e, off=off, n=n, ta=ta, CH=CH, cw=cw: e.scalar_tensor_tensor(
                            out=tmpf[ta][:, :n], in0=CH[:, 14 + off:14 + off + n], scalar=vcol(cw + 0), in1=tmpf[ta][:, :n],
                            op0=ALU.mult, op1=ALU.add), [("U", cb), ("tmpf", ta), "vecs"], [("tmpf", ta)])
                        VE(lambda e, off=off, n=n, ta=ta, tz=tz, jj=jj, ypar=ypar: e.tensor_tensor(
                            out=Y[:, ypar, jj, off:off + n], in0=tmpf[ta][:, :n], in1=tmpf[tz][:, :n], op=ALU.mult),
                           [("tmpf", ta), ("tmpf", tz)], [("y", ypar, jj, ti)])
                    if jj == 0:
                        flush_pending()
                        load_wO(0, w_out, jh * 512)
                    if hooks:
                        hooks(j)
                pend.append(emit_wout(0, ypar, [0, 1, 2]))
            flush_pending()
            S.alias(WX_K, XK_A + XK_B)

        def emit_rope_tab(s, c0, ncol):
            ops = []
            T1, T2, T3 = TA[0:64, 0:ncol], TA[0:64, 528:528 + ncol], TB[0:64, 0:ncol]
            k1 = k2 = ("TA",)
            k3 = ("TB",)
            pieces = [(0, min(TW, ncol))] + ([(TW, ncol - TW)] if ncol > TW else [])
            for (o_, n_) in pieces:
                ops.append(lambda o_=o_, n_=n_: DMA("sync", lambda e: e.dma_start(out=posi[:, :n_], in_=pos_in[s:s + 1, c0 + o_:c0 + o_ + n_].partition_broadcast(64)),
                    [], ["posi"], "c0"))
                ops.append(lambda o_=o_, n_=n_: VE(lambda e: e.tensor_copy(out=T1[:, o_:o_ + n_], in_=posi[:, :n_]), ["posi", k1], [k1]))
            ops.append(lambda: VE(lambda e: e.tensor_scalar(out=T1, in0=T1, scalar1=vcol(C_INVF, 64), scalar2=None, op0=ALU.mult), [k1, "vecs"], [k1]))
            for which in range(2):
                ops.append(lambda which=which: VE(lambda e: e.tensor_scalar(out=T2, in0=T1, scalar1=INV2PI, scalar2=(0.25 if which == 0 else 0.0),
                                                                            op0=ALU.mult, op1=ALU.add), [k1], [k2]))
                for (o_, n_) in pieces:
                    ops.append(lambda o_=o_, n_=n_: VE(lambda e: e.tensor_copy(out=posi[:, :n_], in_=T2[:, o_:o_ + n_]), [k2, "posi"], ["posi"]))
                    ops.append(lambda o_=o_, n_=n_: VE(lambda e: e.tensor_copy(out=T2[:, o_:o_ + n_], in_=posi[:, :n_]), ["posi", k2], [k2]))
                ops.append(lambda: VE(lambda e: e.scalar_tensor_tensor(out=T3, in0=T2, scalar=-CW1, in1=T1, op0=ALU.mult, op1=ALU.add), [k1, k2], [k3]))
                ops.append(lambda: VE(lambda e: e.scalar_tensor_tensor(out=T3, in0=T2, scalar=-CW2, in1=T3, op0=ALU.mult, op1=ALU.add), [k2, k3], [k3]))
                if which == 0:
                    ops.append(lambda: VE(lambda e: e.tensor_scalar(out=T3, in0=T3, scalar1=float(np.pi / 2), scalar2=3.1415925, op0=ALU.add, op1=ALU.min), [k3], [k3]))
                    ops.append(lambda: VE(lambda e: e.tensor_scalar(out=T3, in0=T3, scalar1=-3.1415925, scalar2=None, op0=ALU.max), [k3], [k3]))
                else:
                    ops.append(lambda: VE(lambda e: e.tensor_scalar(out=T3, in0=T3, scalar1=3.1415925, scalar2=-3.1415925, op0=ALU.min, op1=ALU.max), [k3], [k3]))
                ops.append(lambda: AC(lambda e: e.activation(out=T3, in_=T3, func=AF.Sin), [k3], [k3]))
                cskeys = [("cs", which, ti) for ti in range(3)]
                if which == 0:
                    ops.append(lambda cskeys=cskeys: VE(lambda e: e.tensor_copy(out=cs_tab[:, 0, c0:c0 + ncol], in_=T3), [k3] + cskeys, cskeys))
                else:
                    ops.append(lambda cskeys=cskeys: VE(lambda e: e.tensor_scalar(out=cs_tab[:, 1, c0:c0 + ncol], in0=T3, scalar1=vcol(C_SGN, 64), scalar2=None, op0=ALU.mult),
                                                        [k3, "vecs"] + cskeys, cskeys))
            return ops

        def emit_rope(dst, ps_x, ps_sw, ti, r, w):
            off, n = TILES[ti]
            t1 = rot("tmpf", 4)
            t2 = rot("tmpf", 4)
            VE(lambda e: e.tensor_tensor(out=tmpf[t1][0:64, :n], in0=ps_x[0:64, :n], in1=cs_tab[:, 0, off:off + n], op=ALU.mult),
               r + [("cs", 0, ti)], [("tmpf", t1)])
            VE(lambda e: e.tensor_tensor(out=tmpf[t2][0:64, :n], in0=ps_sw[0:64, :n], in1=cs_tab[:, 1, off:off + n], op=ALU.mult),
               r + [("cs", 1, ti)], [("tmpf", t2)])
            PO(lambda e: e.tensor_tensor(out=dst, in0=tmpf[t1][0:64, :n], in1=tmpf[t2][0:64, :n], op=ALU.add),
               [("tmpf", t1), ("tmpf", t2)], w)

        def emit_lat(s):
            w_in = mla_w_in[0]
            wb = 0
            src = w_in.rearrange("(k p) n -> p k n", p=128)

            def ld(e):
                return [e.dma_start(out=wA[wb][:, :, 0:320], in_=src[:, :, 384:704]),
                        e.dma_start(out=wA[wb][:, :, 320:352], in_=src[:, :, 672:704]),
                        e.dma_start(out=wA[wb][:, :, 352:384], in_=src[:, :, 640:672])]
            DMA("gpsimd", ld, [], [("wA", wb)], "wA%d" % wb, n=3)
            emit_norm(C_MNORM, [0, 1, 2])
            for ti in (0, 1, 2):
                off, n = TILES[ti]
                b0, b1, b2, b3, b4 = [nb() for _ in range(5)]

                def mm(e, c0, c1, b):
                    for k in range(8):
                        r = e.matmul(ps[b][0:c1 - c0, :n], lhsT=wA[wb][:, k, c0:c1], rhs=xn[:, k, off:off + n], start=(k == 0), stop=(k == 7))
                    return r
                for (c0, c1, b) in ((0, 128, b0), (128, 256, b1), (256, 320, b2), (320, 384, b3)):
                    PE(lambda e, c0=c0, c1=c1, b=b: mm(e, c0, c1, b), [("wA", wb)] + xkeys(ti), [("ps", b)])
                sis = []
                for c, b in ((0, b0), (1, b1)):
                    si = rot("sq", 3)
                    sis.append(si)
                    AC(lambda e, b=b, si=si: e.activation(out=sq[si][:, :n], in_=ps[b][:, :n], func=AF.Square), [("ps", b)], [("sq", si)])
                    PE(lambda e, c=c, si=si: e.matmul(ps[b4][:, :n], lhsT=ones[:], rhs=sq[si][:, :n], start=(c == 0), stop=(c == 1)),
                       [("sq", si), "ones"], [("ps", b4)])
                ri = 0
                AC(lambda e, ri=ri: e.activation(out=rstd[ri][:, :n], in_=ps[b4][:, :n], func=AF.Ln, bias=vcol(C_EPS), scale=1.0 / 256),
                   [("ps", b4), "vecs"], [("rstd", ri)])
                AC(lambda e, ri=ri: e.activation(out=rstd[ri][:, :n], in_=rstd[ri][:, :n], func=AF.Exp, scale=-0.5), [("rstd", ri)], [("rstd", ri)])
                st = rot("stage", 2)
                for c, b in ((0, b0), (1, b1)):
                    VE(lambda e, c=c, b=b, ri=ri, st=st: e.scalar_tensor_tensor(out=stage[st][:, c, :n], in0=ps[b][:, :n], scalar=vcol(C_KVNORM + c),
                                                                                in1=rstd[ri][:, :n], op0=ALU.mult, op1=ALU.mult),
                       [("ps", b), ("rstd", ri), "vecs"], [("stage", st)])
                emit_rope(stage[st][0:64, 2, :n], ps[b2], ps[b3], ti, [("ps", b2), ("ps", b3)], [("stage", st)])
                lo = max(off, HALO) - off
                t0 = off + lo - HALO
                no = n - lo

                def st_dma(e, st=st, t0=t0, n=n, lo=lo, no=no):
                    return [e.dma_start(out=gin2[s][0:256, t0:t0 + no].rearrange("(c p) n -> p c n", p=128), in_=stage[st][:, 0:2, lo:n]),
                            e.dma_start(out=gin2[s][256:320, t0:t0 + no], in_=stage[st][0:64, 2, lo:n])]
                DMA("sync", st_dma, [("stage", st)], [("gin", s)], "gin", n=2)

        lat_done = set()
        cs_done = set()

        def lat_segs(s):
            if s == 0:
                return [("r0A", gout2[0][0:320, :]), ("own", gin2[0][0:320, :])]
            return [("r0A", gout2[0][0:320, :]), ("r1A", gout2[0][320:640, :]), ("r1B", gout2[1][320:640, :]),
                    ("own", gin2[1][0:320, :])]

        def load_latents(s):
            if s in lat_done:
                return
            lat_done.add(s)
            rd = [("gout", 0), ("gin", 0)] if s == 0 else [("gout", 0), ("gout", 1), ("gin", 0), ("gin", 1)]
            for i, (nm, ap_) in enumerate(lat_segs(s)):
                def ld(e, i=i, ap_=ap_):
                    return [e.dma_start(out=kvn[:, :, i * 1024:(i + 1) * 1024], in_=ap_[0:256, :].rearrange("(c p) n -> p c n", p=128)),
                            e.dma_start(out=kr[:, i * 1024:(i + 1) * 1024], in_=ap_[256:320, :])]
                DMA("sync", ld, rd, [("kvn", i), ("kr", i)], "lat", n=2)

        def load_cs(s):
            if s in cs_done:
                return
            cs_done.add(s)
            DMA("sync", lambda e: e.dma_start(out=cs_tab[:].rearrange("p a t -> p (a t)"), in_=csd[s]), ["csd"],
                [("cs", w_, t_) for w_ in range(2) for t_ in range(3)], "lat")

        def emit_attn(s):
            w_in = mla_w_in[0]
            w_out = mla_w_out[0]
            NK = 2048 if s == 0 else 4096
            load_latents(s)
            segs = lat_segs(s)
            nseg = len(segs)
            wb = 0
            load_wA(wb, win3(w_in, 0, 384), 384)
            load_wA(1, win3(w_in, 704, 704 + 512), 512)
            emit_norm(C_MNORM, [0, 1, 2])
            for ti in range(3):
                off, n = TILES[ti]
                bq = [nb() for _ in range(3)]
                b4 = nb()
                for c in range(3):
                    def mm(e, c=c, b=bq[c]):
                        for k in range(8):
                            r = e.matmul(ps[b][:, :n], lhsT=wA[wb][:, k, c * 128:(c + 1) * 128], rhs=xn[:, k, off:off + n], start=(k == 0), stop=(k == 7))
                        return r
                    PE(mm, [("wA", wb)] + xkeys(ti), [("ps", bq[c])])
                for c in range(3):
                    si = rot("sq", 3)
                    AC(lambda e, c=c, si=si: e.activation(out=sq[si][:, :n], in_=ps[bq[c]][:, :n], func=AF.Square), [("ps", bq[c])], [("sq", si)])
                    PE(lambda e, c=c, si=si: e.matmul(ps[b4][:, :n], lhsT=ones[:], rhs=sq[si][:, :n], start=(c == 0), stop=(c == 2)),
                       [("sq", si), "ones"], [("ps", b4)])
                ri = 0
                AC(lambda e, ri=ri: e.activation(out=rstd[ri][:, :n], in_=ps[b4][:, :n], func=AF.Ln, bias=vcol(C_EPS), scale=1.0 / 384),
                   [("ps", b4), "vecs"], [("rstd", ri)])
                AC(lambda e, ri=ri: e.activation(out=rstd[ri][:, :n], in_=rstd[ri][:, :n], func=AF.Exp, scale=-0.5), [("rstd", ri)], [("rstd", ri)])
                for c in range(3):
                    VE(lambda e, c=c, ri=ri: e.scalar_tensor_tensor(out=qn[:, c, off:off + n], in0=ps[bq[c]][:, :n], scalar=vcol(C_QNORM + c),
                                                                    in1=rstd[ri][:, :n], op0=ALU.mult, op1=ALU.mult),
                       [("ps", bq[c]), ("rstd", ri), "vecs"], [("qn", c, ti)])
            qnk = lambda ti: [("qn", c, ti) for c in range(3)]

            B_S = (0, 1, 2, 3)
            B_O = (4, 5)
            B_L = (6,)
            B_P = (6, 7)

            def pbank():
                return B_P[rot("pbank", 2)]

            def prep_steps(h, hb, wzb, hh):
                steps = []
                wq = hb

                def ldw():
                    srcq = mla_w_q_up[0].rearrange("(c p) n -> p c n", p=128)
                    srck = mla_w_kv_up[0].rearrange("(c p) n -> p c n", p=128)

                    def ld(e):
                        return [e.dma_start(out=wQ[wq][:, :, 0:192], in_=srcq[:, :, h * 192:(h + 1) * 192]),
                                e.dma_start(out=wQ[wq][:, :, 192:224], in_=srcq[:, :, h * 192 + 160:h * 192 + 192]),
                                e.dma_start(out=wQ[wq][:, :, 224:256], in_=srcq[:, :, h * 192 + 128:h * 192 + 160]),
                                e.dma_start(out=wKV[wq][:, :, :], in_=srck[:, :, h * 256:(h + 1) * 256])]
                    DMA("gpsimd", ld, [], [("wQ", wq), ("wKV", wq)], "wG%d" % wq, n=4)
                steps.append(ldw)
                for kt in range(NK // 512):
                    def kstep(kt=kt):
                        b = pbank()
                        seg = kt // 2

                        def mm(e):
                            for c in range(2):
                                r = e.matmul(ps[b][:, :], lhsT=wKV[wq][:, c, 0:128], rhs=kvn[:, c, kt * 512:(kt + 1) * 512], start=(c == 0), stop=(c == 1))
                            return r
                        PE(mm, [("wKV", wq), ("kvn", seg)], [("ps", b)])
                        return lambda: VE(lambda e: e.tensor_copy(out=Kb[hb][:, kt * 512:(kt + 1) * 512], in_=ps[b][:, :]), [("ps", b)], [("K", hb, kt)])
                    steps.append(kstep)

                    def vstep(kt=kt):
                        b = pbank()
                        seg = kt // 2

                        def mm(e):
                            for jb in range(4):
                                for c in range(2):
                                    r = e.matmul(ps[b][:, jb * 128:(jb + 1) * 128], lhsT=kvn[:, c, kt * 512 + jb * 128:kt * 512 + (jb + 1) * 128],
                                                 rhs=wKV[wq][:, c, 128:256], start=(c == 0), stop=(c == 1))
                            return r
                        PE(mm, [("wKV", wq), ("kvn", seg)], [("ps", b)])
                        return lambda: VE(lambda e: e.tensor_copy(out=Vb[hb][:, kt * 4:(kt + 1) * 4, :], in_=ps[b][:, :].rearrange("p (j d) -> p j d", d=128)),
                                          [("ps", b)], [("V", hb, kt)])
                    steps.append(vstep)
                for ti in range(3):
                    off, n = TILES[ti]

                    def qstep(ti=ti, off=off, n=n):
                        b = pbank()

                        def mm(e):
                            for c in range(3):
                                r = e.matmul(ps[b][:, :n], lhsT=wQ[wq][:, c, 0:128], rhs=qn[:, c, off:off + n], start=(c == 0), stop=(c == 2))
                            return r
                        PE(mm, [("wQ", wq)] + qnk(ti), [("ps", b)])
                        return lambda: VE(lambda e: e.tensor_copy(out=qnope[hb][:, off:off + n], in_=ps[b][:, :n]), [("ps", b)], [("qnope", hb, ti)])
                    steps.append(qstep)

                    def rstep(ti=ti, off=off, n=n):
                        b1 = pbank()
                        b2 = pbank()

                        def mm1(e):
                            for c in range(3):
                                r = e.matmul(ps[b1][0:64, :n], lhsT=wQ[wq][:, c, 128:192], rhs=qn[:, c, off:off + n], start=(c == 0), stop=(c == 2))
                            return r

                        def mm2(e):
                            for c in range(3):
                                r = e.matmul(ps[b2][0:64, :n], lhsT=wQ[wq][:, c, 192:256], rhs=qn[:, c, off:off + n], start=(c == 0), stop=(c == 2))
                            return r
                        PE(mm1, [("wQ", wq)] + qnk(ti), [("ps", b1)])
                        PE(mm2, [("wQ", wq)] + qnk(ti), [("ps", b2)])
                        return lambda: emit_rope(qrope[hb][:, off:off + n], ps[b1], ps[b2], ti, [("ps", b1), ("ps", b2)], [("qrope", hb, ti)])
                    steps.append(rstep)

                    def zstep(ti=ti, off=off, n=n):
                        b = pbank()

                        def mm(e):
                            for k in range(8):
                                r = e.matmul(ps[b][:, :n], lhsT=wA[wzb][:, k, hh * 128:(hh + 1) * 128], rhs=xn[:, k, off:off + n], start=(k == 0), stop=(k == 7))
                            return r
                        PE(mm, [("wA", wzb)] + xkeys(ti), [("ps", b)])
                        def zfin():
                            tq = rot("tmpf", 4)
                            AC(lambda e: e.activation(out=tmpf[tq][:, :n], in_=ps[b][:, :n], func=AF.Tanh, scale=0.5), [("ps", b)], [("tmpf", tq)])
                            VE(lambda e: e.scalar_tensor_tensor(out=szb[hb][:, off:off + n], in0=tmpf[tq][:, :n], scalar=1.0, in1=ps[b][:, :n],
                                                                op0=ALU.add, op1=ALU.mult), [("tmpf", tq), ("ps", b)], [("sz", hb, ti)])
                        return zfin
                    steps.append(zstep)
                return steps

            own = nseg - 1

            def tile_sched(ti):
                off, n = TILES[ti]
                a0 = off - HALO
                sched = []
                for sg in range(nseg - 1):
                    for jb in range(8):
                        bias = None
                        if s == 0 and sg == 0:
                            bias = C_FLAGA
                        if s == 1 and sg == 2:
                            bias = C_FLAGB
                        mk = []
                        if ti == 0 and jb == 7:
                            if s == 0 and sg == 0:
                                mk.append((0, HALO, tri_h))
                            if s == 1 and sg == 1:
                                mk.append((0, HALO, mBH1))
                            if s == 1 and sg == 2:
                                mk.append((0, HALO, tri_h))
                        sched.append((sg * 8 + jb, 0, n, bias, mk))
                for kb in range(8):
                    delta = 128 * kb - a0
                    if delta >= n:
                        continue
                    if delta >= 0:
                        w = min(128, n - delta)
                        sched.append((own * 8 + kb, delta, n, None, [(delta, w, tri[:, 0:w])]))
                    else:
                        wv = 128 + delta
                        if wv > 0:
                            sched.append((own * 8 + kb, 0, n, None, [(0, wv, tri[:, -delta:128])]))
                        else:
                            sched.append((own * 8 + kb, 0, n, None, []))
                return sched

            items = []
            head_range = {}
            tcount = 0
            for h in range(16):
                st0 = len(items)
                for ti in range(3):
                    off, n = TILES[ti]
                    ctx = dict(h=h, hb=h % 2, hh=h % 4, ypar=(h // 4) % 2, ti=ti, off=off, n=n, sched=tile_sched(ti),
                               bo=B_O[tcount % 2], bl=B_L[0], bs={}, ap=0)
                    tcount += 1
                    for i in range(len(ctx["sched"])):
                        items.append((ctx, i))
                head_range[h] = (st0, len(items))

            def qk(ctx, i):
                kb, c0, c1, bias, mk = ctx["sched"][i]
                hb, off, ti = ctx["hb"], ctx["off"], ctx["ti"]
                bs = B_S[rot("bs", 4)]
                ctx["bs"][i] = bs
                seg = kb // 8

                def mm(e):
                    e.matmul(ps[bs][:, c0:c1], lhsT=Kb[hb][:, kb * 128:(kb + 1) * 128], rhs=qnope[hb][:, off + c0:off + c1], start=True, stop=False)
                    r = e.matmul(ps[bs][:, c0:c1], lhsT=kr[:, kb * 128:(kb + 1) * 128], rhs=qrope[hb][:, off + c0:off + c1], start=False, stop=(len(mk) == 0))
                    for mi, (m0, mw, map_) in enumerate(mk):
                        r = e.matmul(ps[bs][:, m0:m0 + mw], lhsT=ident, rhs=map_, start=False, stop=(mi == len(mk) - 1))
                    return r
                PE(mm, [("K", hb, kb // 4), ("kr", seg), ("qnope", hb, ti), ("qrope", hb, ti), "masks"], [("ps", bs)])

            def pv(ctx, i):
                kb, c0, c1, bias, mk = ctx["sched"][i]
                hb, bo, bl = ctx["hb"], ctx["bo"], ctx["bl"]
                nblk = len(ctx["sched"])
                bs = ctx["bs"][i]
                pi = rot("P", 5)
                bias_ap = vcol(bias) if bias is not None else vcol(C_ZERO)
                AC(lambda e: e.activation(out=Pt[pi][:, c0:c1], in_=ps[bs][:, c0:c1], func=AF.Exp, bias=bias_ap, scale=ATT_SCALE),
                   [("ps", bs), "vecs"], [("P", pi)])

                if i % 3 != 2:
                    EN, acc, akey = VE, accs[ctx["ap"]], ("acc", ctx["ap"])
                else:
                    EN, acc, akey = PO, accq[ctx["ap"]], ("accq", ctx["ap"])
                if i == 0 or i == 2:
                    EN(lambda e: e.tensor_copy(out=acc[:, c0:c1], in_=Pt[pi][:, c0:c1]), [("P", pi)], [akey])
                else:
                    EN(lambda e: e.tensor_tensor(out=acc[:, c0:c1], in0=acc[:, c0:c1], in1=Pt[pi][:, c0:c1], op=ALU.add), [("P", pi), akey], [akey])
                PE(lambda e: e.matmul(ps[bo][:, c0:c1], lhsT=Vb[hb][:, kb, :], rhs=Pt[pi][:, c0:c1], start=(i == 0), stop=(i == nblk - 1)),
                   [("V", hb, kb // 4), ("P", pi)], [("ps", bo)])

            def fin(ctx):
                hb, bo, bl, off, n, ti = ctx["hb"], ctx["bo"], ctx["bl"], ctx["off"], ctx["n"], ctx["ti"]
                ypar, hh = ctx["ypar"], ctx["hh"]
                t1 = rot("tmpf", 4)
                t2 = rot("tmpf", 4)
                acc = accs[ctx["ap"]]
                acq = accq[ctx["ap"]]
                bl = pbank()
                PO(lambda e: e.tensor_tensor(out=accb[:, :n], in0=acc[:, :n], in1=acq[:, :n], op=ALU.add),
                   [("acc", ctx["ap"]), ("accq", ctx["ap"])], [("accb",)])
                PE(lambda e: e.matmul(ps[bl][:, :n], lhsT=ones[:], rhs=accb[:, :n], start=True, stop=True), [("accb",), "ones"], [("ps", bl)])
                VE(lambda e: e.tensor_scalar(out=tmpf[t1][:, :n], in0=ps[bl][:, :n], scalar1=2.0, scalar2=1e-30, op0=ALU.mult, op1=ALU.add),
                   [("ps", bl)], [("tmpf", t1)])
                VE(lambda e: e.reciprocal(out=tmpf[t1][:, :n], in_=tmpf[t1][:, :n]), [("tmpf", t1)], [("tmpf", t1)])
                VE(lambda e: e.tensor_tensor(out=tmpf[t2][:, :n], in0=ps[bo][:, :n], in1=tmpf[t1][:, :n], op=ALU.mult),
                   [("ps", bo), ("tmpf", t1)], [("tmpf", t2)])
                PO(lambda e: e.tensor_tensor(out=Y[:, ypar, hh, off:off + n], in0=tmpf[t2][:, :n], in1=szb[hb][:, off:off + n], op=ALU.mult),
                   [("tmpf", t2), ("sz", hb, ti)], [("y", ypar, hh, ti)])

            wz_of = {0: 1}

            def wz_for(hg):
                if hg not in wz_of:
                    bfi = (1 + hg) % 2
                    load_wA(bfi, win3(w_in, 704 + hg * 512, 704 + (hg + 1) * 512), 512)
                    wz_of[hg] = bfi
                return wz_of[hg]

            load_cs(s)
            for st_ in prep_steps(0, 0, wz_for(0), 0):
                r_ = st_()
                if callable(r_):
                    r_()
            load_wO(0, w_out, 0)
            prep_steps(1, 1, wz_for(0), 1)[0]()
            side_at = {}
            for h in range(16):
                side = []
                if h % 4 == 0 and h > 0:
                    side += emit_wout(0, ((h - 1) // 4) % 2, [0, 1, 2], pbank, defer=True)
                    side.append(lambda h=h: load_wO(0, w_out, (h // 4) * 512))
                if h % 4 == 1 and h // 4 + 1 < 4:
                    side.append(lambda h=h: wz_for(h // 4 + 1))
                if h + 1 < 16:
                    side.append(("prep", h + 1))
                if h + 2 < 16:
                    side.append(("ldw", h + 2))
                a_, b_ = head_range[h]
                span = max(1, int((b_ - a_) * 0.85))
                side_at[h] = (a_, span, side)

            def run_side(h):
                a_, span, side = side_at[h]
                flat = []
                for x in side:
                    if isinstance(x, tuple):
                        hn = x[1]
                        flat += prep_steps(hn, hn % 2, wz_of[hn // 4], hn % 4)
                    else:
                        flat.append(x)
                return flat

            qk(*items[0])
            qk(*items[1])
            qk(*items[2])
            cur_h = -1
            flat = []
            fi = 0
            pend_fin = []

            def run_side_step(f):
                drain_fin()
                r_ = f()
                if callable(r_):
                    pend_fin.append(r_)

            def drain_fin():
                while pend_fin:
                    pend_fin.pop(0)()
            for idx, (ctx, i) in enumerate(items):
                h = ctx["h"]
                if h != cur_h:
                    while fi < len(flat):
                        drain_fin()
                        run_side_step(flat[fi])
                        fi += 1
                    cur_h = h
                    a_, span, side = side_at[h]
                    pre = [x for x in side if not isinstance(x, tuple)]
                    flat = []
                    for x in side:
                        if isinstance(x, tuple) and x[0] == "prep":
                            hn = x[1]
                            wz_for(hn // 4)
                            flat += prep_steps(hn, hn % 2, wz_of[hn // 4], hn % 4)[1:]
                        elif isinstance(x, tuple) and x[0] == "ldw":
                            hn = x[1]
                            flat.append(prep_steps(hn, hn % 2, 0, hn % 4)[0])
                        else:
                            flat.append(x)
                    fi = 0
                if idx + 3 < len(items):
                    qk(*items[idx + 3])
                pv(ctx, i)
                if i == len(ctx["sched"]) - 1:
                    drain_fin()
                    fin(ctx)
                drain_fin()
                a_, span, _sd = side_at[h]
                tgt = ((idx - a_ + 1) * len(flat)) // span
                while fi < min(tgt, len(flat)):
                    run_side_step(flat[fi])
                    fi += 1
            drain_fin()
            while fi < len(flat):
                run_side_step(flat[fi])
                drain_fin()
                fi += 1
            for f in emit_wout(0, 1, [0, 1, 2], pbank):
                f()

        def emit_final(s):
            for ti in (0, 1, 2):
                off, n = TILES[ti]
                b = nb()
                for k in range(8):
                    si = rot("sq", 3)
                    AC(lambda e, k=k, si=si: e.activation(out=sq[si][:, :n], in_=resid[:, k, off:off + n], func=AF.Square),
                       [("resid", k, ti)], [("sq", si)])
                    PE(lambda e, k=k, si=si: e.matmul(ps[b][:, :n], lhsT=ones[:], rhs=sq[si][:, :n], start=(k == 0), stop=(k == 7)),
                       [("sq", si), "ones"], [("ps", b)])
                ri = 0
                AC(lambda e, ri=ri: e.activation(out=rstd[ri][:, :n], in_=ps[b][:, :n], func=AF.Ln, bias=vcol(C_EPS), scale=1.0 / 1024),
                   [("ps", b), "vecs"], [("rstd", ri)])
                AC(lambda e, ri=ri: e.activation(out=rstd[ri][:, :n], in_=rstd[ri][:, :n], func=AF.Exp, scale=-0.5), [("rstd", ri)], [("rstd", ri)])
                for k in range(8):
                    VE(lambda e, k=k, ri=ri: e.scalar_tensor_tensor(out=resid[:, k, off:off + n], in0=resid[:, k, off:off + n],
                                                                    scalar=vcol(C_FNORM + k), in1=rstd[ri][:, :n], op0=ALU.mult, op1=ALU.mult),
                       [("resid", k, ti), ("rstd", ri), "vecs"], [("resid", k, ti)])
                lo = max(off, HALO)
                t0 = lo - HALO
                DMA("sync", lambda e, t0=t0, lo=lo, off=off, n=n: e.dma_start(out=out_d[s, :, :, t0:t0 + (off + n - lo)], in_=resid[:, :, lo:off + n]),
                    rkeys(ti), ["out"], "out")

        def load_resid(s, src, rkey=None):
            for ti in range(3):
                off, n = TILES[ti]
                DMA("gpsimd", lambda e, off=off, n=n: e.dma_start(out=resid[:, :, off:off + n], in_=src[s, :, :, off:off + n]),
                    [rkey] if rkey else [], [("resid", m, ti) for m in range(8)], "res")

        def store_resid(s, dst, slot):
            for ti in range(3):
                off, n = TILES[ti]
                DMA("sync", lambda e, off=off, n=n: e.dma_start(out=dst[s, :, :, off:off + n], in_=resid[:, :, off:off + n]),
                    [("resid", m, ti) for m in range(8)], [slot], "xst")

        if "p1" in parts:
            for s in range(2):
                load_resid(s, x_in)
                emit_pool(s, 0, [0, 1, 2], True)
                rq = emit_rope_tab(s, 0, 528) + emit_rope_tab(s, 528, 528)
                rq.append(lambda s=s: DMA("sync", lambda e: e.dma_start(out=csd[s], in_=cs_tab[:].rearrange("p a t -> p (a t)")),
                                          [("cs", w_, t_) for w_ in range(2) for t_ in range(3)], ["csd"], "xst"))

                def rope_hook(j, rq=rq, s=s):
                    if s == 1 and j == 8 and "cc" in parts and "p2" in parts:
                        load_latents(0)
                    if j < 1:
                        return
                    k_ = (len(rq) + (14 - j)) // max(1, 15 - j) if j < 15 else len(rq)
                    for _ in range(min(k_, len(rq))):
                        rq.pop(0)()
                emit_conv(s, hooks=rope_hook)
                while rq:
                    rq.pop(0)()
                S.alias(PS_A, PS_B)
                store_resid(s, xpark, "xpark")
                emit_lat(s)
                if debug_out:
                    store_resid(s, dbg, "dbg")
                    DMA("sync", lambda e, s=s: e.dma_start(out=dbg_lat[:, s * 1024:(s + 1) * 1024], in_=gin2[s]), [("gin", s)], ["dbg_lat"], "xst")
                if "cc" in parts:
                    def cc(e, s=s):
                        return e.collective_compute("AllGather", ALU.bypass, replica_groups=[[0, 1], [2, 3], [4, 5], [6, 7]],
                                                    ins=[gin2[s]], outs=[gout2[s]])
                    S.add("gpsimd", cc, reads=[("gin", s)], writes=[("gout", s)], dma_slot="cc%d" % s, inc=1)
        if "p2" in parts:
            for s in range(2):
                load_resid(s, xpark, "xpark")
                S.alias(XK_A, XK_B)
                S.alias(PS_B, PS_A)
                S.alias(WG_A, WG_B)
                emit_attn(s)
                if s == 0:
                    load_latents(1)
                    load_cs(1)
                S.alias(XK_B, XK_A)
                S.alias(WG_B, WG_A)
                emit_pool(s, 1, [0, 1, 2], True)
                emit_final(s)
        fin_reads = ["out"]
        if debug_out:
            fin_reads += ["dbg", "dbg_lat"]
        if "p2" not in parts:
            fin_reads = ["dbg", "dbg_lat"]
        S.add("sync", None, reads=fin_reads)

        def new_sem(name):
            return es.enter_context(nc.semaphore(name))
        S.emit(new_sem)
    return nc, S


def _subshard_ids(rank):
    return (0, 3) if rank == 0 else (1, 2)


def _prep_inputs(inputs):
    x = np.asarray(inputs["x"], dtype=np.float32)
    positions = np.asarray(inputs["positions"], dtype=np.int32)
    B, S_, D = x.shape
    f32 = np.float32

    def chunks(v):
        v = np.asarray(v, dtype=f32)
        return np.ascontiguousarray(v.reshape(-1, 128).T)

    shared = {}
    for k in ("pool_w_in", "pool_w_grp", "pool_w_out", "conv_w_in", "conv_w_out", "mla_w_in", "mla_w_q_up", "mla_w_kv_up", "mla_w_out"):
        shared[k] = np.ascontiguousarray(np.asarray(inputs[k], dtype=f32))
    invf = (10000.0 ** (-np.arange(0, 64, 2, dtype=np.float32) / np.float32(64))).astype(f32)
    kq = np.arange(128)[:, None]
    tri = (np.arange(128)[None, :] >= kq).astype(f32)
    tri_h = (kq <= 96 + np.arange(32)[None, :]).astype(f32)
    in_maps = []
    for core in range(8):
        b, rank = core // 2, core % 2
        gids = _subshard_ids(rank)
        vecs = np.zeros((128, NV), f32)
        vecs[:, 0:8] = chunks(inputs["pool_norm"][0])
        vecs[:, 8:16] = chunks(inputs["pool_norm"][1])
        vecs[:, 16:32] = chunks(inputs["pool_scale"][0])
        vecs[:, 32:48] = chunks(inputs["pool_scale"][1])
        vecs[:, 48:56] = chunks(inputs["conv_norm"][0])
        cw = np.asarray(inputs["conv_w"][0], dtype=f32)
        for j in range(16):
            for i in range(3):
                vecs[:, C_CONVW + j * 3 + i] = cw[i, j * 128:(j + 1) * 128]
        vecs[:, 104:112] = chunks(inputs["mla_norm"][0])
        vecs[:, 112:115] = chunks(inputs["mla_q_norm"][0])
        vecs[:, 115:117] = chunks(inputs["mla_kv_norm"][0])
        vecs[:, 117:125] = chunks(inputs["final_norm"])
        vecs[0:32, C_INVF] = invf
        vecs[32:64, C_INVF] = invf
        vecs[:, C_FLAGA] = NEG if rank == 0 else 0.0
        vecs[:, C_FLAGB] = 0.0 if rank == 0 else NEG
        vecs[0:32, C_SGN] = -1.0
        vecs[32:64, C_SGN] = 1.0
        vecs[:, C_EPS] = EPS
        vecs[:, C_NEGPI] = -np.pi
        vecs[:, C_ZERO] = 0.0
        vecs[:, C_TINY] = 1e-30
        masks = np.zeros((128, 320), f32)
        masks[:, 0:128] = (1.0 - tri) * NEG
        masks[:, 128:160] = (1.0 - tri_h) * NEG
        masks[:, 160:192] = 0.0 if rank == 0 else (1.0 - tri_h) * NEG
        masks[:, 192:320] = np.eye(128, dtype=f32)
        x_in = np.zeros((2, 128, 8, NTOK), f32)
        pos_in = np.zeros((2, NTOK), np.int32)
        rcnt = np.zeros((2, 128, 64), f32)
        for s, g in enumerate(gids):
            t0 = g * OWN - HALO
            lo = max(t0, 0)
            xs = x[b, lo:g * OWN + OWN, :]
            xt = xs.T.reshape(8, 128, -1).transpose(1, 0, 2)
            x_in[s, :, :, lo - t0:] = xt
            pos_in[s, lo - t0:] = positions[b, lo:g * OWN + OWN]
            if lo > t0:
                pos_in[s, :lo - t0] = positions[b, 0]
            for gi in range(4):
                w = 2 << gi
                if g == 0:
                    rcnt[s, :, gi * 16:(gi + 1) * 16] = 1.0 / np.minimum(np.arange(1, 17), w).astype(f32)
                else:
                    rcnt[s, :, gi * 16:(gi + 1) * 16] = 1.0 / w
        m = dict(shared)
        m.update(x_in=x_in, pos_in=pos_in, vecs=vecs, masks=masks, rcnt=rcnt)
        in_maps.append(m)
    return in_maps


_CACHE = {}


def kernel(**inputs):
    in_maps = _prep_inputs(inputs)
    if "nc" not in _CACHE:
        _CACHE["nc"] = build_program()[0]
    nc = _CACHE["nc"]
    res = run_bass_kernel_spmd(nc, in_maps, core_ids=list(range(8)))
    B = 4
    out = np.zeros((B, 4096, 1024), np.float32)
    for core in range(8):
        b, rank = core // 2, core % 2
        o = np.asarray(res.results[core]["out"])
        for s, g in enumerate(_subshard_ids(rank)):
            out[b, g * OWN:(g + 1) * OWN, :] = o[s].transpose(2, 1, 0).reshape(OWN, 1024)
    return out
```

```python
import numpy as np
from contextlib import ExitStack
import concourse.bass as bass
import concourse.mybir as mybir
from concourse.bass_utils import run_bass_kernel_spmd

F32 = mybir.dt.float32
BF16 = mybir.dt.bfloat16
I32 = mybir.dt.int32
ALU = mybir.AluOpType
AF = mybir.ActivationFunctionType

NTOK = 1056
HALO = 32
OWN = 1024
TW = 352
TILES = [(0, TW), (TW, TW), (2 * TW, TW)]
EPS = 1e-6
ATT_SCALE = float(192 ** -0.5)
NEG = -30000.0
NV = 136
C_PNORM = (0, 8)
C_PSCALE = (16, 32)
C_CNORM = 48
C_CONVW = 56
C_MNORM = 104
C_QNORM = 112
C_KVNORM = 115
C_FNORM = 117
C_INVF = 125
C_FLAGA = 126
C_FLAGB = 127
C_SGN = 128
C_EPS = 129
C_NEGPI = 130
C_ZERO = 131
C_TINY = 132
CW1 = 6.28125
CW2 = float(2 * np.pi - 6.28125)
INV2PI = float(1.0 / (2 * np.pi))


def _freeze(fn, depth=0):
    import types
    if not isinstance(fn, types.FunctionType) or fn.__closure__ is None or depth > 4:
        return fn
    cells = []
    for c in fn.__closure__:
        try:
            v = c.cell_contents
        except ValueError:
            cells.append(c)
            continue
        if isinstance(v, types.FunctionType) and v is not fn:
            v = _freeze(v, depth + 1)
        cells.append(types.CellType(v))
    g = types.FunctionType(fn.__code__, fn.__globals__, fn.__name__, fn.__defaults__, tuple(cells))
    g.__kwdefaults__ = fn.__kwdefaults__
    return g


class Sched:
    ENG = ("sync", "gpsimd", "scalar", "vector", "tensor")

    def __init__(self, nc):
        self.nc = nc
        self.ops = {e: [] for e in self.ENG}
        self.last_w = {}
        self.readers = {}
        self.dma_cnt = {}
        self.dma_inc = {}

    def add(self, eng, fn, reads=(), writes=(), dma_slot=None, ndma=1, inc=16):
        idx = len(self.ops[eng])
        fn = _freeze(fn)
        if dma_slot is not None:
            writes = list(writes) + [("slot", dma_slot)]
        deps = set()
        for k in reads:
            d = self.last_w.get(k)
            if d is not None:
                deps.add(d)
        for k in writes:
            d = self.last_w.get(k)
            if d is not None:
                deps.add(d)
            for d in self.readers.get(k, ()):
                deps.add(d)
        if dma_slot is not None:
            n = self.dma_cnt.get(dma_slot, 0) + ndma
            self.dma_cnt[dma_slot] = n
            self.dma_inc[dma_slot] = inc
            me = ("d", dma_slot, n)
        else:
            me = ("e", eng, idx)
        for k in reads:
            lst = self.readers.setdefault(k, [])
            lst[:] = [d for d in lst if not (d[0] == me[0] and d[1] == me[1])]
            lst.append(me)
        for k in writes:
            self.last_w[k] = me
            self.readers[k] = []
        deps.discard(me)
        self.ops[eng].append(dict(fn=fn, deps=deps, me=me, dma=dma_slot, signal=False))
        return me

    def alias(self, old_keys, new_keys):
        deps = set()
        for k in old_keys:
            d = self.last_w.get(k)
            if d is not None:
                deps.add(d)
            for d in self.readers.get(k, ()):
                deps.add(d)
        for k in new_keys:
            lst = self.readers.setdefault(k, [])
            for d in deps:
                if d not in lst:
                    lst.append(d)

    def emit(self, new_sem):
        nc = self.nc
        for eng in self.ENG:
            for idx, op in enumerate(self.ops[eng]):
                best = {}
                for d in op["deps"]:
                    key = (d[0], d[1])
                    if key not in best or best[key][2] < d[2]:
                        best[key] = d
                need = []
                for d in best.values():
                    if d[0] == "e" and d[1] == eng:
                        if eng == "tensor":
                            continue
                        if eng in ("vector", "scalar") and idx - d[2] >= 4:
                            continue
                    need.append(d)
                    if d[0] == "e":
                        self.ops[d[1]][d[2]]["signal"] = True
                op["need"] = need
        cnt = {}
        for eng in self.ENG:
            c = 0
            arr = []
            for op in self.ops[eng]:
                if op["signal"]:
                    c += 1
                arr.append(c)
            cnt[eng] = arr
        esem = {eng: new_sem("e_" + eng) for eng in self.ENG if self.ops[eng]}
        dsem = {slot: new_sem("d_" + str(slot)) for slot in self.dma_cnt}
        self.nsem = len(esem) + len(dsem)
        ops = self.ops

        def make_body(eng):
            def body(e):
                seen = {}
                for op in ops[eng]:
                    for d in op["need"]:
                        if d[0] == "e":
                            sem, val, k = esem[d[1]], cnt[d[1]][d[2]], ("e", d[1])
                        else:
                            sem, val, k = dsem[d[1]], self.dma_inc[d[1]] * d[2], ("d", d[1])
                        if seen.get(k, 0) >= val:
                            continue
                        seen[k] = val
                        e.wait_ge(sem, val)
                    if op["fn"] is None:
                        continue
                    ins = op["fn"](e)
                    if op["dma"] is not None:
                        iv = self.dma_inc[op["dma"]]
                        if isinstance(ins, (list, tuple)):
                            for i_ in ins:
                                i_.then_inc(dsem[op["dma"]], iv)
                        else:
                            ins.then_inc(dsem[op["dma"]], iv)
                    elif op["signal"]:
                        ins.then_inc(esem[eng], 1)
            return body

        with nc.Block() as block:
            for eng in self.ENG:
                if ops[eng]:
                    getattr(block, eng)(make_body(eng))


def build_program(parts=("p1", "cc", "p2"), debug_out=False):
    nc = bass.Bass("TRN2", target_bir_lowering=False)

    def din(name, shape, dt=F32):
        return nc.dram_tensor(name, list(shape), dt, kind="ExternalInput").ap()

    x_in = din("x_in", [2, 128, 8, NTOK])
    pos_in = din("pos_in", [2, NTOK], I32)
    vecs_d = din("vecs", [128, NV])
    masks_d = din("masks", [128, 320])
    rcnt_d = din("rcnt", [2, 128, 64])
    pool_w_in = din("pool_w_in", [2, 1024, 4096])
    pool_w_grp = din("pool_w_grp", [2, 4, 512, 512])
    pool_w_out = din("pool_w_out", [2, 2048, 1024])
    conv_w_in = din("conv_w_in", [1, 1024, 8192])
    conv_w_out = din("conv_w_out", [1, 2048, 1024])
    mla_w_in = din("mla_w_in", [1, 1024, 2752])
    mla_w_q_up = din("mla_w_q_up", [1, 384, 3072])
    mla_w_kv_up = din("mla_w_kv_up", [1, 256, 4096])
    mla_w_out = din("mla_w_out", [1, 2048, 1024])
    out_d = nc.dram_tensor("out", [2, 128, 8, OWN], F32, kind="ExternalOutput").ap()
    xpark = nc.dram_tensor("xpark", [2, 128, 8, NTOK], F32).ap()
    csd = nc.dram_tensor("csd", [2, 64, 2 * NTOK], BF16).ap()
    gin2 = [nc.dram_tensor("gin%d" % i, [320, 1024], BF16).ap() for i in range(2)]
    gout2 = [nc.dram_tensor("gout%d" % i, [640, 1024], BF16).ap() for i in range(2)]
    dbg = None
    if debug_out:
        dbg = nc.dram_tensor("dbg", [2, 128, 8, NTOK], F32, kind="ExternalOutput").ap()
        dbg_lat = nc.dram_tensor("dbg_lat", [320, 2048], BF16, kind="ExternalOutput").ap()

    S = Sched(nc)
    es = ExitStack()
    with es:
        def sb(name, shape, dt):
            return es.enter_context(nc.sbuf_tensor(name, list(shape), dt))

        resid = sb("resid", [128, 8, NTOK], F32)
        xn = sb("xn", [128, 8, NTOK], BF16)
        vecs = sb("vecs_sb", [128, NV], F32)
        masks = sb("masks_sb", [128, 320], BF16)
        ones = sb("ones_sb", [128, 128], BF16)
        rcnt = sb("rcnt_sb", [128, 64], F32)
        wA = [sb("wA%d" % i, [128, 8, 512], BF16) for i in range(2)]
        wGr = [sb("wG%d" % i, [128, 2048], BF16) for i in range(2)]
        wO = [sb("wO%d" % i, [128, 4, 1024], BF16) for i in range(1)]
        sq = [sb("sq%d" % i, [128, TW], BF16) for i in range(3)]
        rstd = [sb("rstd%d" % i, [128, TW], F32) for i in range(1)] * 2
        tmpf = [sb("tmpf%d" % i, [128, TW], F32) for i in range(4)]
        X = sb("X", [128, 8512], F32)
        Y = sb("Y", [128, 2, 4, NTOK], BF16)
        qn = sb("qn", [128, 3, NTOK], BF16)
        kvn = sb("kvn", [128, 2, 4096], BF16)
        kr = sb("kr", [64, 4096], BF16)
        qnope = [sb("qnope%d" % i, [128, NTOK], BF16) for i in range(2)]
        qrope = [sb("qrope%d" % i, [64, NTOK], BF16) for i in range(2)]
        szb = [sb("sz%d" % i, [128, NTOK], BF16) for i in range(2)]
        cs_tab = sb("cs_tab", [64, 2, NTOK], BF16)
        PSr = sb("PSr", [128, 6 * TW], BF16)
        accs = [sb("acc%d" % i, [128, TW], F32) for i in range(1)] * 2
        accq = [sb("accq%d" % i, [128, TW], F32) for i in range(1)] * 2
        posi = sb("posi", [64, TW], I32)
        ps = [es.enter_context(nc.psum_tensor("ps%d" % i, [128, 512], F32)) for i in range(8)]

        Xa = X[:]
        U = [Xa[:, b * 1072:(b + 1) * 1072] for b in range(2)]
        TA = Xa[:, 2144:3216]
        TB = Xa[:, 3216:4288]
        pooled = Xa[:, 4288:8512].bitcast(BF16).rearrange("p (b c t) -> p b c t", b=2, c=4)
        Kb = [Xa[:, b * 2048:(b + 1) * 2048].bitcast(BF16) for b in range(2)]
        Vb = [Xa[:, 4096 + b * 2048:4096 + (b + 1) * 2048].bitcast(BF16).rearrange("p (j d) -> p j d", d=128)
              for b in range(2)]
        wAx = [wA[0], wA[1]] + [Xa[:, 4288 + i * 2048:4288 + (i + 1) * 2048].bitcast(BF16).rearrange("p (k n) -> p k n", k=8)
                                for i in range(2)]
        WX_K = [("wA", 2), ("wA", 3)]
        wG = [w[:].rearrange("p (c n) -> p c n", c=4) for w in wGr]
        wQ = [w[:, 0:768].rearrange("p (c n) -> p c n", c=3) for w in wGr]
        wKV = [w[:, 1024:1536].rearrange("p (c n) -> p c n", c=2) for w in wGr]
        Pt = [PSr[:, i * TW:(i + 1) * TW] for i in range(5)]
        accb = PSr[:, 5 * TW:6 * TW]
        stage = [PSr[:, i * 3 * TW:(i + 1) * 3 * TW].rearrange("p (c n) -> p c n", c=3) for i in range(2)]
        tri = masks[:, 0:128]
        tri_h = masks[:, 128:160]
        mBH1 = masks[:, 160:192]
        ident = masks[:, 192:320]

        XK_A = [("U", 0), ("U", 1), ("TA",), ("TB",)] + [("pooled", p, c) for p in range(2) for c in range(4)]
        XK_B = [(kv_, b_, t_) for kv_ in ("K", "V") for b_ in range(2) for t_ in range(8)]
        PS_A = [("P", i) for i in range(5)] + [("accb",)]
        PS_B = [("stage", i) for i in range(2)]
        WG_A = [("wG", i) for i in range(2)]
        WG_B = [("wQ", i) for i in range(2)] + [("wKV", i) for i in range(2)]

        def vcol(c, rows=128):
            return vecs[0:rows, c:c + 1]

        rr = {}

        def rot(name, n):
            v = rr.get(name, 0)
            rr[name] = (v + 1) % n
            return v

        def nb():
            return rot("bank", 8)

        def VE(fn, r, w):
            S.add("vector", fn, reads=r, writes=w)

        def AC(fn, r, w):
            S.add("scalar", fn, reads=r, writes=w)

        def PO(fn, r, w):
            S.add("gpsimd", fn, reads=r, writes=w)

        def PE(fn, r, w):
            S.add("tensor", fn, reads=r, writes=w)

        def DMA(eng, fn, r, w, slot, n=1):
            S.add(eng, fn, reads=r, writes=w, dma_slot=slot, ndma=n)

        def rkeys(ti):
            return [("resid", m, ti) for m in range(8)]

        def xkeys(ti):
            return [("xn", k, ti) for k in range(8)]

        DMA("sync", lambda e: e.dma_start(out=vecs[:], in_=vecs_d), [], ["vecs"], "c0")
        DMA("gpsimd", lambda e: e.dma_start(out=masks[:], in_=masks_d), [], ["masks"], "c1")
        PO(lambda e: e.memset(ones[:], 1.0), [], ["ones"])
        for (buf_, key_) in ((U[0], ("U", 0)), (U[1], ("U", 1)), (TA, ("TA",)), (TB, ("TB",))):
            PO(lambda e, buf_=buf_: e.memset(buf_[:, 0:16], 0.0), [], [key_])

        def emit_norm(gcol, tis):
            for ti in tis:
                off, n = TILES[ti]
                b = nb()
                for k in range(8):
                    si = rot("sq", 3)
                    if k % 2 == 0:
                        AC(lambda e, k=k, si=si: e.activation(out=sq[si][:, :n], in_=resid[:, k, off:off + n], func=AF.Square),
                           [("resid", k, ti)], [("sq", si)])
                    else:
                        PO(lambda e, k=k, si=si: e.tensor_tensor(out=sq[si][:, :n], in0=resid[:, k, off:off + n],
                                                                 in1=resid[:, k, off:off + n], op=ALU.mult),
                           [("resid", k, ti)], [("sq", si)])
                    PE(lambda e, k=k, si=si: e.matmul(ps[b][:, :n], lhsT=ones[:], rhs=sq[si][:, :n], start=(k == 0), stop=(k == 7)),
                       [("sq", si), "ones"], [("ps", b)])
                ri = 0
                AC(lambda e, ri=ri: e.activation(out=rstd[ri][:, :n], in_=ps[b][:, :n], func=AF.Ln, bias=vcol(C_EPS), scale=1.0 / 1024),
                   [("ps", b), "vecs"], [("rstd", ri)])
                AC(lambda e, ri=ri: e.activation(out=rstd[ri][:, :n], in_=rstd[ri][:, :n], func=AF.Exp, scale=-0.5), [("rstd", ri)], [("rstd", ri)])
                for k in range(8):
                    VE(lambda e, k=k, ri=ri: e.scalar_tensor_tensor(out=xn[:, k, off:off + n], in0=resid[:, k, off:off + n],
                                                                      scalar=vcol(gcol + k), in1=rstd[ri][:, :n],
                                                                      op0=ALU.mult, op1=ALU.mult),
                       [("resid", k, ti), ("rstd", ri), "vecs"], [("xn", k, ti)])

        def load_wA(buf, src3, ncols, col0=0):
            DMA("gpsimd", lambda e: e.dma_start(out=wA[buf][:, :, col0:col0 + ncols], in_=src3), [], [("wA", buf)], "wA%d" % buf)

        def win3(w2d, c0, c1):
            return w2d.rearrange("(k p) n -> p k n", p=128)[:, :, c0:c1]

        pend = []

        def flush_pending():
            while pend:
                for st_ in pend.pop(0):
                    st_()

        def emit_wout(wobuf, ypar, tis, bankfn=None, defer=False):
            bankfn = bankfn or nb
            steps = []
            for m in range(8):
                for ti in tis:
                    def step(m=m, ti=ti):
                        off, n = TILES[ti]
                        b = bankfn()

                        def mm(e):
                            for d in range(4):
                                r = e.matmul(ps[b][:, :n], lhsT=wO[wobuf][:, d, m * 128:(m + 1) * 128],
                                             rhs=Y[:, ypar, d, off:off + n], start=(d == 0), stop=(d == 3))
                            return r
                        PE(mm, [("wO", wobuf)] + [("y", ypar, d, ti) for d in range(4)], [("ps", b)])

                        def wfin():
                            VE(lambda e: e.tensor_tensor(out=resid[:, m, off:off + n], in0=resid[:, m, off:off + n],
                                                         in1=ps[b][:, :n], op=ALU.add),
                               [("ps", b), ("resid", m, ti)], [("resid", m, ti)])
                        if defer:
                            return wfin
                        wfin()
                    steps.append(step)
            return steps

        def load_wO(buf, w2d, r0):
            DMA("gpsimd", lambda e: e.dma_start(out=wO[buf][:], in_=w2d[r0:r0 + 512, :].rearrange("(d p) n -> p d n", p=128)),
                [], [("wO", buf)], "wO%d" % buf)

        def emit_pool(s, j, tis, last, after_norm=None):
            gc = C_PNORM[j]
            sc = C_PSCALE[j]
            w_in = pool_w_in[j]
            w_out = pool_w_out[j]
            wu, wz = 0, 1

            def ld_u(g):
                load_wA(wu, win3(w_in, g * 512, (g + 1) * 512), 512)

            def ld_z(g):
                load_wA(wz, win3(w_in, 2048 + g * 512, 2048 + (g + 1) * 512), 512)

            def ld_g(g):
                wg_ = g % 2
                DMA("gpsimd", lambda e: e.dma_start(out=wG[wg_], in_=pool_w_grp[j, g].rearrange("(c p) n -> p c n", p=128)),
                    [], [("wG", wg_)], "wG%d" % wg_)
            ld_u(0)
            ld_z(0)
            ld_g(0)
            emit_norm(gc, [0, 1, 2])
            if after_norm is not None:
                after_norm()
            DMA("sync", lambda e: e.dma_start(out=rcnt[:], in_=rcnt_d[s]), [], ["rcnt"], "c0")
            for g in range(4):
                wg = g % 2
                ppar = g % 2
                win = 2 << g
                pend_steps = []
                while pend:
                    pend_steps += pend.pop(0)
                for c in range(4):
                    q0, q1 = (c * len(pend_steps)) // 4, ((c + 1) * len(pend_steps)) // 4
                    if c > 0:
                        for st_ in pend_steps[(c - 1) * len(pend_steps) // 4:c * len(pend_steps) // 4]:
                            st_()
                    ub = rot("U", 2)
                    for ti in range(3):
                        off, n = TILES[ti]
                        b = nb()

                        def mm(e, c=c, off=off, n=n, b=b, wu=wu):
                            for k in range(8):
                                r = e.matmul(ps[b][:, :n], lhsT=wA[wu][:, k, c * 128:(c + 1) * 128], rhs=xn[:, k, off:off + n],
                                             start=(k == 0), stop=(k == 7))
                            return r
                        PE(mm, [("wA", wu)] + xkeys(ti), [("ps", b)])
                        AC(lambda e, off=off, n=n, b=b, ub=ub: e.activation(out=U[ub][:, 16 + off:16 + off + n], in_=ps[b][:, :n], func=AF.Copy),
                           [("ps", b)], [("U", ub)])
                    src = U[ub]
                    srck = ("U", ub)
                    sh = 1
                    bufs = [(TA, ("TA",)), (TB, ("TB",))]
                    bi = 0
                    while sh < win:
                        dst, dstk = bufs[bi]
                        (PO if (c % 2 == 1 or (g >= 2 and c > 0)) else VE)(lambda e, src=src, dst=dst, sh=sh: e.tensor_tensor(out=dst[:, 16:16 + NTOK], in0=src[:, 16:16 + NTOK],
                                                                              in1=src[:, 16 - sh:16 - sh + NTOK], op=ALU.add),
                           [srck], [dstk])
                        src, srck = dst, dstk
                        sh *= 2
                        bi ^= 1
                    VE(lambda e, src=src, c=c, ub=ub: e.scalar_tensor_tensor(out=pooled[:, ppar, c, :], in0=src[:, 16:16 + NTOK], scalar=1.0 / win,
                                                                            in1=U[ub][:, 16:16 + NTOK], op0=ALU.mult, op1=ALU.subtract),
                       [srck, ("U", ub)], [("pooled", ppar, c)])
                    t16 = rot("tmpf", 4)
                    VE(lambda e, src=src, t16=t16: e.tensor_tensor(out=tmpf[t16][:, 0:16], in0=src[:, 16 + HALO:16 + HALO + 16],
                                                                   in1=rcnt[:, g * 16:(g + 1) * 16], op=ALU.mult),
                       [srck, "rcnt"], [("tmpf", t16)])
                    VE(lambda e, c=c, ub=ub, t16=t16: e.tensor_tensor(out=pooled[:, ppar, c, HALO:HALO + 16], in0=tmpf[t16][:, 0:16],
                                                                      in1=U[ub][:, 16 + HALO:16 + HALO + 16], op=ALU.subtract),
                       [("tmpf", t16), ("U", ub)], [("pooled", ppar, c)])
                if g + 1 < 4:
                    ld_u(g + 1)
                    ld_g(g + 1)
                for st_ in pend_steps[3 * len(pend_steps) // 4:]:
                    st_()
                flush_pending()
                ypar = rot("ypar", 2)
                load_wO(0, w_out, g * 512)
                for d in range(4):
                    for ti in tis:
                        off, n = TILES[ti]
                        bz = nb()

                        def mmz(e, d=d, off=off, n=n, bz=bz, wz=wz):
                            for k in range(8):
                                r = e.matmul(ps[bz][:, :n], lhsT=wA[wz][:, k, d * 128:(d + 1) * 128], rhs=xn[:, k, off:off + n],
                                             start=(k == 0), stop=(k == 7))
                            return r
                        PE(mmz, [("wA", wz)] + xkeys(ti), [("ps", bz)])
                        AC(lambda e, d=d, off=off, n=n, bz=bz, ypar=ypar: e.activation(out=Y[:, ypar, d, off:off + n], in_=ps[bz][:, :n], func=AF.Silu),
                           [("ps", bz)], [("y", ypar, d, ti)])
                for d in range(4):
                    for ti in tis:
                        off, n = TILES[ti]
                        bm = nb()

                        def mmg(e, d=d, off=off, n=n, bm=bm, wg=wg):
                            for c in range(4):
                                r = e.matmul(ps[bm][:, :n], lhsT=wG[wg][:, c, d * 128:(d + 1) * 128], rhs=pooled[:, ppar, c, off:off + n],
                                             start=(c == 0), stop=(c == 3))
                            return r
                        PE(mmg, [("wG", wg)] + [("pooled", ppar, c) for c in range(4)], [("ps", bm)])
                        VE(lambda e, d=d, off=off, n=n, bm=bm, ypar=ypar: e.scalar_tensor_tensor(
                            out=Y[:, ypar, d, off:off + n], in0=ps[bm][:, :n], scalar=vcol(sc + g * 4 + d), in1=Y[:, ypar, d, off:off + n],
                            op0=ALU.mult, op1=ALU.mult),
                           [("ps", bm), ("y", ypar, d, ti), "vecs"], [("y", ypar, d, ti)])
                if g + 1 < 4:
                    ld_z(g + 1)
                pend.append(emit_wout(0, ypar, tis))
            if last:
                flush_pending()

        def emit_conv(s, hooks=None):
            w_in = conv_w_in[0]
            w_out = conv_w_out[0]
            S.alias([("pooled", p_, c_) for p_ in range(2) for c_ in range(4)], WX_K)
            src = w_in.rearrange("(k p) n -> p k n", p=128)

            def ld_chunk(j):
                wb_ = j % 4

                def ld(e):
                    r = []
                    for q in range(4):
                        r.append(e.dma_start(out=wAx[wb_][:, :, q * 128:(q + 1) * 128],
                                             in_=src[:, :, q * 2048 + j * 128:q * 2048 + (j + 1) * 128]))
                    return r
                DMA("gpsimd", ld, [], [("wA", wb_)], "wA%d" % wb_, n=4)
            for j0 in range(3):
                ld_chunk(j0)
            emit_norm(C_CNORM, [0, 1, 2])
            for jh in range(4):
                ypar = rot("ypar", 2)
                for jj in range(4):
                    j = jh * 4 + jj
                    wb = j % 4
                    if j + 3 < 16:
                        ld_chunk(j + 3)
                    cb = rot("U", 2)
                    CH = U[cb]
                    for ti in range(3):
                        off, n = TILES[ti]
                        bq = []
                        for q in range(4):
                            b = nb()
                            bq.append(b)

                            def mm(e, q=q, off=off, n=n, b=b, wb=wb):
                                for k in range(8):
                                    r = e.matmul(ps[b][:, :n], lhsT=wAx[wb][:, k, q * 128:(q + 1) * 128], rhs=xn[:, k, off:off + n],
                                                 start=(k == 0), stop=(k == 7))
                                return r
                            PE(mm, [("wA", wb)] + xkeys(ti), [("ps", b)])
                        bb, bc, bh, bz = bq
                        tc_ = rot("tmpf", 4)
                        AC(lambda e, n=n, bc=bc, tc_=tc_: e.activation(out=tmpf[tc_][:, :n], in_=ps[bc][:, :n], func=AF.Copy),
                           [("ps", bc)], [("tmpf", tc_)])
                        VE(lambda e, off=off, n=n, bh=bh, tc_=tc_, CH=CH: e.tensor_tensor(out=CH[:, 16 + off:16 + off + n], in0=tmpf[tc_][:, :n],
                                                                                      in1=ps[bh][:, :n], op=ALU.mult),
                           [("ps", bh), ("tmpf", tc_)], [("U", cb)])
                        tz = rot("tmpf", 4)
                        AC(lambda e, n=n, bz=bz, tz=tz: e.activation(out=tmpf[tz][:, :n], in_=ps[bz][:, :n], func=AF.Silu),
                           [("ps", bz)], [("tmpf", tz)])
                        VE(lambda e, n=n, bb=bb, tz=tz: e.tensor_tensor(out=tmpf[tz][:, :n], in0=tmpf[tz][:, :n], in1=ps[bb][:, :n], op=ALU.mult),
                           [("ps", bb), ("tmpf", tz)], [("tmpf", tz)])
                        ta = rot("tmpf", 4)
                        cw = C_CONVW + j * 3
                        AC(lambda e, off=off, n=n, ta=ta, CH=CH, cw=cw: e.activation(out=tmpf[ta][:, :n], in_=CH[:, 16 + off:16 + off + n],
                                                                                 func=AF.Copy, scale=vcol(cw + 2)),
                           [("U", cb), "vecs"], [("tmpf", ta)])
                        VE(lambda e, off=off, n=n, ta=ta, CH=CH, cw=cw: e.scalar_tensor_tensor(
                            out=tmpf[ta][:, :n], in0=CH[:, 15 + off:15 + off + n], scalar=vcol(cw + 1), in1=tmpf[ta][:, :n],
                            op0=ALU.mult, op1=ALU.add), [("U", cb), ("tmpf", ta), "vecs"], [("tmpf", ta)])
                        VE(lambda e, off=off, n=n, ta=ta, CH=CH, cw=cw: e.scalar_tensor_tensor(
                            out=tmpf[ta][:, :n], in0=CH[:, 14 + off:14 + off + n], scalar=vcol(cw + 0), in1=tmpf[ta][:, :n],
                            op0=ALU.mult, op1=ALU.add), [("U", cb), ("tmpf", ta), "vecs"], [("tmpf", ta)])
                        VE(lambda e, off=off, n=n, ta=ta, tz=tz, jj=jj, ypar=ypar: e.tensor_tensor(
                            out=Y[:, ypar, jj, off:off + n], in0=tmpf[ta][:, :n], in1=tmpf[tz][:, :n], op=ALU.mult),
                           [("tmpf", ta), ("tmpf", tz)], [("y", ypar, jj, ti)])
                    if jj == 0:
                        flush_pending()
                        load_wO(0, w_out, jh * 512)
                    if hooks:
                        hooks(j)
                pend.append(emit_wout(0, ypar, [0, 1, 2]))
            flush_pending()
            S.alias(WX_K, XK_A + XK_B)

        def emit_rope_tab(s, c0, ncol):
            ops = []
            T1, T2, T3 = TA[0:64, 0:ncol], TA[0:64, 528:528 + ncol], TB[0:64, 0:ncol]
            k1 = k2 = ("TA",)
            k3 = ("TB",)
            pieces = [(0, min(TW, ncol))] + ([(TW, ncol - TW)] if ncol > TW else [])
            for (o_, n_) in pieces:
                ops.append(lambda o_=o_, n_=n_: DMA("sync", lambda e: e.dma_start(out=posi[:, :n_], in_=pos_in[s:s + 1, c0 + o_:c0 + o_ + n_].partition_broadcast(64)),
                    [], ["posi"], "c0"))
                ops.append(lambda o_=o_, n_=n_: VE(lambda e: e.tensor_copy(out=T1[:, o_:o_ + n_], in_=posi[:, :n_]), ["posi", k1], [k1]))
            ops.append(lambda: VE(lambda e: e.tensor_scalar(out=T1, in0=T1, scalar1=vcol(C_INVF, 64), scalar2=None, op0=ALU.mult), [k1, "vecs"], [k1]))
            for which in range(2):
                ops.append(lambda which=which: VE(lambda e: e.tensor_scalar(out=T2, in0=T1, scalar1=INV2PI, scalar2=(0.25 if which == 0 else 0.0),
                                                                            op0=ALU.mult, op1=ALU.add), [k1], [k2]))
                for (o_, n_) in pieces:
                    ops.append(lambda o_=o_, n_=n_: VE(lambda e: e.tensor_copy(out=posi[:, :n_], in_=T2[:, o_:o_ + n_]), [k2, "posi"], ["posi"]))
                    ops.append(lambda o_=o_, n_=n_: VE(lambda e: e.tensor_copy(out=T2[:, o_:o_ + n_], in_=posi[:, :n_]), ["posi", k2], [k2]))
                ops.append(lambda: VE(lambda e: e.scalar_tensor_tensor(out=T3, in0=T2, scalar=-CW1, in1=T1, op0=ALU.mult, op1=ALU.add), [k1, k2], [k3]))
                ops.append(lambda: VE(lambda e: e.scalar_tensor_tensor(out=T3, in0=T2, scalar=-CW2, in1=T3, op0=ALU.mult, op1=ALU.add), [k2, k3], [k3]))
                if which == 0:
                    ops.append(lambda: VE(lambda e: e.tensor_scalar(out=T3, in0=T3, scalar1=float(np.pi / 2), scalar2=3.1415925, op0=ALU.add, op1=ALU.min), [k3], [k3]))
                    ops.append(lambda: VE(lambda e: e.tensor_scalar(out=T3, in0=T3, scalar1=-3.1415925, scalar2=None, op0=ALU.max), [k3], [k3]))
                else:
                    ops.append(lambda: VE(lambda e: e.tensor_scalar(out=T3, in0=T3, scalar1=3.1415925, scalar2=-3.1415925, op0=ALU.min, op1=ALU.max), [k3], [k3]))
                ops.append(lambda: AC(lambda e: e.activation(out=T3, in_=T3, func=AF.Sin), [k3], [k3]))
                cskeys = [("cs", which, ti) for ti in range(3)]
                if which == 0:
                    ops.append(lambda cskeys=cskeys: VE(lambda e: e.tensor_copy(out=cs_tab[:, 0, c0:c0 + ncol], in_=T3), [k3] + cskeys, cskeys))
                else:
                    ops.append(lambda cskeys=cskeys: VE(lambda e: e.tensor_scalar(out=cs_tab[:, 1, c0:c0 + ncol], in0=T3, scalar1=vcol(C_SGN, 64), scalar2=None, op0=ALU.mult),
                                                        [k3, "vecs"] + cskeys, cskeys))
            return ops

        def emit_rope(dst, ps_x, ps_sw, ti, r, w):
            off, n = TILES[ti]
            t1 = rot("tmpf", 4)
            t2 = rot("tmpf", 4)
            VE(lambda e: e.tensor_tensor(out=tmpf[t1][0:64, :n], in0=ps_x[0:64, :n], in1=cs_tab[:, 0, off:off + n], op=ALU.mult),
               r + [("cs", 0, ti)], [("tmpf", t1)])
            VE(lambda e: e.tensor_tensor(out=tmpf[t2][0:64, :n], in0=ps_sw[0:64, :n], in1=cs_tab[:, 1, off:off + n], op=ALU.mult),
               r + [("cs", 1, ti)], [("tmpf", t2)])
            PO(lambda e: e.tensor_tensor(out=dst, in0=tmpf[t1][0:64, :n], in1=tmpf[t2][0:64, :n], op=ALU.add),
               [("tmpf", t1), ("tmpf", t2)], w)

        def emit_lat(s):
            w_in = mla_w_in[0]
            wb = 0
            src = w_in.rearrange("(k p) n -> p k n", p=128)

            def ld(e):
                return [e.dma_start(out=wA[wb][:, :, 0:320], in_=src[:, :, 384:704]),
                        e.dma_start(out=wA[wb][:, :, 320:352], in_=src[:, :, 672:704]),
                        e.dma_start(out=wA[wb][:, :, 352:384], in_=src[:, :, 640:672])]
            DMA("gpsimd", ld, [], [("wA", wb)], "wA%d" % wb, n=3)
            emit_norm(C_MNORM, [0, 1, 2])
            for ti in (0, 1, 2):
                off, n = TILES[ti]
                b0, b1, b2, b3, b4 = [nb() for _ in range(5)]

                def mm(e, c0, c1, b):
                    for k in range(8):
                        r = e.matmul(ps[b][0:c1 - c0, :n], lhsT=wA[wb][:, k, c0:c1], rhs=xn[:, k, off:off + n], start=(k == 0), stop=(k == 7))
                    return r
                for (c0, c1, b) in ((0, 128, b0), (128, 256, b1), (256, 320, b2), (320, 384, b3)):
                    PE(lambda e, c0=c0, c1=c1, b=b: mm(e, c0, c1, b), [("wA", wb)] + xkeys(ti), [("ps", b)])
                sis = []
                for c, b in ((0, b0), (1, b1)):
                    si = rot("sq", 3)
                    sis.append(si)
                    AC(lambda e, b=b, si=si: e.activation(out=sq[si][:, :n], in_=ps[b][:, :n], func=AF.Square), [("ps", b)], [("sq", si)])
                    PE(lambda e, c=c, si=si: e.matmul(ps[b4][:, :n], lhsT=ones[:], rhs=sq[si][:, :n], start=(c == 0), stop=(c == 1)),
                       [("sq", si), "ones"], [("ps", b4)])
                ri = 0
                AC(lambda e, ri=ri: e.activation(out=rstd[ri][:, :n], in_=ps[b4][:, :n], func=AF.Ln, bias=vcol(C_EPS), scale=1.0 / 256),
                   [("ps", b4), "vecs"], [("rstd", ri)])
                AC(lambda e, ri=ri: e.activation(out=rstd[ri][:, :n], in_=rstd[ri][:, :n], func=AF.Exp, scale=-0.5), [("rstd", ri)], [("rstd", ri)])
                st = rot("stage", 2)
                for c, b in ((0, b0), (1, b1)):
                    VE(lambda e, c=c, b=b, ri=ri, st=st: e.scalar_tensor_tensor(out=stage[st][:, c, :n], in0=ps[b][:, :n], scalar=vcol(C_KVNORM + c),
                                                                                in1=rstd[ri][:, :n], op0=ALU.mult, op1=ALU.mult),
                       [("ps", b), ("rstd", ri), "vecs"], [("stage", st)])
                emit_rope(stage[st][0:64, 2, :n], ps[b2], ps[b3], ti, [("ps", b2), ("ps", b3)], [("stage", st)])
                lo = max(off, HALO) - off
                t0 = off + lo - HALO
                no = n - lo

                def st_dma(e, st=st, t0=t0, n=n, lo=lo, no=no):
                    return [e.dma_start(out=gin2[s][0:256, t0:t0 + no].rearrange("(c p) n -> p c n", p=128), in_=stage[st][:, 0:2, lo:n]),
                            e.dma_start(out=gin2[s][256:320, t0:t0 + no], in_=stage[st][0:64, 2, lo:n])]
                DMA("sync", st_dma, [("stage", st)], [("gin", s)], "gin", n=2)

        lat_done = set()
        cs_done = set()

        def lat_segs(s):
            if s == 0:
                return [("r0A", gout2[0][0:320, :]), ("own", gin2[0][0:320, :])]
            return [("r0A", gout2[0][0:320, :]), ("r1A", gout2[0][320:640, :]), ("r1B", gout2[1][320:640, :]),
                    ("own", gin2[1][0:320, :])]

        def load_latents(s):
            if s in lat_done:
                return
            lat_done.add(s)
            rd = [("gout", 0), ("gin", 0)] if s == 0 else [("gout", 0), ("gout", 1), ("gin", 0), ("gin", 1)]
            for i, (nm, ap_) in enumerate(lat_segs(s)):
                def ld(e, i=i, ap_=ap_):
                    return [e.dma_start(out=kvn[:, :, i * 1024:(i + 1) * 1024], in_=ap_[0:256, :].rearrange("(c p) n -> p c n", p=128)),
                            e.dma_start(out=kr[:, i * 1024:(i + 1) * 1024], in_=ap_[256:320, :])]
                DMA("sync", ld, rd, [("kvn", i), ("kr", i)], "lat", n=2)

        def load_cs(s):
            if s in cs_done:
                return
            cs_done.add(s)
            DMA("sync", lambda e: e.dma_start(out=cs_tab[:].rearrange("p a t -> p (a t)"), in_=csd[s]), ["csd"],
                [("cs", w_, t_) for w_ in range(2) for t_ in range(3)], "lat")

        def emit_attn(s):
            w_in = mla_w_in[0]
            w_out = mla_w_out[0]
            NK = 2048 if s == 0 else 4096
            load_latents(s)
            segs = lat_segs(s)
            nseg = len(segs)
            wb = 0
            load_wA(wb, win3(w_in, 0, 384), 384)
            load_wA(1, win3(w_in, 704, 704 + 512), 512)
            emit_norm(C_MNORM, [0, 1, 2])
            for ti in range(3):
                off, n = TILES[ti]
                bq = [nb() for _ in range(3)]
                b4 = nb()
                for c in range(3):
                    def mm(e, c=c, b=bq[c]):
                        for k in range(8):
                            r = e.matmul(ps[b][:, :n], lhsT=wA[wb][:, k, c * 128:(c + 1) * 128], rhs=xn[:, k, off:off + n], start=(k == 0), stop=(k == 7))
                        return r
                    PE(mm, [("wA", wb)] + xkeys(ti), [("ps", bq[c])])
                for c in range(3):
                    si = rot("sq", 3)
                    AC(lambda e, c=c, si=si: e.activation(out=sq[si][:, :n], in_=ps[bq[c]][:, :n], func=AF.Square), [("ps", bq[c])], [("sq", si)])
                    PE(lambda e, c=c, si=si: e.matmul(ps[b4][:, :n], lhsT=ones[:], rhs=sq[si][:, :n], start=(c == 0), stop=(c == 2)),
                       [("sq", si), "ones"], [("ps", b4)])
                ri = 0
                AC(lambda e, ri=ri: e.activation(out=rstd[ri][:, :n], in_=ps[b4][:, :n], func=AF.Ln, bias=vcol(C_EPS), scale=1.0 / 384),
                   [("ps", b4), "vecs"], [("rstd", ri)])
                AC(lambda e, ri=ri: e.activation(out=rstd[ri][:, :n], in_=rstd[ri][:, :n], func=AF.Exp, scale=-0.5), [("rstd", ri)], [("rstd", ri)])
                for c in range(3):
                    VE(lambda e, c=c, ri=ri: e.scalar_tensor_tensor(out=qn[:, c, off:off + n], in0=ps[bq[c]][:, :n], scalar=vcol(C_QNORM + c),
                                                                    in1=rstd[ri][:, :n], op0=ALU.mult, op1=ALU.mult),
                       [("ps", bq[c]), ("rstd", ri), "vecs"], [("qn", c, ti)])
            qnk = lambda ti: [("qn", c, ti) for c in range(3)]

            B_S = (0, 1, 2, 3)
            B_O = (4, 5)
            B_L = (6,)
            B_P = (6, 7)

            def pbank():
                return B_P[rot("pbank", 2)]

            def prep_steps(h, hb, wzb, hh):
                steps = []
                wq = hb

                def ldw():
                    srcq = mla_w_q_up[0].rearrange("(c p) n -> p c n", p=128)
                    srck = mla_w_kv_up[0].rearrange("(c p) n -> p c n", p=128)

                    def ld(e):
                        return [e.dma_start(out=wQ[wq][:, :, 0:192], in_=srcq[:, :, h * 192:(h + 1) * 192]),
                                e.dma_start(out=wQ[wq][:, :, 192:224], in_=srcq[:, :, h * 192 + 160:h * 192 + 192]),
                                e.dma_start(out=wQ[wq][:, :, 224:256], in_=srcq[:, :, h * 192 + 128:h * 192 + 160]),
                                e.dma_start(out=wKV[wq][:, :, :], in_=srck[:, :, h * 256:(h + 1) * 256])]
                    DMA("gpsimd", ld, [], [("wQ", wq), ("wKV", wq)], "wG%d" % wq, n=4)
                steps.append(ldw)
                for kt in range(NK // 512):
                    def kstep(kt=kt):
                        b = pbank()
                        seg = kt // 2

                        def mm(e):
                            for c in range(2):
                                r = e.matmul(ps[b][:, :], lhsT=wKV[wq][:, c, 0:128], rhs=kvn[:, c, kt * 512:(kt + 1) * 512], start=(c == 0), stop=(c == 1))
                            return r
                        PE(mm, [("wKV", wq), ("kvn", seg)], [("ps", b)])
                        return lambda: VE(lambda e: e.tensor_copy(out=Kb[hb][:, kt * 512:(kt + 1) * 512], in_=ps[b][:, :]), [("ps", b)], [("K", hb, kt)])
                    steps.append(kstep)

                    def vstep(kt=kt):
                        b = pbank()
                        seg = kt // 2

                        def mm(e):
                            for jb in range(4):
                                for c in range(2):
                                    r = e.matmul(ps[b][:, jb * 128:(jb + 1) * 128], lhsT=kvn[:, c, kt * 512 + jb * 128:kt * 512 + (jb + 1) * 128],
                                                 rhs=wKV[wq][:, c, 128:256], start=(c == 0), stop=(c == 1))
                            return r
                        PE(mm, [("wKV", wq), ("kvn", seg)], [("ps", b)])
                        return lambda: VE(lambda e: e.tensor_copy(out=Vb[hb][:, kt * 4:(kt + 1) * 4, :], in_=ps[b][:, :].rearrange("p (j d) -> p j d", d=128)),
                                          [("ps", b)], [("V", hb, kt)])
                    steps.append(vstep)
                for ti in range(3):
                    off, n = TILES[ti]

                    def qstep(ti=ti, off=off, n=n):
                        b = pbank()

                        def mm(e):
                            for c in range(3):
                                r = e.matmul(ps[b][:, :n], lhsT=wQ[wq][:, c, 0:128], rhs=qn[:, c, off:off + n], start=(c == 0), stop=(c == 2))
                            return r
                        PE(mm, [("wQ", wq)] + qnk(ti), [("ps", b)])
                        return lambda: VE(lambda e: e.tensor_copy(out=qnope[hb][:, off:off + n], in_=ps[b][:, :n]), [("ps", b)], [("qnope", hb, ti)])
                    steps.append(qstep)

                    def rstep(ti=ti, off=off, n=n):
                        b1 = pbank()
                        b2 = pbank()

                        def mm1(e):
                            for c in range(3):
                                r = e.matmul(ps[b1][0:64, :n], lhsT=wQ[wq][:, c, 128:192], rhs=qn[:, c, off:off + n], start=(c == 0), stop=(c == 2))
                            return r

                        def mm2(e):
                            for c in range(3):
                                r = e.matmul(ps[b2][0:64, :n], lhsT=wQ[wq][:, c, 192:256], rhs=qn[:, c, off:off + n], start=(c == 0), stop=(c == 2))
                            return r
                        PE(mm1, [("wQ", wq)] + qnk(ti), [("ps", b1)])
                        PE(mm2, [("wQ", wq)] + qnk(ti), [("ps", b2)])
                        return lambda: emit_rope(qrope[hb][:, off:off + n], ps[b1], ps[b2], ti, [("ps", b1), ("ps", b2)], [("qrope", hb, ti)])
                    steps.append(rstep)

                    def zstep(ti=ti, off=off, n=n):
                        b = pbank()

                        def mm(e):
                            for k in range(8):
                                r = e.matmul(ps[b][:, :n], lhsT=wA[wzb][:, k, hh * 128:(hh + 1) * 128], rhs=xn[:, k, off:off + n], start=(k == 0), stop=(k == 7))
                            return r
                        PE(mm, [("wA", wzb)] + xkeys(ti), [("ps", b)])
                        def zfin():
                            tq = rot("tmpf", 4)
                            AC(lambda e: e.activation(out=tmpf[tq][:, :n], in_=ps[b][:, :n], func=AF.Tanh, scale=0.5), [("ps", b)], [("tmpf", tq)])
                            VE(lambda e: e.scalar_tensor_tensor(out=szb[hb][:, off:off + n], in0=tmpf[tq][:, :n], scalar=1.0, in1=ps[b][:, :n],
                                                                op0=ALU.add, op1=ALU.mult), [("tmpf", tq), ("ps", b)], [("sz", hb, ti)])
                        return zfin
                    steps.append(zstep)
                return steps

            own = nseg - 1

            def tile_sched(ti):
                off, n = TILES[ti]
                a0 = off - HALO
                sched = []
                for sg in range(nseg - 1):
                    for jb in range(8):
                        bias = None
                        if s == 0 and sg == 0:
                            bias = C_FLAGA
                        if s == 1 and sg == 2:
                            bias = C_FLAGB
                        mk = []
                        if ti == 0 and jb == 7:
                            if s == 0 and sg == 0:
                                mk.append((0, HALO, tri_h))
                            if s == 1 and sg == 1:
                                mk.append((0, HALO, mBH1))
                            if s == 1 and sg == 2:
                                mk.append((0, HALO, tri_h))
                        sched.append((sg * 8 + jb, 0, n, bias, mk))
                for kb in range(8):
                    delta = 128 * kb - a0
                    if delta >= n:
                        continue
                    if delta >= 0:
                        w = min(128, n - delta)
                        sched.append((own * 8 + kb, delta, n, None, [(delta, w, tri[:, 0:w])]))
                    else:
                        wv = 128 + delta
                        if wv > 0:
                            sched.append((own * 8 + kb, 0, n, None, [(0, wv, tri[:, -delta:128])]))
                        else:
                            sched.append((own * 8 + kb, 0, n, None, []))
                return sched

            items = []
            head_range = {}
            tcount = 0
            for h in range(16):
                st0 = len(items)
                for ti in range(3):
                    off, n = TILES[ti]
                    ctx = dict(h=h, hb=h % 2, hh=h % 4, ypar=(h // 4) % 2, ti=ti, off=off, n=n, sched=tile_sched(ti),
                               bo=B_O[tcount % 2], bl=B_L[0], bs={}, ap=0)
                    tcount += 1
                    for i in range(len(ctx["sched"])):
                        items.append((ctx, i))
                head_range[h] = (st0, len(items))

            def qk(ctx, i):
                kb, c0, c1, bias, mk = ctx["sched"][i]
                hb, off, ti = ctx["hb"], ctx["off"], ctx["ti"]
                bs = B_S[rot("bs", 4)]
                ctx["bs"][i] = bs
                seg = kb // 8

                def mm(e):
                    e.matmul(ps[bs][:, c0:c1], lhsT=Kb[hb][:, kb * 128:(kb + 1) * 128], rhs=qnope[hb][:, off + c0:off + c1], start=True, stop=False)
                    r = e.matmul(ps[bs][:, c0:c1], lhsT=kr[:, kb * 128:(kb + 1) * 128], rhs=qrope[hb][:, off + c0:off + c1], start=False, stop=(len(mk) == 0))
                    for mi, (m0, mw, map_) in enumerate(mk):
                        r = e.matmul(ps[bs][:, m0:m0 + mw], lhsT=ident, rhs=map_, start=False, stop=(mi == len(mk) - 1))
                    return r
                PE(mm, [("K", hb, kb // 4), ("kr", seg), ("qnope", hb, ti), ("qrope", hb, ti), "masks"], [("ps", bs)])

            def pv(ctx, i):
                kb, c0, c1, bias, mk = ctx["sched"][i]
                hb, bo, bl = ctx["hb"], ctx["bo"], ctx["bl"]
                nblk = len(ctx["sched"])
                bs = ctx["bs"][i]
                pi = rot("P", 5)
                bias_ap = vcol(bias) if bias is not None else vcol(C_ZERO)
                AC(lambda e: e.activation(out=Pt[pi][:, c0:c1], in_=ps[bs][:, c0:c1], func=AF.Exp, bias=bias_ap, scale=ATT_SCALE),
                   [("ps", bs), "vecs"], [("P", pi)])

                if i % 2 == 0:
                    EN, acc, akey = VE, accs[ctx["ap"]], ("acc", ctx["ap"])
                else:
                    EN, acc, akey = PO, accq[ctx["ap"]], ("accq", ctx["ap"])
                if i < 2:
                    EN(lambda e: e.tensor_copy(out=acc[:, c0:c1], in_=Pt[pi][:, c0:c1]), [("P", pi)], [akey])
                else:
                    EN(lambda e: e.tensor_tensor(out=acc[:, c0:c1], in0=acc[:, c0:c1], in1=Pt[pi][:, c0:c1], op=ALU.add), [("P", pi), akey], [akey])
                PE(lambda e: e.matmul(ps[bo][:, c0:c1], lhsT=Vb[hb][:, kb, :], rhs=Pt[pi][:, c0:c1], start=(i == 0), stop=(i == nblk - 1)),
                   [("V", hb, kb // 4), ("P", pi)], [("ps", bo)])

            def fin(ctx):
                hb, bo, bl, off, n, ti = ctx["hb"], ctx["bo"], ctx["bl"], ctx["off"], ctx["n"], ctx["ti"]
                ypar, hh = ctx["ypar"], ctx["hh"]
                t1 = rot("tmpf", 4)
                t2 = rot("tmpf", 4)
                acc = accs[ctx["ap"]]
                acq = accq[ctx["ap"]]
                bl = pbank()
                PO(lambda e: e.tensor_tensor(out=accb[:, :n], in0=acc[:, :n], in1=acq[:, :n], op=ALU.add),
                   [("acc", ctx["ap"]), ("accq", ctx["ap"])], [("accb",)])
                PE(lambda e: e.matmul(ps[bl][:, :n], lhsT=ones[:], rhs=accb[:, :n], start=True, stop=True), [("accb",), "ones"], [("ps", bl)])
                VE(lambda e: e.tensor_scalar(out=tmpf[t1][:, :n], in0=ps[bl][:, :n], scalar1=2.0, scalar2=1e-30, op0=ALU.mult, op1=ALU.add),
                   [("ps", bl)], [("tmpf", t1)])
                VE(lambda e: e.reciprocal(out=tmpf[t1][:, :n], in_=tmpf[t1][:, :n]), [("tmpf", t1)], [("tmpf", t1)])
                VE(lambda e: e.tensor_tensor(out=tmpf[t2][:, :n], in0=ps[bo][:, :n], in1=tmpf[t1][:, :n], op=ALU.mult),
                   [("ps", bo), ("tmpf", t1)], [("tmpf", t2)])
                PO(lambda e: e.tensor_tensor(out=Y[:, ypar, hh, off:off + n], in0=tmpf[t2][:, :n], in1=szb[hb][:, off:off + n], op=ALU.mult),
                   [("tmpf", t2), ("sz", hb, ti)], [("y", ypar, hh, ti)])

            wz_of = {0: 1}

            def wz_for(hg):
                if hg not in wz_of:
                    bfi = (1 + hg) % 2
                    load_wA(bfi, win3(w_in, 704 + hg * 512, 704 + (hg + 1) * 512), 512)
                    wz_of[hg] = bfi
                return wz_of[hg]

            load_cs(s)
            for st_ in prep_steps(0, 0, wz_for(0), 0):
                r_ = st_()
                if callable(r_):
                    r_()
            load_wO(0, w_out, 0)
            prep_steps(1, 1, wz_for(0), 1)[0]()
            side_at = {}
            for h in range(16):
                side = []
                if h % 4 == 0 and h > 0:
                    side += emit_wout(0, ((h - 1) // 4) % 2, [0, 1, 2], pbank, defer=True)
                    side.append(lambda h=h: load_wO(0, w_out, (h // 4) * 512))
                if h % 4 == 1 and h // 4 + 1 < 4:
                    side.append(lambda h=h: wz_for(h // 4 + 1))
                if h + 1 < 16:
                    side.append(("prep", h + 1))
                if h + 2 < 16:
                    side.append(("ldw", h + 2))
                a_, b_ = head_range[h]
                span = max(1, int((b_ - a_) * 0.7))
                side_at[h] = (a_, span, side)

            def run_side(h):
                a_, span, side = side_at[h]
                flat = []
                for x in side:
                    if isinstance(x, tuple):
                        hn = x[1]
                        flat += prep_steps(hn, hn % 2, wz_of[hn // 4], hn % 4)
                    else:
                        flat.append(x)
                return flat

            qk(*items[0])
            qk(*items[1])
            qk(*items[2])
            cur_h = -1
            flat = []
            fi = 0
            pend_fin = []

            def run_side_step(f):
                drain_fin()
                r_ = f()
                if callable(r_):
                    pend_fin.append(r_)

            def drain_fin():
                while pend_fin:
                    pend_fin.pop(0)()
            for idx, (ctx, i) in enumerate(items):
                h = ctx["h"]
                if h != cur_h:
                    while fi < len(flat):
                        drain_fin()
                        run_side_step(flat[fi])
                        fi += 1
                    cur_h = h
                    a_, span, side = side_at[h]
                    pre = [x for x in side if not isinstance(x, tuple)]
                    flat = []
                    for x in side:
                        if isinstance(x, tuple) and x[0] == "prep":
                            hn = x[1]
                            wz_for(hn // 4)
                            flat += prep_steps(hn, hn % 2, wz_of[hn // 4], hn % 4)[1:]
                        elif isinstance(x, tuple) and x[0] == "ldw":
                            hn = x[1]
                            flat.append(prep_steps(hn, hn % 2, 0, hn % 4)[0])
                        else:
                            flat.append(x)
                    fi = 0
                if idx + 3 < len(items):
                    qk(*items[idx + 3])
                pv(ctx, i)
                if i == len(ctx["sched"]) - 1:
                    drain_fin()
                    fin(ctx)
                drain_fin()
                a_, span, _sd = side_at[h]
                tgt = ((idx - a_ + 1) * len(flat)) // span
                while fi < min(tgt, len(flat)):
                    run_side_step(flat[fi])
                    fi += 1
            drain_fin()
            while fi < len(flat):
                run_side_step(flat[fi])
                drain_fin()
                fi += 1
            for f in emit_wout(0, 1, [0, 1, 2], pbank):
                f()

        def emit_final(s):
            for ti in (0, 1, 2):
                off, n = TILES[ti]
                b = nb()
                for k in range(8):
                    si = rot("sq", 3)
                    AC(lambda e, k=k, si=si: e.activation(out=sq[si][:, :n], in_=resid[:, k, off:off + n], func=AF.Square),
                       [("resid", k, ti)], [("sq", si)])
                    PE(lambda e, k=k, si=si: e.matmul(ps[b][:, :n], lhsT=ones[:], rhs=sq[si][:, :n], start=(k == 0), stop=(k == 7)),
                       [("sq", si), "ones"], [("ps", b)])
                ri = 0
                AC(lambda e, ri=ri: e.activation(out=rstd[ri][:, :n], in_=ps[b][:, :n], func=AF.Ln, bias=vcol(C_EPS), scale=1.0 / 1024),
                   [("ps", b), "vecs"], [("rstd", ri)])
                AC(lambda e, ri=ri: e.activation(out=rstd[ri][:, :n], in_=rstd[ri][:, :n], func=AF.Exp, scale=-0.5), [("rstd", ri)], [("rstd", ri)])
                for k in range(8):
                    VE(lambda e, k=k, ri=ri: e.scalar_tensor_tensor(out=resid[:, k, off:off + n], in0=resid[:, k, off:off + n],
                                                                    scalar=vcol(C_FNORM + k), in1=rstd[ri][:, :n], op0=ALU.mult, op1=ALU.mult),
                       [("resid", k, ti), ("rstd", ri), "vecs"], [("resid", k, ti)])
                lo = max(off, HALO)
                t0 = lo - HALO
                DMA("sync", lambda e, t0=t0, lo=lo, off=off, n=n: e.dma_start(out=out_d[s, :, :, t0:t0 + (off + n - lo)], in_=resid[:, :, lo:off + n]),
                    rkeys(ti), ["out"], "out")

        def load_resid(s, src, rkey=None):
            for ti in range(3):
                off, n = TILES[ti]
                DMA("gpsimd", lambda e, off=off, n=n: e.dma_start(out=resid[:, :, off:off + n], in_=src[s, :, :, off:off + n]),
                    [rkey] if rkey else [], [("resid", m, ti) for m in range(8)], "res")

        def store_resid(s, dst, slot):
            for ti in range(3):
                off, n = TILES[ti]
                DMA("sync", lambda e, off=off, n=n: e.dma_start(out=dst[s, :, :, off:off + n], in_=resid[:, :, off:off + n]),
                    [("resid", m, ti) for m in range(8)], [slot], "xst")

        if "p1" in parts:
            for s in range(2):
                load_resid(s, x_in)
                emit_pool(s, 0, [0, 1, 2], True)
                rq = emit_rope_tab(s, 0, 528) + emit_rope_tab(s, 528, 528)
                rq.append(lambda s=s: DMA("sync", lambda e: e.dma_start(out=csd[s], in_=cs_tab[:].rearrange("p a t -> p (a t)")),
                                          [("cs", w_, t_) for w_ in range(2) for t_ in range(3)], ["csd"], "xst"))

                def rope_hook(j, rq=rq, s=s):
                    if s == 1 and j == 8 and "cc" in parts and "p2" in parts:
                        load_latents(0)
                    if j < 1:
                        return
                    k_ = (len(rq) + (14 - j)) // max(1, 15 - j) if j < 15 else len(rq)
                    for _ in range(min(k_, len(rq))):
                        rq.pop(0)()
                emit_conv(s, hooks=rope_hook)
                while rq:
                    rq.pop(0)()
                S.alias(PS_A, PS_B)
                store_resid(s, xpark, "xpark")
                emit_lat(s)
                if debug_out:
                    store_resid(s, dbg, "dbg")
                    DMA("sync", lambda e, s=s: e.dma_start(out=dbg_lat[:, s * 1024:(s + 1) * 1024], in_=gin2[s]), [("gin", s)], ["dbg_lat"], "xst")
                if "cc" in parts:
                    def cc(e, s=s):
                        return e.collective_compute("AllGather", ALU.bypass, replica_groups=[[0, 1], [2, 3], [4, 5], [6, 7]],
                                                    ins=[gin2[s]], outs=[gout2[s]])
                    S.add("gpsimd", cc, reads=[("gin", s)], writes=[("gout", s)], dma_slot="cc%d" % s, inc=1)
        if "p2" in parts:
            for s in range(2):
                load_resid(s, xpark, "xpark")
                S.alias(XK_A, XK_B)
                S.alias(PS_B, PS_A)
                S.alias(WG_A, WG_B)
                emit_attn(s)
                if s == 0:
                    load_latents(1)
                    load_cs(1)
                S.alias(XK_B, XK_A)
                S.alias(WG_B, WG_A)
                emit_pool(s, 1, [0, 1, 2], True)
                emit_final(s)
        fin_reads = ["out"]
        if debug_out:
            fin_reads += ["dbg", "dbg_lat"]
        if "p2" not in parts:
            fin_reads = ["dbg", "dbg_lat"]
        S.add("sync", None, reads=fin_reads)

        def new_sem(name):
            return es.enter_context(nc.semaphore(name))
        S.emit(new_sem)
    return nc, S


def _subshard_ids(rank):
    return (0, 3) if rank == 0 else (1, 2)


def _prep_inputs(inputs):
    x = np.asarray(inputs["x"], dtype=np.float32)
    positions = np.asarray(inputs["positions"], dtype=np.int32)
    B, S_, D = x.shape
    f32 = np.float32

    def chunks(v):
        v = np.asarray(v, dtype=f32)
        return np.ascontiguousarray(v.reshape(-1, 128).T)

    shared = {}
    for k in ("pool_w_in", "pool_w_grp", "pool_w_out", "conv_w_in", "conv_w_out", "mla_w_in", "mla_w_q_up", "mla_w_kv_up", "mla_w_out"):
        shared[k] = np.ascontiguousarray(np.asarray(inputs[k], dtype=f32))
    invf = (10000.0 ** (-np.arange(0, 64, 2, dtype=np.float32) / np.float32(64))).astype(f32)
    kq = np.arange(128)[:, None]
    tri = (np.arange(128)[None, :] >= kq).astype(f32)
    tri_h = (kq <= 96 + np.arange(32)[None, :]).astype(f32)
    in_maps = []
    for core in range(8):
        b, rank = core // 2, core % 2
        gids = _subshard_ids(rank)
        vecs = np.zeros((128, NV), f32)
        vecs[:, 0:8] = chunks(inputs["pool_norm"][0])
        vecs[:, 8:16] = chunks(inputs["pool_norm"][1])
        vecs[:, 16:32] = chunks(inputs["pool_scale"][0])
        vecs[:, 32:48] = chunks(inputs["pool_scale"][1])
        vecs[:, 48:56] = chunks(inputs["conv_norm"][0])
        cw = np.asarray(inputs["conv_w"][0], dtype=f32)
        for j in range(16):
            for i in range(3):
                vecs[:, C_CONVW + j * 3 + i] = cw[i, j * 128:(j + 1) * 128]
        vecs[:, 104:112] = chunks(inputs["mla_norm"][0])
        vecs[:, 112:115] = chunks(inputs["mla_q_norm"][0])
        vecs[:, 115:117] = chunks(inputs["mla_kv_norm"][0])
        vecs[:, 117:125] = chunks(inputs["final_norm"])
        vecs[0:32, C_INVF] = invf
        vecs[32:64, C_INVF] = invf
        vecs[:, C_FLAGA] = NEG if rank == 0 else 0.0
        vecs[:, C_FLAGB] = 0.0 if rank == 0 else NEG
        vecs[0:32, C_SGN] = -1.0
        vecs[32:64, C_SGN] = 1.0
        vecs[:, C_EPS] = EPS
        vecs[:, C_NEGPI] = -np.pi
        vecs[:, C_ZERO] = 0.0
        vecs[:, C_TINY] = 1e-30
        masks = np.zeros((128, 320), f32)
        masks[:, 0:128] = (1.0 - tri) * NEG
        masks[:, 128:160] = (1.0 - tri_h) * NEG
        masks[:, 160:192] = 0.0 if rank == 0 else (1.0 - tri_h) * NEG
        masks[:, 192:320] = np.eye(128, dtype=f32)
        x_in = np.zeros((2, 128, 8, NTOK), f32)
        pos_in = np.zeros((2, NTOK), np.int32)
        rcnt = np.zeros((2, 128, 64), f32)
        for s, g in enumerate(gids):
            t0 = g * OWN - HALO
            lo = max(t0, 0)
            xs = x[b, lo:g * OWN + OWN, :]
            xt = xs.T.reshape(8, 128, -1).transpose(1, 0, 2)
            x_in[s, :, :, lo - t0:] = xt
            pos_in[s, lo - t0:] = positions[b, lo:g * OWN + OWN]
            if lo > t0:
                pos_in[s, :lo - t0] = positions[b, 0]
            for gi in range(4):
                w = 2 << gi
                if g == 0:
                    rcnt[s, :, gi * 16:(gi + 1) * 16] = 1.0 / np.minimum(np.arange(1, 17), w).astype(f32)
                else:
                    rcnt[s, :, gi * 16:(gi + 1) * 16] = 1.0 / w
        m = dict(shared)
        m.update(x_in=x_in, pos_in=pos_in, vecs=vecs, masks=masks, rcnt=rcnt)
        in_maps.append(m)
    return in_maps


_CACHE = {}


def kernel(**inputs):
    in_maps = _prep_inputs(inputs)
    if "nc" not in _CACHE:
        _CACHE["nc"] = build_program()[0]
    nc = _CACHE["nc"]
    res = run_bass_kernel_spmd(nc, in_maps, core_ids=list(range(8)))
    B = 4
    out = np.zeros((B, 4096, 1024), np.float32)
    for core in range(8):
        b, rank = core // 2, core % 2
        o = np.asarray(res.results[core]["out"])
        for s, g in enumerate(_subshard_ids(rank)):
            out[b, g * OWN:(g + 1) * OWN, :] = o[s].transpose(2, 1, 0).reshape(OWN, 1024)
    return out
```

```python
import numpy as np
from contextlib import ExitStack
import concourse.bass as bass
import concourse.mybir as mybir
from concourse.bass_utils import run_bass_kernel_spmd

F32 = mybir.dt.float32
BF16 = mybir.dt.bfloat16
I32 = mybir.dt.int32
ALU = mybir.AluOpType
AF = mybir.ActivationFunctionType

NTOK = 1056
HALO = 32
OWN = 1024
TW = 352
TILES = [(0, TW), (TW, TW), (2 * TW, TW)]
EPS = 1e-6
ATT_SCALE = float(192 ** -0.5)
NEG = -30000.0
NV = 136
C_PNORM = (0, 8)
C_PSCALE = (16, 32)
C_CNORM = 48
C_CONVW = 56
C_MNORM = 104
C_QNORM = 112
C_KVNORM = 115
C_FNORM = 117
C_INVF = 125
C_FLAGA = 126
C_FLAGB = 127
C_SGN = 128
C_EPS = 129
C_NEGPI = 130
C_ZERO = 131
C_TINY = 132
CW1 = 6.28125
CW2 = float(2 * np.pi - 6.28125)
INV2PI = float(1.0 / (2 * np.pi))


def _freeze(fn, depth=0):
    import types
    if not isinstance(fn, types.FunctionType) or fn.__closure__ is None or depth > 4:
        return fn
    cells = []
    for c in fn.__closure__:
        try:
            v = c.cell_contents
        except ValueError:
            cells.append(c)
            continue
        if isinstance(v, types.FunctionType) and v is not fn:
            v = _freeze(v, depth + 1)
        cells.append(types.CellType(v))
    g = types.FunctionType(fn.__code__, fn.__globals__, fn.__name__, fn.__defaults__, tuple(cells))
    g.__kwdefaults__ = fn.__kwdefaults__
    return g


class Sched:
    ENG = ("sync", "gpsimd", "scalar", "vector", "tensor")

    def __init__(self, nc):
        self.nc = nc
        self.ops = {e: [] for e in self.ENG}
        self.last_w = {}
        self.readers = {}
        self.dma_cnt = {}
        self.dma_inc = {}

    def add(self, eng, fn, reads=(), writes=(), dma_slot=None, ndma=1, inc=16):
        idx = len(self.ops[eng])
        fn = _freeze(fn)
        if dma_slot is not None:
            writes = list(writes) + [("slot", dma_slot)]
        deps = set()
        for k in reads:
            d = self.last_w.get(k)
            if d is not None:
                deps.add(d)
        for k in writes:
            d = self.last_w.get(k)
            if d is not None:
                deps.add(d)
            for d in self.readers.get(k, ()):
                deps.add(d)
        if dma_slot is not None:
            n = self.dma_cnt.get(dma_slot, 0) + ndma
            self.dma_cnt[dma_slot] = n
            self.dma_inc[dma_slot] = inc
            me = ("d", dma_slot, n)
        else:
            me = ("e", eng, idx)
        for k in reads:
            lst = self.readers.setdefault(k, [])
            lst[:] = [d for d in lst if not (d[0] == me[0] and d[1] == me[1])]
            lst.append(me)
        for k in writes:
            self.last_w[k] = me
            self.readers[k] = []
        deps.discard(me)
        self.ops[eng].append(dict(fn=fn, deps=deps, me=me, dma=dma_slot, signal=False))
        return me

    def alias(self, old_keys, new_keys):
        deps = set()
        for k in old_keys:
            d = self.last_w.get(k)
            if d is not None:
                deps.add(d)
            for d in self.readers.get(k, ()):
                deps.add(d)
        for k in new_keys:
            lst = self.readers.setdefault(k, [])
            for d in deps:
                if d not in lst:
                    lst.append(d)

    def emit(self, new_sem):
        nc = self.nc
        for eng in self.ENG:
            for idx, op in enumerate(self.ops[eng]):
                best = {}
                for d in op["deps"]:
                    key = (d[0], d[1])
                    if key not in best or best[key][2] < d[2]:
                        best[key] = d
                need = []
                for d in best.values():
                    if d[0] == "e" and d[1] == eng:
                        if eng == "tensor":
                            continue
                        if eng in ("vector", "scalar") and idx - d[2] >= 4:
                            continue
                    need.append(d)
                    if d[0] == "e":
                        self.ops[d[1]][d[2]]["signal"] = True
                op["need"] = need
        cnt = {}
        for eng in self.ENG:
            c = 0
            arr = []
            for op in self.ops[eng]:
                if op["signal"]:
                    c += 1
                arr.append(c)
            cnt[eng] = arr
        esem = {eng: new_sem("e_" + eng) for eng in self.ENG if self.ops[eng]}
        dsem = {slot: new_sem("d_" + str(slot)) for slot in self.dma_cnt}
        self.nsem = len(esem) + len(dsem)
        ops = self.ops

        def make_body(eng):
            def body(e):
                seen = {}
                for op in ops[eng]:
                    for d in op["need"]:
                        if d[0] == "e":
                            sem, val, k = esem[d[1]], cnt[d[1]][d[2]], ("e", d[1])
                        else:
                            sem, val, k = dsem[d[1]], self.dma_inc[d[1]] * d[2], ("d", d[1])
                        if seen.get(k, 0) >= val:
                            continue
                        seen[k] = val
                        e.wait_ge(sem, val)
                    if op["fn"] is None:
                        continue
                    ins = op["fn"](e)
                    if op["dma"] is not None:
                        iv = self.dma_inc[op["dma"]]
                        if isinstance(ins, (list, tuple)):
                            for i_ in ins:
                                i_.then_inc(dsem[op["dma"]], iv)
                        else:
                            ins.then_inc(dsem[op["dma"]], iv)
                    elif op["signal"]:
                        ins.then_inc(esem[eng], 1)
            return body

        with nc.Block() as block:
            for eng in self.ENG:
                if ops[eng]:
                    getattr(block, eng)(make_body(eng))


def build_program(parts=("p1", "cc", "p2"), debug_out=False):
    nc = bass.Bass("TRN2", target_bir_lowering=False)

    def din(name, shape, dt=F32):
        return nc.dram_tensor(name, list(shape), dt, kind="ExternalInput").ap()

    x_in = din("x_in", [2, 128, 8, NTOK])
    pos_in = din("pos_in", [2, NTOK], I32)
    vecs_d = din("vecs", [128, NV])
    masks_d = din("masks", [128, 320])
    rcnt_d = din("rcnt", [2, 128, 64])
    pool_w_in = din("pool_w_in", [2, 1024, 4096])
    pool_w_grp = din("pool_w_grp", [2, 4, 512, 512])
    pool_w_out = din("pool_w_out", [2, 2048, 1024])
    conv_w_in = din("conv_w_in", [1, 1024, 8192])
    conv_w_out = din("conv_w_out", [1, 2048, 1024])
    mla_w_in = din("mla_w_in", [1, 1024, 2752])
    mla_w_q_up = din("mla_w_q_up", [1, 384, 3072])
    mla_w_kv_up = din("mla_w_kv_up", [1, 256, 4096])
    mla_w_out = din("mla_w_out", [1, 2048, 1024])
    out_d = nc.dram_tensor("out", [2, 128, 8, OWN], F32, kind="ExternalOutput").ap()
    xpark = nc.dram_tensor("xpark", [2, 128, 8, NTOK], F32).ap()
    csd = nc.dram_tensor("csd", [2, 64, 2 * NTOK], BF16).ap()
    gin2 = [nc.dram_tensor("gin%d" % i, [320, 1024], BF16).ap() for i in range(2)]
    gout2 = [nc.dram_tensor("gout%d" % i, [640, 1024], BF16).ap() for i in range(2)]
    dbg = None
    if debug_out:
        dbg = nc.dram_tensor("dbg", [2, 128, 8, NTOK], F32, kind="ExternalOutput").ap()
        dbg_lat = nc.dram_tensor("dbg_lat", [320, 2048], BF16, kind="ExternalOutput").ap()

    S = Sched(nc)
    es = ExitStack()
    with es:
        def sb(name, shape, dt):
            return es.enter_context(nc.sbuf_tensor(name, list(shape), dt))

        resid = sb("resid", [128, 8, NTOK], F32)
        xn = sb("xn", [128, 8, NTOK], BF16)
        vecs = sb("vecs_sb", [128, NV], F32)
        masks = sb("masks_sb", [128, 320], BF16)
        ones = sb("ones_sb", [128, 128], BF16)
        rcnt = sb("rcnt_sb", [128, 64], F32)
        wA = [sb("wA%d" % i, [128, 8, 512], BF16) for i in range(2)]
        wGr = [sb("wG%d" % i, [128, 2048], BF16) for i in range(2)]
        wO = [sb("wO%d" % i, [128, 4, 1024], BF16) for i in range(1)]
        sq = [sb("sq%d" % i, [128, TW], BF16) for i in range(3)]
        rstd = [sb("rstd%d" % i, [128, TW], F32) for i in range(1)] * 2
        tmpf = [sb("tmpf%d" % i, [128, TW], F32) for i in range(4)]
        X = sb("X", [128, 8512], F32)
        Y = sb("Y", [128, 2, 4, NTOK], BF16)
        qn = sb("qn", [128, 3, NTOK], BF16)
        kvn = sb("kvn", [128, 2, 4096], BF16)
        kr = sb("kr", [64, 4096], BF16)
        qnope = [sb("qnope%d" % i, [128, NTOK], BF16) for i in range(2)]
        qrope = [sb("qrope%d" % i, [64, NTOK], BF16) for i in range(2)]
        szb = [sb("sz%d" % i, [128, NTOK], BF16) for i in range(2)]
        cs_tab = sb("cs_tab", [64, 2, NTOK], BF16)
        PSr = sb("PSr", [128, 6 * TW], BF16)
        accs = [sb("acc%d" % i, [128, TW], F32) for i in range(1)] * 2
        accq = [sb("accq%d" % i, [128, TW], F32) for i in range(1)] * 2
        posi = sb("posi", [64, TW], I32)
        ps = [es.enter_context(nc.psum_tensor("ps%d" % i, [128, 512], F32)) for i in range(8)]

        Xa = X[:]
        U = [Xa[:, b * 1072:(b + 1) * 1072] for b in range(2)]
        TA = Xa[:, 2144:3216]
        TB = Xa[:, 3216:4288]
        pooled = Xa[:, 4288:8512].bitcast(BF16).rearrange("p (b c t) -> p b c t", b=2, c=4)
        Kb = [Xa[:, b * 2048:(b + 1) * 2048].bitcast(BF16) for b in range(2)]
        Vb = [Xa[:, 4096 + b * 2048:4096 + (b + 1) * 2048].bitcast(BF16).rearrange("p (j d) -> p j d", d=128)
              for b in range(2)]
        wAx = [wA[0], wA[1]] + [Xa[:, 4288 + i * 2048:4288 + (i + 1) * 2048].bitcast(BF16).rearrange("p (k n) -> p k n", k=8)
                                for i in range(2)]
        WX_K = [("wA", 2), ("wA", 3)]
        wG = [w[:].rearrange("p (c n) -> p c n", c=4) for w in wGr]
        wQ = [w[:, 0:768].rearrange("p (c n) -> p c n", c=3) for w in wGr]
        wKV = [w[:, 1024:1536].rearrange("p (c n) -> p c n", c=2) for w in wGr]
        Pt = [PSr[:, i * TW:(i + 1) * TW] for i in range(5)]
        accb = PSr[:, 5 * TW:6 * TW]
        stage = [PSr[:, i * 3 * TW:(i + 1) * 3 * TW].rearrange("p (c n) -> p c n", c=3) for i in range(2)]
        tri = masks[:, 0:128]
        tri_h = masks[:, 128:160]
        mBH1 = masks[:, 160:192]
        ident = masks[:, 192:320]

        XK_A = [("U", 0), ("U", 1), ("TA",), ("TB",)] + [("pooled", p, c) for p in range(2) for c in range(4)]
        XK_B = [(kv_, b_, t_) for kv_ in ("K", "V") for b_ in range(2) for t_ in range(8)]
        PS_A = [("P", i) for i in range(5)] + [("accb",)]
        PS_B = [("stage", i) for i in range(2)]
        WG_A = [("wG", i) for i in range(2)]
        WG_B = [("wQ", i) for i in range(2)] + [("wKV", i) for i in range(2)]

        def vcol(c, rows=128):
            return vecs[0:rows, c:c + 1]

        rr = {}

        def rot(name, n):
            v = rr.get(name, 0)
            rr[name] = (v + 1) % n
            return v

        def nb():
            return rot("bank", 8)

        def VE(fn, r, w):
            S.add("vector", fn, reads=r, writes=w)

        def AC(fn, r, w):
            S.add("scalar", fn, reads=r, writes=w)

        def PO(fn, r, w):
            S.add("gpsimd", fn, reads=r, writes=w)

        def PE(fn, r, w):
            S.add("tensor", fn, reads=r, writes=w)

        def DMA(eng, fn, r, w, slot, n=1):
            S.add(eng, fn, reads=r, writes=w, dma_slot=slot, ndma=n)

        def rkeys(ti):
            return [("resid", m, ti) for m in range(8)]

        def xkeys(ti):
            return [("xn", k, ti) for k in range(8)]

        DMA("sync", lambda e: e.dma_start(out=vecs[:], in_=vecs_d), [], ["vecs"], "c0")
        DMA("gpsimd", lambda e: e.dma_start(out=masks[:], in_=masks_d), [], ["masks"], "c1")
        PO(lambda e: e.memset(ones[:], 1.0), [], ["ones"])
        for (buf_, key_) in ((U[0], ("U", 0)), (U[1], ("U", 1)), (TA, ("TA",)), (TB, ("TB",))):
            PO(lambda e, buf_=buf_: e.memset(buf_[:, 0:16], 0.0), [], [key_])

        def emit_norm(gcol, tis):
            for ti in tis:
                off, n = TILES[ti]
                b = nb()
                for k in range(8):
                    si = rot("sq", 3)
                    if k % 2 == 0:
                        AC(lambda e, k=k, si=si: e.activation(out=sq[si][:, :n], in_=resid[:, k, off:off + n], func=AF.Square),
                           [("resid", k, ti)], [("sq", si)])
                    else:
                        PO(lambda e, k=k, si=si: e.tensor_tensor(out=sq[si][:, :n], in0=resid[:, k, off:off + n],
                                                                 in1=resid[:, k, off:off + n], op=ALU.mult),
                           [("resid", k, ti)], [("sq", si)])
                    PE(lambda e, k=k, si=si: e.matmul(ps[b][:, :n], lhsT=ones[:], rhs=sq[si][:, :n], start=(k == 0), stop=(k == 7)),
                       [("sq", si), "ones"], [("ps", b)])
                ri = 0
                AC(lambda e, ri=ri: e.activation(out=rstd[ri][:, :n], in_=ps[b][:, :n], func=AF.Ln, bias=vcol(C_EPS), scale=1.0 / 1024),
                   [("ps", b), "vecs"], [("rstd", ri)])
                AC(lambda e, ri=ri: e.activation(out=rstd[ri][:, :n], in_=rstd[ri][:, :n], func=AF.Exp, scale=-0.5), [("rstd", ri)], [("rstd", ri)])
                for k in range(8):
                    VE(lambda e, k=k, ri=ri: e.scalar_tensor_tensor(out=xn[:, k, off:off + n], in0=resid[:, k, off:off + n],
                                                                      scalar=vcol(gcol + k), in1=rstd[ri][:, :n],
                                                                      op0=ALU.mult, op1=ALU.mult),
                       [("resid", k, ti), ("rstd", ri), "vecs"], [("xn", k, ti)])

        def load_wA(buf, src3, ncols, col0=0):
            DMA("gpsimd", lambda e: e.dma_start(out=wA[buf][:, :, col0:col0 + ncols], in_=src3), [], [("wA", buf)], "wA%d" % buf)

        def win3(w2d, c0, c1):
            return w2d.rearrange("(k p) n -> p k n", p=128)[:, :, c0:c1]

        pend = []

        def flush_pending():
            while pend:
                for st_ in pend.pop(0):
                    st_()

        def emit_wout(wobuf, ypar, tis, bankfn=None, defer=False):
            bankfn = bankfn or nb
            steps = []
            for m in range(8):
                for ti in tis:
                    def step(m=m, ti=ti):
                        off, n = TILES[ti]
                        b = bankfn()

                        def mm(e):
                            for d in range(4):
                                r = e.matmul(ps[b][:, :n], lhsT=wO[wobuf][:, d, m * 128:(m + 1) * 128],
                                             rhs=Y[:, ypar, d, off:off + n], start=(d == 0), stop=(d == 3))
                            return r
                        PE(mm, [("wO", wobuf)] + [("y", ypar, d, ti) for d in range(4)], [("ps", b)])

                        def wfin():
                            VE(lambda e: e.tensor_tensor(out=resid[:, m, off:off + n], in0=resid[:, m, off:off + n],
                                                         in1=ps[b][:, :n], op=ALU.add),
                               [("ps", b), ("resid", m, ti)], [("resid", m, ti)])
                        if defer:
                            return wfin
                        wfin()
                    steps.append(step)
            return steps

        def load_wO(buf, w2d, r0):
            DMA("gpsimd", lambda e: e.dma_start(out=wO[buf][:], in_=w2d[r0:r0 + 512, :].rearrange("(d p) n -> p d n", p=128)),
                [], [("wO", buf)], "wO%d" % buf)

        def emit_pool(s, j, tis, last, after_norm=None):
            gc = C_PNORM[j]
            sc = C_PSCALE[j]
            w_in = pool_w_in[j]
            w_out = pool_w_out[j]
            wu, wz = 0, 1

            def ld_u(g):
                load_wA(wu, win3(w_in, g * 512, (g + 1) * 512), 512)

            def ld_z(g):
                load_wA(wz, win3(w_in, 2048 + g * 512, 2048 + (g + 1) * 512), 512)

            def ld_g(g):
                wg_ = g % 2
                DMA("gpsimd", lambda e: e.dma_start(out=wG[wg_], in_=pool_w_grp[j, g].rearrange("(c p) n -> p c n", p=128)),
                    [], [("wG", wg_)], "wG%d" % wg_)
            ld_u(0)
            ld_z(0)
            ld_g(0)
            emit_norm(gc, [0, 1, 2])
            if after_norm is not None:
                after_norm()
            DMA("sync", lambda e: e.dma_start(out=rcnt[:], in_=rcnt_d[s]), [], ["rcnt"], "c0")
            for g in range(4):
                wg = g % 2
                ppar = g % 2
                win = 2 << g
                pend_steps = []
                while pend:
                    pend_steps += pend.pop(0)
                for c in range(4):
                    q0, q1 = (c * len(pend_steps)) // 4, ((c + 1) * len(pend_steps)) // 4
                    if c > 0:
                        for st_ in pend_steps[(c - 1) * len(pend_steps) // 4:c * len(pend_steps) // 4]:
                            st_()
                    ub = rot("U", 2)
                    for ti in range(3):
                        off, n = TILES[ti]
                        b = nb()

                        def mm(e, c=c, off=off, n=n, b=b, wu=wu):
                            for k in range(8):
                                r = e.matmul(ps[b][:, :n], lhsT=wA[wu][:, k, c * 128:(c + 1) * 128], rhs=xn[:, k, off:off + n],
                                             start=(k == 0), stop=(k == 7))
                            return r
                        PE(mm, [("wA", wu)] + xkeys(ti), [("ps", b)])
                        AC(lambda e, off=off, n=n, b=b, ub=ub: e.activation(out=U[ub][:, 16 + off:16 + off + n], in_=ps[b][:, :n], func=AF.Copy),
                           [("ps", b)], [("U", ub)])
                    src = U[ub]
                    srck = ("U", ub)
                    sh = 1
                    bufs = [(TA, ("TA",)), (TB, ("TB",))]
                    bi = 0
                    while sh < win:
                        dst, dstk = bufs[bi]
                        (PO if (c % 2 == 1 or (g >= 2 and c > 0)) else VE)(lambda e, src=src, dst=dst, sh=sh: e.tensor_tensor(out=dst[:, 16:16 + NTOK], in0=src[:, 16:16 + NTOK],
                                                                              in1=src[:, 16 - sh:16 - sh + NTOK], op=ALU.add),
                           [srck], [dstk])
                        src, srck = dst, dstk
                        sh *= 2
                        bi ^= 1
                    VE(lambda e, src=src, c=c, ub=ub: e.scalar_tensor_tensor(out=pooled[:, ppar, c, :], in0=src[:, 16:16 + NTOK], scalar=1.0 / win,
                                                                            in1=U[ub][:, 16:16 + NTOK], op0=ALU.mult, op1=ALU.subtract),
                       [srck, ("U", ub)], [("pooled", ppar, c)])
                    t16 = rot("tmpf", 4)
                    VE(lambda e, src=src, t16=t16: e.tensor_tensor(out=tmpf[t16][:, 0:16], in0=src[:, 16 + HALO:16 + HALO + 16],
                                                                   in1=rcnt[:, g * 16:(g + 1) * 16], op=ALU.mult),
                       [srck, "rcnt"], [("tmpf", t16)])
                    VE(lambda e, c=c, ub=ub, t16=t16: e.tensor_tensor(out=pooled[:, ppar, c, HALO:HALO + 16], in0=tmpf[t16][:, 0:16],
                                                                      in1=U[ub][:, 16 + HALO:16 + HALO + 16], op=ALU.subtract),
                       [("tmpf", t16), ("U", ub)], [("pooled", ppar, c)])
                if g + 1 < 4:
                    ld_u(g + 1)
                    ld_g(g + 1)
                for st_ in pend_steps[3 * len(pend_steps) // 4:]:
                    st_()
                flush_pending()
                ypar = rot("ypar", 2)
                load_wO(0, w_out, g * 512)
                for d in range(4):
                    for ti in tis:
                        off, n = TILES[ti]
                        bz = nb()

                        def mmz(e, d=d, off=off, n=n, bz=bz, wz=wz):
                            for k in range(8):
                                r = e.matmul(ps[bz][:, :n], lhsT=wA[wz][:, k, d * 128:(d + 1) * 128], rhs=xn[:, k, off:off + n],
                                             start=(k == 0), stop=(k == 7))
                            return r
                        PE(mmz, [("wA", wz)] + xkeys(ti), [("ps", bz)])
                        AC(lambda e, d=d, off=off, n=n, bz=bz, ypar=ypar: e.activation(out=Y[:, ypar, d, off:off + n], in_=ps[bz][:, :n], func=AF.Silu),
                           [("ps", bz)], [("y", ypar, d, ti)])
                if g + 1 < 4:
                    ld_z(g + 1)
                for d in range(4):
                    for ti in tis:
                        off, n = TILES[ti]
                        bm = nb()

                        def mmg(e, d=d, off=off, n=n, bm=bm, wg=wg):
                            for c in range(4):
                                r = e.matmul(ps[bm][:, :n], lhsT=wG[wg][:, c, d * 128:(d + 1) * 128], rhs=pooled[:, ppar, c, off:off + n],
                                             start=(c == 0), stop=(c == 3))
                            return r
                        PE(mmg, [("wG", wg)] + [("pooled", ppar, c) for c in range(4)], [("ps", bm)])
                        VE(lambda e, d=d, off=off, n=n, bm=bm, ypar=ypar: e.scalar_tensor_tensor(
                            out=Y[:, ypar, d, off:off + n], in0=ps[bm][:, :n], scalar=vcol(sc + g * 4 + d), in1=Y[:, ypar, d, off:off + n],
                            op0=ALU.mult, op1=ALU.mult),
                           [("ps", bm), ("y", ypar, d, ti), "vecs"], [("y", ypar, d, ti)])
                pend.append(emit_wout(0, ypar, tis))
            if last:
                flush_pending()

        def emit_conv(s, hooks=None):
            w_in = conv_w_in[0]
            w_out = conv_w_out[0]
            S.alias([("pooled", p_, c_) for p_ in range(2) for c_ in range(4)], WX_K)
            src = w_in.rearrange("(k p) n -> p k n", p=128)

            def ld_chunk(j):
                wb_ = j % 4

                def ld(e):
                    r = []
                    for q in range(4):
                        r.append(e.dma_start(out=wAx[wb_][:, :, q * 128:(q + 1) * 128],
                                             in_=src[:, :, q * 2048 + j * 128:q * 2048 + (j + 1) * 128]))
                    return r
                DMA("gpsimd", ld, [], [("wA", wb_)], "wA%d" % wb_, n=4)
            for j0 in range(3):
                ld_chunk(j0)
            emit_norm(C_CNORM, [0, 1, 2])
            for jh in range(4):
                ypar = rot("ypar", 2)
                for jj in range(4):
                    j = jh * 4 + jj
                    wb = j % 4
                    if j + 3 < 16:
                        ld_chunk(j + 3)
                    cb = rot("U", 2)
                    CH = U[cb]
                    for ti in range(3):
                        off, n = TILES[ti]
                        bq = []
                        for q in range(4):
                            b = nb()
                            bq.append(b)

                            def mm(e, q=q, off=off, n=n, b=b, wb=wb):
                                for k in range(8):
                                    r = e.matmul(ps[b][:, :n], lhsT=wAx[wb][:, k, q * 128:(q + 1) * 128], rhs=xn[:, k, off:off + n],
                                                 start=(k == 0), stop=(k == 7))
                                return r
                            PE(mm, [("wA", wb)] + xkeys(ti), [("ps", b)])
                        bb, bc, bh, bz = bq
                        tc_ = rot("tmpf", 4)
                        AC(lambda e, n=n, bc=bc, tc_=tc_: e.activation(out=tmpf[tc_][:, :n], in_=ps[bc][:, :n], func=AF.Copy),
                           [("ps", bc)], [("tmpf", tc_)])
                        VE(lambda e, off=off, n=n, bh=bh, tc_=tc_, CH=CH: e.tensor_tensor(out=CH[:, 16 + off:16 + off + n], in0=tmpf[tc_][:, :n],
                                                                                      in1=ps[bh][:, :n], op=ALU.mult),
                           [("ps", bh), ("tmpf", tc_)], [("U", cb)])
                        tz = rot("tmpf", 4)
                        AC(lambda e, n=n, bz=bz, tz=tz: e.activation(out=tmpf[tz][:, :n], in_=ps[bz][:, :n], func=AF.Silu),
                           [("ps", bz)], [("tmpf", tz)])
                        VE(lambda e, n=n, bb=bb, tz=tz: e.tensor_tensor(out=tmpf[tz][:, :n], in0=tmpf[tz][:, :n], in1=ps[bb][:, :n], op=ALU.mult),
                           [("ps", bb), ("tmpf", tz)], [("tmpf", tz)])
                        ta = rot("tmpf", 4)
                        cw = C_CONVW + j * 3
                        AC(lambda e, off=off, n=n, ta=ta, CH=CH, cw=cw: e.activation(out=tmpf[ta][:, :n], in_=CH[:, 16 + off:16 + off + n],
                                                                                 func=AF.Copy, scale=vcol(cw + 2)),
                           [("U", cb), "vecs"], [("tmpf", ta)])
                        VE(lambda e, off=off, n=n, ta=ta, CH=CH, cw=cw: e.scalar_tensor_tensor(
                            out=tmpf[ta][:, :n], in0=CH[:, 15 + off:15 + off + n], scalar=vcol(cw + 1), in1=tmpf[ta][:, :n],
                            op0=ALU.mult, op1=ALU.add), [("U", cb), ("tmpf", ta), "vecs"], [("tmpf", ta)])
                        VE(lambda e, off=off, n=n, ta=ta, CH=CH, cw=cw: e.scalar_tensor_tensor(
                            out=tmpf[ta][:, :n], in0=CH[:, 14 + off:14 + off + n], scalar=vcol(cw + 0), in1=tmpf[ta][:, :n],
                            op0=ALU.mult, op1=ALU.add), [("U", cb), ("tmpf", ta), "vecs"], [("tmpf", ta)])
                        VE(lambda e, off=off, n=n, ta=ta, tz=tz, jj=jj, ypar=ypar: e.tensor_tensor(
                            out=Y[:, ypar, jj, off:off + n], in0=tmpf[ta][:, :n], in1=tmpf[tz][:, :n], op=ALU.mult),
                           [("tmpf", ta), ("tmpf", tz)], [("y", ypar, jj, ti)])
                    if jj == 0:
                        flush_pending()
                        load_wO(0, w_out, jh * 512)
                    if hooks:
                        hooks(j)
                pend.append(emit_wout(0, ypar, [0, 1, 2]))
            flush_pending()
            S.alias(WX_K, XK_A + XK_B)

        def emit_rope_tab(s, c0, ncol):
            ops = []
            T1, T2, T3 = TA[0:64, 0:ncol], TA[0:64, 528:528 + ncol], TB[0:64, 0:ncol]
            k1 = k2 = ("TA",)
            k3 = ("TB",)
            pieces = [(0, min(TW, ncol))] + ([(TW, ncol - TW)] if ncol > TW else [])
            for (o_, n_) in pieces:
                ops.append(lambda o_=o_, n_=n_: DMA("sync", lambda e: e.dma_start(out=posi[:, :n_], in_=pos_in[s:s + 1, c0 + o_:c0 + o_ + n_].partition_broadcast(64)),
                    [], ["posi"], "c0"))
                ops.append(lambda o_=o_, n_=n_: VE(lambda e: e.tensor_copy(out=T1[:, o_:o_ + n_], in_=posi[:, :n_]), ["posi", k1], [k1]))
            ops.append(lambda: VE(lambda e: e.tensor_scalar(out=T1, in0=T1, scalar1=vcol(C_INVF, 64), scalar2=None, op0=ALU.mult), [k1, "vecs"], [k1]))
            for which in range(2):
                ops.append(lambda which=which: VE(lambda e: e.tensor_scalar(out=T2, in0=T1, scalar1=INV2PI, scalar2=(0.25 if which == 0 else 0.0),
                                                                            op0=ALU.mult, op1=ALU.add), [k1], [k2]))
                for (o_, n_) in pieces:
                    ops.append(lambda o_=o_, n_=n_: VE(lambda e: e.tensor_copy(out=posi[:, :n_], in_=T2[:, o_:o_ + n_]), [k2, "posi"], ["posi"]))
                    ops.append(lambda o_=o_, n_=n_: VE(lambda e: e.tensor_copy(out=T2[:, o_:o_ + n_], in_=posi[:, :n_]), ["posi", k2], [k2]))
                ops.append(lambda: VE(lambda e: e.scalar_tensor_tensor(out=T3, in0=T2, scalar=-CW1, in1=T1, op0=ALU.mult, op1=ALU.add), [k1, k2], [k3]))
                ops.append(lambda: VE(lambda e: e.scalar_tensor_tensor(out=T3, in0=T2, scalar=-CW2, in1=T3, op0=ALU.mult, op1=ALU.add), [k2, k3], [k3]))
                if which == 0:
                    ops.append(lambda: VE(lambda e: e.tensor_scalar(out=T3, in0=T3, scalar1=float(np.pi / 2), scalar2=3.1415925, op0=ALU.add, op1=ALU.min), [k3], [k3]))
                    ops.append(lambda: VE(lambda e: e.tensor_scalar(out=T3, in0=T3, scalar1=-3.1415925, scalar2=None, op0=ALU.max), [k3], [k3]))
                else:
                    ops.append(lambda: VE(lambda e: e.tensor_scalar(out=T3, in0=T3, scalar1=3.1415925, scalar2=-3.1415925, op0=ALU.min, op1=ALU.max), [k3], [k3]))
                ops.append(lambda: AC(lambda e: e.activation(out=T3, in_=T3, func=AF.Sin), [k3], [k3]))
                cskeys = [("cs", which, ti) for ti in range(3)]
                if which == 0:
                    ops.append(lambda cskeys=cskeys: VE(lambda e: e.tensor_copy(out=cs_tab[:, 0, c0:c0 + ncol], in_=T3), [k3] + cskeys, cskeys))
                else:
                    ops.append(lambda cskeys=cskeys: VE(lambda e: e.tensor_scalar(out=cs_tab[:, 1, c0:c0 + ncol], in0=T3, scalar1=vcol(C_SGN, 64), scalar2=None, op0=ALU.mult),
                                                        [k3, "vecs"] + cskeys, cskeys))
            return ops

        def emit_rope(dst, ps_x, ps_sw, ti, r, w):
            off, n = TILES[ti]
            t1 = rot("tmpf", 4)
            t2 = rot("tmpf", 4)
            VE(lambda e: e.tensor_tensor(out=tmpf[t1][0:64, :n], in0=ps_x[0:64, :n], in1=cs_tab[:, 0, off:off + n], op=ALU.mult),
               r + [("cs", 0, ti)], [("tmpf", t1)])
            VE(lambda e: e.tensor_tensor(out=tmpf[t2][0:64, :n], in0=ps_sw[0:64, :n], in1=cs_tab[:, 1, off:off + n], op=ALU.mult),
               r + [("cs", 1, ti)], [("tmpf", t2)])
            PO(lambda e: e.tensor_tensor(out=dst, in0=tmpf[t1][0:64, :n], in1=tmpf[t2][0:64, :n], op=ALU.add),
               [("tmpf", t1), ("tmpf", t2)], w)

        def emit_lat(s):
            w_in = mla_w_in[0]
            wb = 0
            src = w_in.rearrange("(k p) n -> p k n", p=128)

            def ld(e):
                return [e.dma_start(out=wA[wb][:, :, 0:320], in_=src[:, :, 384:704]),
                        e.dma_start(out=wA[wb][:, :, 320:352], in_=src[:, :, 672:704]),
                        e.dma_start(out=wA[wb][:, :, 352:384], in_=src[:, :, 640:672])]
            DMA("gpsimd", ld, [], [("wA", wb)], "wA%d" % wb, n=3)
            emit_norm(C_MNORM, [0, 1, 2])
            for ti in (0, 1, 2):
                off, n = TILES[ti]
                b0, b1, b2, b3, b4 = [nb() for _ in range(5)]

                def mm(e, c0, c1, b):
                    for k in range(8):
                        r = e.matmul(ps[b][0:c1 - c0, :n], lhsT=wA[wb][:, k, c0:c1], rhs=xn[:, k, off:off + n], start=(k == 0), stop=(k == 7))
                    return r
                for (c0, c1, b) in ((0, 128, b0), (128, 256, b1), (256, 320, b2), (320, 384, b3)):
                    PE(lambda e, c0=c0, c1=c1, b=b: mm(e, c0, c1, b), [("wA", wb)] + xkeys(ti), [("ps", b)])
                sis = []
                for c, b in ((0, b0), (1, b1)):
                    si = rot("sq", 3)
                    sis.append(si)
                    AC(lambda e, b=b, si=si: e.activation(out=sq[si][:, :n], in_=ps[b][:, :n], func=AF.Square), [("ps", b)], [("sq", si)])
                    PE(lambda e, c=c, si=si: e.matmul(ps[b4][:, :n], lhsT=ones[:], rhs=sq[si][:, :n], start=(c == 0), stop=(c == 1)),
                       [("sq", si), "ones"], [("ps", b4)])
                ri = 0
                AC(lambda e, ri=ri: e.activation(out=rstd[ri][:, :n], in_=ps[b4][:, :n], func=AF.Ln, bias=vcol(C_EPS), scale=1.0 / 256),
                   [("ps", b4), "vecs"], [("rstd", ri)])
                AC(lambda e, ri=ri: e.activation(out=rstd[ri][:, :n], in_=rstd[ri][:, :n], func=AF.Exp, scale=-0.5), [("rstd", ri)], [("rstd", ri)])
                st = rot("stage", 2)
                for c, b in ((0, b0), (1, b1)):
                    VE(lambda e, c=c, b=b, ri=ri, st=st: e.scalar_tensor_tensor(out=stage[st][:, c, :n], in0=ps[b][:, :n], scalar=vcol(C_KVNORM + c),
                                                                                in1=rstd[ri][:, :n], op0=ALU.mult, op1=ALU.mult),
                       [("ps", b), ("rstd", ri), "vecs"], [("stage", st)])
                emit_rope(stage[st][0:64, 2, :n], ps[b2], ps[b3], ti, [("ps", b2), ("ps", b3)], [("stage", st)])
                lo = max(off, HALO) - off
                t0 = off + lo - HALO
                no = n - lo

                def st_dma(e, st=st, t0=t0, n=n, lo=lo, no=no):
                    return [e.dma_start(out=gin2[s][0:256, t0:t0 + no].rearrange("(c p) n -> p c n", p=128), in_=stage[st][:, 0:2, lo:n]),
                            e.dma_start(out=gin2[s][256:320, t0:t0 + no], in_=stage[st][0:64, 2, lo:n])]
                DMA("sync", st_dma, [("stage", st)], [("gin", s)], "gin", n=2)

        lat_done = set()
        cs_done = set()

        def lat_segs(s):
            if s == 0:
                return [("r0A", gout2[0][0:320, :]), ("own", gin2[0][0:320, :])]
            return [("r0A", gout2[0][0:320, :]), ("r1A", gout2[0][320:640, :]), ("r1B", gout2[1][320:640, :]),
                    ("own", gin2[1][0:320, :])]

        def load_latents(s):
            if s in lat_done:
                return
            lat_done.add(s)
            rd = [("gout", 0), ("gin", 0)] if s == 0 else [("gout", 0), ("gout", 1), ("gin", 0), ("gin", 1)]
            for i, (nm, ap_) in enumerate(lat_segs(s)):
                def ld(e, i=i, ap_=ap_):
                    return [e.dma_start(out=kvn[:, :, i * 1024:(i + 1) * 1024], in_=ap_[0:256, :].rearrange("(c p) n -> p c n", p=128)),
                            e.dma_start(out=kr[:, i * 1024:(i + 1) * 1024], in_=ap_[256:320, :])]
                DMA("sync", ld, rd, [("kvn", i), ("kr", i)], "lat", n=2)

        def load_cs(s):
            if s in cs_done:
                return
            cs_done.add(s)
            DMA("sync", lambda e: e.dma_start(out=cs_tab[:].rearrange("p a t -> p (a t)"), in_=csd[s]), ["csd"],
                [("cs", w_, t_) for w_ in range(2) for t_ in range(3)], "lat")

        def emit_attn(s):
            w_in = mla_w_in[0]
            w_out = mla_w_out[0]
            NK = 2048 if s == 0 else 4096
            load_latents(s)
            segs = lat_segs(s)
            nseg = len(segs)
            wb = 0
            load_wA(wb, win3(w_in, 0, 384), 384)
            load_wA(1, win3(w_in, 704, 704 + 512), 512)
            emit_norm(C_MNORM, [0, 1, 2])
            for ti in range(3):
                off, n = TILES[ti]
                bq = [nb() for _ in range(3)]
                b4 = nb()
                for c in range(3):
                    def mm(e, c=c, b=bq[c]):
                        for k in range(8):
                            r = e.matmul(ps[b][:, :n], lhsT=wA[wb][:, k, c * 128:(c + 1) * 128], rhs=xn[:, k, off:off + n], start=(k == 0), stop=(k == 7))
                        return r
                    PE(mm, [("wA", wb)] + xkeys(ti), [("ps", bq[c])])
                for c in range(3):
                    si = rot("sq", 3)
                    AC(lambda e, c=c, si=si: e.activation(out=sq[si][:, :n], in_=ps[bq[c]][:, :n], func=AF.Square), [("ps", bq[c])], [("sq", si)])
                    PE(lambda e, c=c, si=si: e.matmul(ps[b4][:, :n], lhsT=ones[:], rhs=sq[si][:, :n], start=(c == 0), stop=(c == 2)),
                       [("sq", si), "ones"], [("ps", b4)])
                ri = 0
                AC(lambda e, ri=ri: e.activation(out=rstd[ri][:, :n], in_=ps[b4][:, :n], func=AF.Ln, bias=vcol(C_EPS), scale=1.0 / 384),
                   [("ps", b4), "vecs"], [("rstd", ri)])
                AC(lambda e, ri=ri: e.activation(out=rstd[ri][:, :n], in_=rstd[ri][:, :n], func=AF.Exp, scale=-0.5), [("rstd", ri)], [("rstd", ri)])
                for c in range(3):
                    VE(lambda e, c=c, ri=ri: e.scalar_tensor_tensor(out=qn[:, c, off:off + n], in0=ps[bq[c]][:, :n], scalar=vcol(C_QNORM + c),
                                                                    in1=rstd[ri][:, :n], op0=ALU.mult, op1=ALU.mult),
                       [("ps", bq[c]), ("rstd", ri), "vecs"], [("qn", c, ti)])
            qnk = lambda ti: [("qn", c, ti) for c in range(3)]

            B_S = (0, 1, 2, 3)
            B_O = (4, 5)
            B_L = (6,)
            B_P = (6, 7)

            def pbank():
                return B_P[rot("pbank", 2)]

            def prep_steps(h, hb, wzb, hh):
                steps = []
                wq = hb

                def ldw():
                    srcq = mla_w_q_up[0].rearrange("(c p) n -> p c n", p=128)
                    srck = mla_w_kv_up[0].rearrange("(c p) n -> p c n", p=128)

                    def ld(e):
                        return [e.dma_start(out=wQ[wq][:, :, 0:192], in_=srcq[:, :, h * 192:(h + 1) * 192]),
                                e.dma_start(out=wQ[wq][:, :, 192:224], in_=srcq[:, :, h * 192 + 160:h * 192 + 192]),
                                e.dma_start(out=wQ[wq][:, :, 224:256], in_=srcq[:, :, h * 192 + 128:h * 192 + 160]),
                                e.dma_start(out=wKV[wq][:, :, :], in_=srck[:, :, h * 256:(h + 1) * 256])]
                    DMA("gpsimd", ld, [], [("wQ", wq), ("wKV", wq)], "wG%d" % wq, n=4)
                steps.append(ldw)
                for kt in range(NK // 512):
                    def kstep(kt=kt):
                        b = pbank()
                        seg = kt // 2

                        def mm(e):
                            for c in range(2):
                                r = e.matmul(ps[b][:, :], lhsT=wKV[wq][:, c, 0:128], rhs=kvn[:, c, kt * 512:(kt + 1) * 512], start=(c == 0), stop=(c == 1))
                            return r
                        PE(mm, [("wKV", wq), ("kvn", seg)], [("ps", b)])
                        return lambda: VE(lambda e: e.tensor_copy(out=Kb[hb][:, kt * 512:(kt + 1) * 512], in_=ps[b][:, :]), [("ps", b)], [("K", hb, kt)])
                    steps.append(kstep)

                    def vstep(kt=kt):
                        b = pbank()
                        seg = kt // 2

                        def mm(e):
                            for jb in range(4):
                                for c in range(2):
                                    r = e.matmul(ps[b][:, jb * 128:(jb + 1) * 128], lhsT=kvn[:, c, kt * 512 + jb * 128:kt * 512 + (jb + 1) * 128],
                                                 rhs=wKV[wq][:, c, 128:256], start=(c == 0), stop=(c == 1))
                            return r
                        PE(mm, [("wKV", wq), ("kvn", seg)], [("ps", b)])
                        return lambda: VE(lambda e: e.tensor_copy(out=Vb[hb][:, kt * 4:(kt + 1) * 4, :], in_=ps[b][:, :].rearrange("p (j d) -> p j d", d=128)),
                                          [("ps", b)], [("V", hb, kt)])
                    steps.append(vstep)
                for ti in range(3):
                    off, n = TILES[ti]

                    def qstep(ti=ti, off=off, n=n):
                        b = pbank()

                        def mm(e):
                            for c in range(3):
                                r = e.matmul(ps[b][:, :n], lhsT=wQ[wq][:, c, 0:128], rhs=qn[:, c, off:off + n], start=(c == 0), stop=(c == 2))
                            return r
                        PE(mm, [("wQ", wq)] + qnk(ti), [("ps", b)])
                        return lambda: VE(lambda e: e.tensor_copy(out=qnope[hb][:, off:off + n], in_=ps[b][:, :n]), [("ps", b)], [("qnope", hb, ti)])
                    steps.append(qstep)

                    def rstep(ti=ti, off=off, n=n):
                        b1 = pbank()
                        b2 = pbank()

                        def mm1(e):
                            for c in range(3):
                                r = e.matmul(ps[b1][0:64, :n], lhsT=wQ[wq][:, c, 128:192], rhs=qn[:, c, off:off + n], start=(c == 0), stop=(c == 2))
                            return r

                        def mm2(e):
                            for c in range(3):
                                r = e.matmul(ps[b2][0:64, :n], lhsT=wQ[wq][:, c, 192:256], rhs=qn[:, c, off:off + n], start=(c == 0), stop=(c == 2))
                            return r
                        PE(mm1, [("wQ", wq)] + qnk(ti), [("ps", b1)])
                        PE(mm2, [("wQ", wq)] + qnk(ti), [("ps", b2)])
                        return lambda: emit_rope(qrope[hb][:, off:off + n], ps[b1], ps[b2], ti, [("ps", b1), ("ps", b2)], [("qrope", hb, ti)])
                    steps.append(rstep)

                    def zstep(ti=ti, off=off, n=n):
                        b = pbank()

                        def mm(e):
                            for k in range(8):
                                r = e.matmul(ps[b][:, :n], lhsT=wA[wzb][:, k, hh * 128:(hh + 1) * 128], rhs=xn[:, k, off:off + n], start=(k == 0), stop=(k == 7))
                            return r
                        PE(mm, [("wA", wzb)] + xkeys(ti), [("ps", b)])
                        def zfin():
                            tq = rot("tmpf", 4)
                            AC(lambda e: e.activation(out=tmpf[tq][:, :n], in_=ps[b][:, :n], func=AF.Tanh, scale=0.5), [("ps", b)], [("tmpf", tq)])
                            VE(lambda e: e.scalar_tensor_tensor(out=szb[hb][:, off:off + n], in0=tmpf[tq][:, :n], scalar=1.0, in1=ps[b][:, :n],
                                                                op0=ALU.add, op1=ALU.mult), [("tmpf", tq), ("ps", b)], [("sz", hb, ti)])
                        return zfin
                    steps.append(zstep)
                return steps

            own = nseg - 1

            def tile_sched(ti):
                off, n = TILES[ti]
                a0 = off - HALO
                sched = []
                for sg in range(nseg - 1):
                    for jb in range(8):
                        bias = None
                        if s == 0 and sg == 0:
                            bias = C_FLAGA
                        if s == 1 and sg == 2:
                            bias = C_FLAGB
                        mk = []
                        if ti == 0 and jb == 7:
                            if s == 0 and sg == 0:
                                mk.append((0, HALO, tri_h))
                            if s == 1 and sg == 1:
                                mk.append((0, HALO, mBH1))
                            if s == 1 and sg == 2:
                                mk.append((0, HALO, tri_h))
                        sched.append((sg * 8 + jb, 0, n, bias, mk))
                for kb in range(8):
                    delta = 128 * kb - a0
                    if delta >= n:
                        continue
                    if delta >= 0:
                        w = min(128, n - delta)
                        sched.append((own * 8 + kb, delta, n, None, [(delta, w, tri[:, 0:w])]))
                    else:
                        wv = 128 + delta
                        if wv > 0:
                            sched.append((own * 8 + kb, 0, n, None, [(0, wv, tri[:, -delta:128])]))
                        else:
                            sched.append((own * 8 + kb, 0, n, None, []))
                return sched

            items = []
            head_range = {}
            tcount = 0
            for h in range(16):
                st0 = len(items)
                for ti in range(3):
                    off, n = TILES[ti]
                    ctx = dict(h=h, hb=h % 2, hh=h % 4, ypar=(h // 4) % 2, ti=ti, off=off, n=n, sched=tile_sched(ti),
                               bo=B_O[tcount % 2], bl=B_L[0], bs={}, ap=0)
                    tcount += 1
                    for i in range(len(ctx["sched"])):
                        items.append((ctx, i))
                head_range[h] = (st0, len(items))

            def qk(ctx, i):
                kb, c0, c1, bias, mk = ctx["sched"][i]
                hb, off, ti = ctx["hb"], ctx["off"], ctx["ti"]
                bs = B_S[rot("bs", 4)]
                ctx["bs"][i] = bs
                seg = kb // 8

                def mm(e):
                    e.matmul(ps[bs][:, c0:c1], lhsT=Kb[hb][:, kb * 128:(kb + 1) * 128], rhs=qnope[hb][:, off + c0:off + c1], start=True, stop=False)
                    r = e.matmul(ps[bs][:, c0:c1], lhsT=kr[:, kb * 128:(kb + 1) * 128], rhs=qrope[hb][:, off + c0:off + c1], start=False, stop=(len(mk) == 0))
                    for mi, (m0, mw, map_) in enumerate(mk):
                        r = e.matmul(ps[bs][:, m0:m0 + mw], lhsT=ident, rhs=map_, start=False, stop=(mi == len(mk) - 1))
                    return r
                PE(mm, [("K", hb, kb // 4), ("kr", seg), ("qnope", hb, ti), ("qrope", hb, ti), "masks"], [("ps", bs)])

            def pv(ctx, i):
                kb, c0, c1, bias, mk = ctx["sched"][i]
                hb, bo, bl = ctx["hb"], ctx["bo"], ctx["bl"]
                nblk = len(ctx["sched"])
                bs = ctx["bs"][i]
                pi = rot("P", 5)
                bias_ap = vcol(bias) if bias is not None else vcol(C_ZERO)
                AC(lambda e: e.activation(out=Pt[pi][:, c0:c1], in_=ps[bs][:, c0:c1], func=AF.Exp, bias=bias_ap, scale=ATT_SCALE),
                   [("ps", bs), "vecs"], [("P", pi)])

                if i % 2 == 0:
                    EN, acc, akey = VE, accs[ctx["ap"]], ("acc", ctx["ap"])
                else:
                    EN, acc, akey = PO, accq[ctx["ap"]], ("accq", ctx["ap"])
                if i < 2:
                    EN(lambda e: e.tensor_copy(out=acc[:, c0:c1], in_=Pt[pi][:, c0:c1]), [("P", pi)], [akey])
                else:
                    EN(lambda e: e.tensor_tensor(out=acc[:, c0:c1], in0=acc[:, c0:c1], in1=Pt[pi][:, c0:c1], op=ALU.add), [("P", pi), akey], [akey])
                PE(lambda e: e.matmul(ps[bo][:, c0:c1], lhsT=Vb[hb][:, kb, :], rhs=Pt[pi][:, c0:c1], start=(i == 0), stop=(i == nblk - 1)),
                   [("V", hb, kb // 4), ("P", pi)], [("ps", bo)])

            def fin(ctx):
                hb, bo, bl, off, n, ti = ctx["hb"], ctx["bo"], ctx["bl"], ctx["off"], ctx["n"], ctx["ti"]
                ypar, hh = ctx["ypar"], ctx["hh"]
                t1 = rot("tmpf", 4)
                t2 = rot("tmpf", 4)
                acc = accs[ctx["ap"]]
                acq = accq[ctx["ap"]]
                bl = pbank()
                PO(lambda e: e.tensor_tensor(out=accb[:, :n], in0=acc[:, :n], in1=acq[:, :n], op=ALU.add),
                   [("acc", ctx["ap"]), ("accq", ctx["ap"])], [("accb",)])
                PE(lambda e: e.matmul(ps[bl][:, :n], lhsT=ones[:], rhs=accb[:, :n], start=True, stop=True), [("accb",), "ones"], [("ps", bl)])
                VE(lambda e: e.tensor_scalar(out=tmpf[t1][:, :n], in0=ps[bl][:, :n], scalar1=2.0, scalar2=1e-30, op0=ALU.mult, op1=ALU.add),
                   [("ps", bl)], [("tmpf", t1)])
                VE(lambda e: e.reciprocal(out=tmpf[t1][:, :n], in_=tmpf[t1][:, :n]), [("tmpf", t1)], [("tmpf", t1)])
                VE(lambda e: e.tensor_tensor(out=tmpf[t2][:, :n], in0=ps[bo][:, :n], in1=tmpf[t1][:, :n], op=ALU.mult),
                   [("ps", bo), ("tmpf", t1)], [("tmpf", t2)])
                PO(lambda e: e.tensor_tensor(out=Y[:, ypar, hh, off:off + n], in0=tmpf[t2][:, :n], in1=szb[hb][:, off:off + n], op=ALU.mult),
                   [("tmpf", t2), ("sz", hb, ti)], [("y", ypar, hh, ti)])

            wz_of = {0: 1}

            def wz_for(hg):
                if hg not in wz_of:
                    bfi = (1 + hg) % 2
                    load_wA(bfi, win3(w_in, 704 + hg * 512, 704 + (hg + 1) * 512), 512)
                    wz_of[hg] = bfi
                return wz_of[hg]

            load_cs(s)
            for st_ in prep_steps(0, 0, wz_for(0), 0):
                r_ = st_()
                if callable(r_):
                    r_()
            load_wO(0, w_out, 0)
            prep_steps(1, 1, wz_for(0), 1)[0]()
            side_at = {}
            for h in range(16):
                side = []
                if h % 4 == 0 and h > 0:
                    side += emit_wout(0, ((h - 1) // 4) % 2, [0, 1, 2], pbank, defer=True)
                    side.append(lambda h=h: load_wO(0, w_out, (h // 4) * 512))
                if h % 4 == 1 and h // 4 + 1 < 4:
                    side.append(lambda h=h: wz_for(h // 4 + 1))
                if h + 1 < 16:
                    side.append(("prep", h + 1))
                if h + 2 < 16:
                    side.append(("ldw", h + 2))
                a_, b_ = head_range[h]
                span = max(1, int((b_ - a_) * 0.85))
                side_at[h] = (a_, span, side)

            def run_side(h):
                a_, span, side = side_at[h]
                flat = []
                for x in side:
                    if isinstance(x, tuple):
                        hn = x[1]
                        flat += prep_steps(hn, hn % 2, wz_of[hn // 4], hn % 4)
                    else:
                        flat.append(x)
                return flat

            qk(*items[0])
            qk(*items[1])
            qk(*items[2])
            cur_h = -1
            flat = []
            fi = 0
            pend_fin = []

            def run_side_step(f):
                drain_fin()
                r_ = f()
                if callable(r_):
                    pend_fin.append(r_)

            def drain_fin():
                while pend_fin:
                    pend_fin.pop(0)()
            for idx, (ctx, i) in enumerate(items):
                h = ctx["h"]
                if h != cur_h:
                    while fi < len(flat):
                        drain_fin()
                        run_side_step(flat[fi])
                        fi += 1
                    cur_h = h
                    a_, span, side = side_at[h]
                    pre = [x for x in side if not isinstance(x, tuple)]
                    flat = []
                    for x in side:
                        if isinstance(x, tuple) and x[0] == "prep":
                            hn = x[1]
                            wz_for(hn // 4)
                            flat += prep_steps(hn, hn % 2, wz_of[hn // 4], hn % 4)[1:]
                        elif isinstance(x, tuple) and x[0] == "ldw":
                            hn = x[1]
                            flat.append(prep_steps(hn, hn % 2, 0, hn % 4)[0])
                        else:
                            flat.append(x)
                    fi = 0
                if idx + 3 < len(items):
                    qk(*items[idx + 3])
                pv(ctx, i)
                if i == len(ctx["sched"]) - 1:
                    drain_fin()
                    fin(ctx)
                drain_fin()
                a_, span, _sd = side_at[h]
                tgt = ((idx - a_ + 1) * len(flat)) // span
                while fi < min(tgt, len(flat)):
                    run_side_step(flat[fi])
                    fi += 1
            drain_fin()
            while fi < len(flat):
                run_side_step(flat[fi])
                drain_fin()
                fi += 1
            for f in emit_wout(0, 1, [0, 1, 2], pbank):
                f()

        def emit_final(s):
            for ti in (0, 1, 2):
                off, n = TILES[ti]
                b = nb()
                for k in range(8):
                    si = rot("sq", 3)
                    AC(lambda e, k=k, si=si: e.activation(out=sq[si][:, :n], in_=resid[:, k, off:off + n], func=AF.Square),
                       [("resid", k, ti)], [("sq", si)])
                    PE(lambda e, k=k, si=si: e.matmul(ps[b][:, :n], lhsT=ones[:], rhs=sq[si][:, :n], start=(k == 0), stop=(k == 7)),
                       [("sq", si), "ones"], [("ps", b)])
                ri = 0
                AC(lambda e, ri=ri: e.activation(out=rstd[ri][:, :n], in_=ps[b][:, :n], func=AF.Ln, bias=vcol(C_EPS), scale=1.0 / 1024),
                   [("ps", b), "vecs"], [("rstd", ri)])
                AC(lambda e, ri=ri: e.activation(out=rstd[ri][:, :n], in_=rstd[ri][:, :n], func=AF.Exp, scale=-0.5), [("rstd", ri)], [("rstd", ri)])
                for k in range(8):
                    VE(lambda e, k=k, ri=ri: e.scalar_tensor_tensor(out=resid[:, k, off:off + n], in0=resid[:, k, off:off + n],
                                                                    scalar=vcol(C_FNORM + k), in1=rstd[ri][:, :n], op0=ALU.mult, op1=ALU.mult),
                       [("resid", k, ti), ("rstd", ri), "vecs"], [("resid", k, ti)])
                lo = max(off, HALO)
                t0 = lo - HALO
                DMA("sync", lambda e, t0=t0, lo=lo, off=off, n=n: e.dma_start(out=out_d[s, :, :, t0:t0 + (off + n - lo)], in_=resid[:, :, lo:off + n]),
                    rkeys(ti), ["out"], "out")

        def load_resid(s, src, rkey=None):
            for ti in range(3):
                off, n = TILES[ti]
                DMA("gpsimd", lambda e, off=off, n=n: e.dma_start(out=resid[:, :, off:off + n], in_=src[s, :, :, off:off + n]),
                    [rkey] if rkey else [], [("resid", m, ti) for m in range(8)], "res")

        def store_resid(s, dst, slot):
            for ti in range(3):
                off, n = TILES[ti]
                DMA("sync", lambda e, off=off, n=n: e.dma_start(out=dst[s, :, :, off:off + n], in_=resid[:, :, off:off + n]),
                    [("resid", m, ti) for m in range(8)], [slot], "xst")

        if "p1" in parts:
            for s in range(2):
                load_resid(s, x_in)
                emit_pool(s, 0, [0, 1, 2], True)
                rq = emit_rope_tab(s, 0, 528) + emit_rope_tab(s, 528, 528)
                rq.append(lambda s=s: DMA("sync", lambda e: e.dma_start(out=csd[s], in_=cs_tab[:].rearrange("p a t -> p (a t)")),
                                          [("cs", w_, t_) for w_ in range(2) for t_ in range(3)], ["csd"], "xst"))

                def rope_hook(j, rq=rq, s=s):
                    if s == 1 and j == 8 and "cc" in parts and "p2" in parts:
                        load_latents(0)
                    if j < 1:
                        return
                    k_ = (len(rq) + (14 - j)) // max(1, 15 - j) if j < 15 else len(rq)
                    for _ in range(min(k_, len(rq))):
                        rq.pop(0)()
                emit_conv(s, hooks=rope_hook)
                while rq:
                    rq.pop(0)()
                S.alias(PS_A, PS_B)
                store_resid(s, xpark, "xpark")
                emit_lat(s)
                if debug_out:
                    store_resid(s, dbg, "dbg")
                    DMA("sync", lambda e, s=s: e.dma_start(out=dbg_lat[:, s * 1024:(s + 1) * 1024], in_=gin2[s]), [("gin", s)], ["dbg_lat"], "xst")
                if "cc" in parts:
                    def cc(e, s=s):
                        return e.collective_compute("AllGather", ALU.bypass, replica_groups=[[0, 1], [2, 3], [4, 5], [6, 7]],
                                                    ins=[gin2[s]], outs=[gout2[s]])
                    S.add("gpsimd", cc, reads=[("gin", s)], writes=[("gout", s)], dma_slot="cc%d" % s, inc=1)
        if "p2" in parts:
            for s in range(2):
                load_resid(s, xpark, "xpark")
                S.alias(XK_A, XK_B)
                S.alias(PS_B, PS_A)
                S.alias(WG_A, WG_B)
                emit_attn(s)
                if s == 0:
                    load_latents(1)
                    load_cs(1)
                S.alias(XK_B, XK_A)
                S.alias(WG_B, WG_A)
                emit_pool(s, 1, [0, 1, 2], True)
                emit_final(s)
        fin_reads = ["out"]
        if debug_out:
            fin_reads += ["dbg", "dbg_lat"]
        if "p2" not in parts:
            fin_reads = ["dbg", "dbg_lat"]
        S.add("sync", None, reads=fin_reads)

        def new_sem(name):
            return es.enter_context(nc.semaphore(name))
        S.emit(new_sem)
    return nc, S


def _subshard_ids(rank):
    return (0, 3) if rank == 0 else (1, 2)


def _prep_inputs(inputs):
    x = np.asarray(inputs["x"], dtype=np.float32)
    positions = np.asarray(inputs["positions"], dtype=np.int32)
    B, S_, D = x.shape
    f32 = np.float32

    def chunks(v):
        v = np.asarray(v, dtype=f32)
        return np.ascontiguousarray(v.reshape(-1, 128).T)

    shared = {}
    for k in ("pool_w_in", "pool_w_grp", "pool_w_out", "conv_w_in", "conv_w_out", "mla_w_in", "mla_w_q_up", "mla_w_kv_up", "mla_w_out"):
        shared[k] = np.ascontiguousarray(np.asarray(inputs[k], dtype=f32))
    invf = (10000.0 ** (-np.arange(0, 64, 2, dtype=np.float32) / np.float32(64))).astype(f32)
    kq = np.arange(128)[:, None]
    tri = (np.arange(128)[None, :] >= kq).astype(f32)
    tri_h = (kq <= 96 + np.arange(32)[None, :]).astype(f32)
    in_maps = []
    for core in range(8):
        b, rank = core // 2, core % 2
        gids = _subshard_ids(rank)
        vecs = np.zeros((128, NV), f32)
        vecs[:, 0:8] = chunks(inputs["pool_norm"][0])
        vecs[:, 8:16] = chunks(inputs["pool_norm"][1])
        vecs[:, 16:32] = chunks(inputs["pool_scale"][0])
        vecs[:, 32:48] = chunks(inputs["pool_scale"][1])
        vecs[:, 48:56] = chunks(inputs["conv_norm"][0])
        cw = np.asarray(inputs["conv_w"][0], dtype=f32)
        for j in range(16):
            for i in range(3):
                vecs[:, C_CONVW + j * 3 + i] = cw[i, j * 128:(j + 1) * 128]
        vecs[:, 104:112] = chunks(inputs["mla_norm"][0])
        vecs[:, 112:115] = chunks(inputs["mla_q_norm"][0])
        vecs[:, 115:117] = chunks(inputs["mla_kv_norm"][0])
        vecs[:, 117:125] = chunks(inputs["final_norm"])
        vecs[0:32, C_INVF] = invf
        vecs[32:64, C_INVF] = invf
        vecs[:, C_FLAGA] = NEG if rank == 0 else 0.0
        vecs[:, C_FLAGB] = 0.0 if rank == 0 else NEG
        vecs[0:32, C_SGN] = -1.0
        vecs[32:64, C_SGN] = 1.0
        vecs[:, C_EPS] = EPS
        vecs[:, C_NEGPI] = -np.pi
        vecs[:, C_ZERO] = 0.0
        vecs[:, C_TINY] = 1e-30
        masks = np.zeros((128, 320), f32)
        masks[:, 0:128] = (1.0 - tri) * NEG
        masks[:, 128:160] = (1.0 - tri_h) * NEG
        masks[:, 160:192] = 0.0 if rank == 0 else (1.0 - tri_h) * NEG
        masks[:, 192:320] = np.eye(128, dtype=f32)
        x_in = np.zeros((2, 128, 8, NTOK), f32)
        pos_in = np.zeros((2, NTOK), np.int32)
        rcnt = np.zeros((2, 128, 64), f32)
        for s, g in enumerate(gids):
            t0 = g * OWN - HALO
            lo = max(t0, 0)
            xs = x[b, lo:g * OWN + OWN, :]
            xt = xs.T.reshape(8, 128, -1).transpose(1, 0, 2)
            x_in[s, :, :, lo - t0:] = xt
            pos_in[s, lo - t0:] = positions[b, lo:g * OWN + OWN]
            if lo > t0:
                pos_in[s, :lo - t0] = positions[b, 0]
            for gi in range(4):
                w = 2 << gi
                if g == 0:
                    rcnt[s, :, gi * 16:(gi + 1) * 16] = 1.0 / np.minimum(np.arange(1, 17), w).astype(f32)
                else:
                    rcnt[s, :, gi * 16:(gi + 1) * 16] = 1.0 / w
        m = dict(shared)
        m.update(x_in=x_in, pos_in=pos_in, vecs=vecs, masks=masks, rcnt=rcnt)
        in_maps.append(m)
    return in_maps


_CACHE = {}


def kernel(**inputs):
    in_maps = _prep_inputs(inputs)
    if "nc" not in _CACHE:
        _CACHE["nc"] = build_program()[0]
    nc = _CACHE["nc"]
    res = run_bass_kernel_spmd(nc, in_maps, core_ids=list(range(8)))
    B = 4
    out = np.zeros((B, 4096, 1024), np.float32)
    for core in range(8):
        b, rank = core // 2, core % 2
        o = np.asarray(res.results[core]["out"])
        for s, g in enumerate(_subshard_ids(rank)):
            out[b, g * OWN:(g + 1) * OWN, :] = o[s].transpose(2, 1, 0).reshape(OWN, 1024)
    return out
```
